# Optimizing a Trainium2 kernel written in Bass

```python
import jax, jax.numpy as jnp
from jax import lax
import numpy as np

D_MODEL = 1024
BATCH = 2
SEQ = 8192
DEPTH = 1
DEC_BATCH = 32
DEC_SEQ = 32
PAST_LEN = 2048

CHUNK = 64
N_HEADS_A = 8
HEAD_DIM = 64
WIDTH_A = N_HEADS_A * HEAD_DIM
IDX_HEADS = 4
IDX_DIM = 64
TOPK_MAX = 256
N_GROUPS_B = 4
GROUP_DIM_B = 128
WIDTH_B = N_GROUPS_B * GROUP_DIM_B
GMLP_CHUNK = 128
MIX_WIDTH = WIDTH_A + WIDTH_B
D_FF = 4 * D_MODEL
ROPE_THETA = 10000.0
EPS = 1e-6
Q_BLOCK = 128
SPLIT_SIZES = (WIDTH_A, WIDTH_A, WIDTH_A, IDX_HEADS * IDX_DIM, IDX_DIM, IDX_HEADS, WIDTH_B, WIDTH_B)
IN_WIDTH = 3 * WIDTH_A + IDX_HEADS * IDX_DIM + IDX_DIM + IDX_HEADS + 2 * WIDTH_B
SPLIT_POINTS = (WIDTH_A, 2 * WIDTH_A, 3 * WIDTH_A, 3 * WIDTH_A + IDX_HEADS * IDX_DIM, 3 * WIDTH_A + IDX_HEADS * IDX_DIM + IDX_DIM, 3 * WIDTH_A + IDX_HEADS * IDX_DIM + IDX_DIM + IDX_HEADS, 3 * WIDTH_A + IDX_HEADS * IDX_DIM + IDX_DIM + IDX_HEADS + WIDTH_B)

kernel_name = "hybrid_dsa_gmlp_streaming_step"


def rms_norm(x, g):
    xf = x.astype(jnp.float32)
    y = xf * lax.rsqrt(jnp.mean(xf * xf, axis=-1, keepdims=True) + EPS)
    return (y * g.astype(jnp.float32)).astype(x.dtype)


def layer_norm(x, g, b):
    xf = x.astype(jnp.float32)
    xc = xf - jnp.mean(xf, axis=-1, keepdims=True)
    y = xc * lax.rsqrt(jnp.mean(xc * xc, axis=-1, keepdims=True) + EPS)
    return (y * g.astype(jnp.float32) + b.astype(jnp.float32)).astype(x.dtype)


def rope(x, pos):
    half = x.shape[-1] // 2
    inv_freq = jnp.power(jnp.float32(ROPE_THETA), -jnp.arange(half, dtype=jnp.float32) / half)
    ang = pos.astype(jnp.float32)[:, None] * inv_freq[None, :]
    cos = jnp.cos(ang)[None, :, None, :]
    sin = jnp.sin(ang)[None, :, None, :]
    xf = x.astype(jnp.float32)
    x1, x2 = xf[..., :half], xf[..., half:]
    return jnp.concatenate([x1 * cos - x2 * sin, x2 * cos + x1 * sin], axis=-1).astype(x.dtype)


def chunk_mask(q_pos, k_pos):
    return k_pos[None, :] < ((q_pos // CHUNK + 1) * CHUNK)[:, None]


def dsa_attend(q, q_idx, w_idx, k_all, v_all, kidx_all, mask, topk):
    s = jnp.einsum('bqhd,bsd->bqhs', q_idx.astype(jnp.float32), kidx_all.astype(jnp.float32))
    score = jnp.einsum('bqhs,bqh->bqs', jax.nn.relu(s), w_idx.astype(jnp.float32))
    score = jnp.where(mask[None], score, -jnp.inf)
    top_val, top_idx = lax.top_k(score, topk)
    valid = jnp.isfinite(top_val)
    gather = jax.vmap(lambda a, i: a[i])
    k_sel = gather(k_all, top_idx)
    v_sel = gather(v_all, top_idx)
    logits = jnp.einsum('bqhd,bqkhd->bqhk', q.astype(jnp.float32), k_sel.astype(jnp.float32)) * (HEAD_DIM ** -0.5)
    logits = jnp.where(valid[:, :, None, :], logits, -jnp.inf)
    p = jax.nn.softmax(logits, axis=-1)
    out = jnp.einsum('bqhk,bqkhd->bqhd', p, v_sel.astype(jnp.float32))
    return out.astype(q.dtype)


def dsa_prompt(q, q_idx, w_idx, k, v, kidx, topk):
    B, T = q.shape[0], q.shape[1]
    nblk = T // Q_BLOCK
    k_pos = jnp.arange(T)

    def to_blocks(a):
        return jnp.swapaxes(a.reshape((B, nblk, Q_BLOCK) + a.shape[2:]), 0, 1)

    def one_block(args):
        qb, qib, wib, start = args
        mask = chunk_mask(start + jnp.arange(Q_BLOCK), k_pos)
        return dsa_attend(qb, qib, wib, k, v, kidx, mask, topk)

    out = lax.map(one_block, (to_blocks(q), to_blocks(q_idx), to_blocks(w_idx), jnp.arange(nblk) * Q_BLOCK))
    return jnp.swapaxes(out, 0, 1).reshape(q.shape)


def gmlp_spatial(u, vn, ws, bs):
    B, T = u.shape[0], u.shape[1]
    L = min(T, GMLP_CHUNK)
    nc = T // L
    ws_m = ws * jnp.tril(jnp.ones((GMLP_CHUNK, GMLP_CHUNK), ws.dtype))
    v5 = vn.reshape(B, nc, L, N_GROUPS_B, GROUP_DIM_B)
    mixed = jnp.einsum('gts,bcsgd->bctgd', ws_m[:, :L, :L], v5) + jnp.transpose(bs[:, :L])[None, None, :, :, None]
    return (u.reshape(B, nc, L, N_GROUPS_B, GROUP_DIM_B) * mixed).reshape(B, T, WIDTH_B)


def hybrid_layer(x, c, past_k, past_v, past_kidx, w_ada, b_ada, norm1_g, norm2_g, w_in, q_norm_g, k_norm_g, gmlp_ln_g, gmlp_ln_b, gmlp_ws, gmlp_bs, w_out, w_ff1, w_ff2):
    B, T, _ = x.shape
    past = 0 if past_k is None else past_k.shape[1]
    pos = past + jnp.arange(T)
    mod = (jax.nn.silu(c) @ w_ada + b_ada)[:, None, :]
    sh1, sc1, g1, sh2, sc2, g2 = jnp.split(mod, 6, axis=-1)

    h = rms_norm(x, norm1_g) * (1 + sc1) + sh1
    z = h @ w_in
    q, k, v, qi, ki, wi, u, vg = jnp.split(z, SPLIT_POINTS, axis=-1)
    q = rope(rms_norm(q.reshape(B, T, N_HEADS_A, HEAD_DIM), q_norm_g), pos)
    k = rope(rms_norm(k.reshape(B, T, N_HEADS_A, HEAD_DIM), k_norm_g), pos)
    v = v.reshape(B, T, N_HEADS_A, HEAD_DIM)
    qi = rope(qi.reshape(B, T, IDX_HEADS, IDX_DIM), pos)
    ki = rope(ki[:, :, None, :], pos)[:, :, 0, :]
    wi = wi * ((IDX_DIM * IDX_HEADS) ** -0.5)
    if past_k is None:
        attn = dsa_prompt(q, qi, wi, k, v, ki, min(TOPK_MAX, T // 4))
    else:
        k_all = jnp.concatenate([past_k, k], axis=1)
        v_all = jnp.concatenate([past_v, v], axis=1)
        kidx_all = jnp.concatenate([past_kidx, ki], axis=1)
        S = past + T
        mask = chunk_mask(pos, jnp.arange(S))
        attn = dsa_attend(q, qi, wi, k_all, v_all, kidx_all, mask, min(TOPK_MAX, S // 4))
    u = jax.nn.gelu(u)
    vn = layer_norm(jax.nn.gelu(vg), gmlp_ln_g, gmlp_ln_b)
    gm = gmlp_spatial(u, vn, gmlp_ws, gmlp_bs)

    y = jnp.concatenate([attn.reshape(B, T, WIDTH_A), gm], axis=-1) @ w_out
    x = x + g1 * y
    h2 = rms_norm(x, norm2_g) * (1 + sc2) + sh2
    x = x + g2 * (jnp.square(jax.nn.relu(h2 @ w_ff1)) @ w_ff2)
    return x, k, v, ki, vn


def setup_inputs(seed: int = 0) -> dict:
    key = jax.random.key(seed)
    ks = jax.random.split(key, 24)
    n = jax.random.normal
    f = jnp.float32
    return {
        "x_prompt": n(ks[0], (BATCH, SEQ, D_MODEL), f),
        "x_sample": n(ks[1], (DEC_BATCH, DEC_SEQ, D_MODEL), f),
        "c_prompt": n(ks[2], (BATCH, D_MODEL), f),
        "c_sample": n(ks[3], (DEC_BATCH, D_MODEL), f),
        "cache_k": n(ks[4], (DEPTH, DEC_BATCH, PAST_LEN, N_HEADS_A, HEAD_DIM), f),
        "cache_v": n(ks[5], (DEPTH, DEC_BATCH, PAST_LEN, N_HEADS_A, HEAD_DIM), f),
        "cache_kidx": n(ks[6], (DEPTH, DEC_BATCH, PAST_LEN, IDX_DIM), f),
        "w_ada": n(ks[7], (DEPTH, D_MODEL, 6 * D_MODEL), f) * (0.5 * D_MODEL ** -0.5),
        "b_ada": n(ks[8], (DEPTH, 6 * D_MODEL), f) * 0.01,
        "norm1_g": 1.0 + 0.05 * n(ks[9], (DEPTH, D_MODEL), f),
        "norm2_g": 1.0 + 0.05 * n(ks[10], (DEPTH, D_MODEL), f),
        "w_in": n(ks[11], (DEPTH, D_MODEL, IN_WIDTH), f) * (D_MODEL ** -0.5),
        "q_norm_g": 1.0 + 0.05 * n(ks[12], (DEPTH, HEAD_DIM), f),
        "k_norm_g": 1.0 + 0.05 * n(ks[13], (DEPTH, HEAD_DIM), f),
        "gmlp_ln_g": 1.0 + 0.05 * n(ks[14], (DEPTH, WIDTH_B), f),
        "gmlp_ln_b": 0.01 * n(ks[15], (DEPTH, WIDTH_B), f),
        "gmlp_ws": n(ks[16], (DEPTH, N_GROUPS_B, GMLP_CHUNK, GMLP_CHUNK), f) * (GMLP_CHUNK ** -0.5),
        "gmlp_bs": 1.0 + 0.1 * n(ks[17], (DEPTH, N_GROUPS_B, GMLP_CHUNK), f),
        "w_out": n(ks[18], (DEPTH, MIX_WIDTH, D_MODEL), f) * (MIX_WIDTH ** -0.5),
        "w_ff1": n(ks[19], (DEPTH, D_MODEL, D_FF), f) * (D_MODEL ** -0.5),
        "w_ff2": n(ks[20], (DEPTH, D_FF, D_MODEL), f) * (D_FF ** -0.5),
    }


def reference(x_prompt, x_sample, c_prompt, c_sample, cache_k, cache_v, cache_kidx, w_ada, b_ada, norm1_g, norm2_g, w_in, q_norm_g, k_norm_g, gmlp_ln_g, gmlp_ln_b, gmlp_ws, gmlp_bs, w_out, w_ff1, w_ff2):
    yp, ys = x_prompt, x_sample
    kp_l, vp_l, kip_l, ks_l, vs_l, kis_l, gvs_l = [], [], [], [], [], [], []
    for l in range(DEPTH):
        params = (w_ada[l], b_ada[l], norm1_g[l], norm2_g[l], w_in[l], q_norm_g[l], k_norm_g[l], gmlp_ln_g[l], gmlp_ln_b[l], gmlp_ws[l], gmlp_bs[l], w_out[l], w_ff1[l], w_ff2[l])
        yp, kp, vp, kip, _ = hybrid_layer(yp, c_prompt, None, None, None, *params)
        ys, ksm, vsm, kism, gvs = hybrid_layer(ys, c_sample, cache_k[l], cache_v[l], cache_kidx[l], *params)
        kp_l.append(kp); vp_l.append(vp); kip_l.append(kip)
        ks_l.append(ksm); vs_l.append(vsm); kis_l.append(kism); gvs_l.append(gvs)
    return (yp, ys, jnp.stack(kp_l), jnp.stack(vp_l), jnp.stack(kip_l), jnp.stack(ks_l), jnp.stack(vs_l), jnp.stack(kis_l), jnp.stack(gvs_l))
```

```python
import numpy as np
import concourse.bass as bass
import concourse.mybir as mybir
from concourse.bass_utils import run_bass_kernel_spmd

F32 = mybir.dt.float32
BF16 = mybir.dt.bfloat16
U32 = mybir.dt.uint32
AF = mybir.ActivationFunctionType
ALU = mybir.AluOpType
AX = mybir.AxisListType


class Buf:
    __slots__ = ("name", "w", "r", "psum")

    def __init__(self, name):
        self.name = name
        self.w = None
        self.r = []
        self.psum = False


class Prog:
    ENGS = ("pe", "act", "dve", "pool", "sp")
    NDMA = 8

    def __init__(self, nc, stack):
        self.nc = nc
        self.ops = {e: [] for e in self.ENGS}
        self.cnt = {e: 0 for e in self.ENGS}
        self.sem = {}
        for e in self.ENGS:
            self.sem[e] = stack.enter_context(nc.semaphore("s_" + e))
        self.dsem = {}
        self.dcnt = {}
        for q in ("sp", "pool", "act"):
            self.dsem[q] = [stack.enter_context(nc.semaphore("d_%s%d" % (q, i))) for i in range(self.NDMA)]
            self.dcnt[q] = 0
        self.seen = {e: {} for e in self.ENGS}
        self.out_waits = []

    def _semobj(self, key):
        if isinstance(key, str):
            return self.sem[key]
        q, i = key
        return self.dsem[q][i]

    def _need(self, eng, waits, key, val):
        if self.seen[eng].get(key, 0) >= val:
            return
        self.seen[eng][key] = val
        waits.append((key, val))

    def _deps(self, eng, reads, writes, waits, same_engine_raw=True, strict=False):
        for b in reads:
            if b.w is not None:
                k, v = b.w
                if k != eng or same_engine_raw or strict:
                    self._need(eng, waits, k, v)
            if b.psum:
                for (k, v) in b.r:
                    if k != eng:
                        self._need(eng, waits, k, v)
        for b in writes:
            if b.w is not None:
                k, v = b.w
                if k != eng or strict:
                    self._need(eng, waits, k, v)
            for (k, v) in b.r:
                if k != eng or strict:
                    self._need(eng, waits, k, v)

    def op(self, eng, fn, reads=(), writes=(), raw_same=True):
        waits = []
        self._deps(eng, reads, writes, waits, strict=(eng != "pe"))
        self.cnt[eng] += 1
        v = self.cnt[eng]
        self.ops[eng].append((waits, fn, (eng, v)))
        for b in reads:
            b.r.append((eng, v))
        for b in writes:
            b.w = (eng, v)
            b.r = []
        return v

    def dma(self, q, fn, reads=(), writes=()):
        waits = []
        self._deps(q, reads, writes, waits, strict=True)
        n = self.dcnt[q]
        self.dcnt[q] += 1
        slot = n % self.NDMA
        key = (q, slot)
        prev = 16 * (n // self.NDMA)
        if prev > 0:
            self._need(q, waits, key, prev)
        val = prev + 16
        self.ops[q].append((waits, fn, (key, 16)))
        for b in reads:
            b.r.append((key, val))
        for b in writes:
            b.w = (key, val)
            b.r = []
        return (key, val)

    def barrier(self):
        targets = [(e, self.cnt[e]) for e in self.ENGS if self.cnt[e] > 0]
        for q in self.dcnt:
            n = self.dcnt[q]
            for slot in range(min(n, self.NDMA)):
                uses = (n - slot + self.NDMA - 1) // self.NDMA
                targets.append(((q, slot), 16 * uses))
        for e in self.ENGS:
            waits = []
            for (k, v) in targets:
                if k != e:
                    self._need(e, waits, k, v)
            if waits:
                self.ops[e].append((waits, None, None))

    def emit(self, block):
        nc = self.nc
        prog = self

        def run(engine, name):
            for (waits, fn, inc) in prog.ops[name]:
                for (k, v) in waits:
                    engine.wait_ge(prog._semobj(k), v)
                if fn is None:
                    continue
                ins = fn(engine)
                key, amt = inc
                if isinstance(key, str):
                    ins.then_inc(prog.sem[key], 1)
                else:
                    ins.then_inc(prog._semobj(key), 16)
            prog.ops[name] = []

        @block.tensor
        def _(t):
            run(t, "pe")

        @block.scalar
        def _(s):
            run(s, "act")

        @block.vector
        def _(v):
            run(v, "dve")

        @block.gpsimd
        def _(g):
            run(g, "pool")

        @block.sync
        def _(s):
            run(s, "sp")


from contextlib import ExitStack

D = 1024
INW = 2884
EPS = 1e-6
NEG = -1.0e30
MNEG = -30000.0


class T:
    def __init__(self, t, name):
        self.t = t
        self.b = Buf(name)

    def __getitem__(self, k):
        return self.t[k]


class Cfg:
    def __init__(self, SEQ=8192, PAST=2048, NITER=14):
        self.SEQ = SEQ
        self.PAST = PAST
        self.NITER = NITER
        self.NBLK = SEQ // 128
        self.NI = self.NBLK // 4
        self.NB1 = self.NI + 1
        self.TOPK_P = min(256, SEQ // 4)
        self.TOPK_S = min(256, (PAST + 32) // 4)
        self.NKS = PAST // 128 + 1
        self.SPAD = self.NKS * 128
        self.KMAX = max(SEQ, self.SPAD)
        self.stop = 99
        self.sub = 99


def build(cfg):
    SEQ, PAST, NITER = cfg.SEQ, cfg.PAST, cfg.NITER
    NBLK, NI, NB1 = cfg.NBLK, cfg.NI, cfg.NB1
    NKS, SPAD, KMAX = cfg.NKS, cfg.SPAD, cfg.KMAX
    nc = bass.Bass("TRN2", target_bir_lowering=False)

    def din(name, shape, dt=F32):
        return nc.dram_tensor(name, list(shape), dt, kind="ExternalInput").ap()

    def dout(name, shape, dt=F32):
        return nc.dram_tensor(name, list(shape), dt, kind="ExternalOutput").ap()

    x_seq = din("x_seq", [SEQ, D])
    x_own = din("x_own", [NB1 * 128, D])
    c_all = din("c_all", [5, D])
    rope_seq = din("rope_seq", [SEQ, 64])
    rope_own = din("rope_own", [NB1 * 128, 64])
    cache_k = din("cache_k", [4, PAST, 512])
    cache_v = din("cache_v", [4, PAST, 512])
    cache_ki = din("cache_ki", [4, PAST, 64])
    w_ada = din("w_ada", [D, 6 * D])
    b_ada = din("b_ada", [1, 6 * D])
    norm1_g = din("norm1_g", [1, D])
    norm2_g = din("norm2_g", [1, D])
    w_in = din("w_in", [D, INW])
    q_norm_g = din("q_norm_g", [1, 64])
    k_norm_g = din("k_norm_g", [1, 64])
    ln_g = din("gmlp_ln_g", [1, 512])
    ln_b = din("gmlp_ln_b", [1, 512])
    gws = din("gmlp_ws", [4, 128, 128])
    gbs = din("gmlp_bs", [4, 128])
    w_out = din("w_out", [1024, 1024])
    w_ff1 = din("w_ff1", [1024, 4096])
    w_ff2 = din("w_ff2", [4096, 1024])
    ident_d = din("ident", [128, 128])
    sel_d = din("sel", [2, 5, 128])
    cmask_p_d = din("cmask_p", [128, 512])
    cmask_s_d = din("cmask_s", [128, 128])
    tril_d = din("tril", [2, 128, 128])
    pow2_d = din("pow2", [128, NITER])

    y_own = dout("y_own", [NB1 * 128, D])
    k_own = dout("k_own", [NB1 * 128, 512])
    v_own = dout("v_own", [NB1 * 128, 512])
    ki_own = dout("ki_own", [NB1 * 128, 64])
    gv_own = dout("gv_own", [128, 512])

    v_scr = nc.dram_tensor("v_scr", [KMAX, 520], BF16).ap()
    mixa_scr = nc.dram_tensor("mixa_scr", [NB1, 65, 1024], BF16).ap()
    mixg_scr = nc.dram_tensor("mixg_scr", [NB1, 128, 512], BF16).ap()
    bc_scr = nc.dram_tensor("bc_scr", [2, 6, 128, 1024], F32).ap()
    win_b = nc.dram_tensor("win_b", [1024, INW], BF16).ap()
    b_winb = Buf("win_b")
    wf1_b = nc.dram_tensor("wf1_b", [1024, 4096], BF16).ap()
    wf2_b = nc.dram_tensor("wf2_b", [4096, 1024], BF16).ap()
    wo_b = nc.dram_tensor("wo_b", [1024, 1024], BF16).ap()
    b_wf1b = Buf("wf1_b")
    b_wf2b = Buf("wf2_b")
    b_wob = Buf("wo_b")
    b_vscr = Buf("v_scr")
    b_mixa = Buf("mixa_scr")
    b_mixg = Buf("mixg_scr")
    b_bcscr = Buf("bc_scr")

    with ExitStack() as outer:
        P = Prog(nc, outer)

        def sb(st, name, shape, dt=F32):
            return T(st.enter_context(nc.sbuf_tensor(name, list(shape), dt)), name)

        psum = outer.enter_context(nc.psum_tensor("psum", [128, 8, 512], F32))
        PS = [T(psum[:, k, :], "ps%d" % k) for k in range(8)]
        for p_ in PS:
            p_.b.psum = True

        def run_block():
            P.barrier()
            with nc.Block() as blk:
                P.emit(blk)

        identb = sb(outer, "identb", [128, 128], BF16)
        ones_f = sb(outer, "ones_f", [128, 128], F32)
        eps_t = sb(outer, "eps_t", [128, 1], F32)
        wis = sb(outer, "wis", [128, NB1, 4], F32)
        ksn_b = sb(outer, "ksn_b", [128, 512], BF16)
        kisn_b = sb(outer, "kisn_b", [128, 128], BF16)
        vsn_b = sb(outer, "vsn_b", [128, 8, 65], BF16)
        gk_bc = sb(outer, "gk_bc", [128, 64], F32)
        gq_bc = sb(outer, "gq_bc", [128, 64], F32)
        mid = ExitStack()
        ident4 = sb(outer, "ident4", [128, 512], BF16)

        P.dma("pool", lambda e: e.dma_start(out=identb[:], in_=ident_d[:, :]), writes=[identb.b])
        for r4 in range(4):
            P.dma("pool", lambda e, r4=r4: e.dma_start(out=ident4[:, r4 * 128:(r4 + 1) * 128], in_=ident_d[:, :]), writes=[ident4.b])
        P.op("dve", lambda e: e.memset(ones_f[:], 1.0), writes=[ones_f.b])
        P.op("dve", lambda e: e.memset(eps_t[:], EPS), writes=[eps_t.b])
        P.dma("sp", lambda e: e.dma_start(out=gk_bc[:], in_=k_norm_g[0:1, :].partition_broadcast(128)), writes=[gk_bc.b])
        P.dma("sp", lambda e: e.dma_start(out=gq_bc[:], in_=q_norm_g[0:1, :].partition_broadcast(128)), writes=[gq_bc.b])

        with ExitStack() as st:
            cT = sb(st, "cT", [128, 8, 5], F32)
            sT = sb(st, "sT", [128, 8, 5], F32)
            mod5 = sb(st, "mod5", [5, 6 * D], F32)
            bad = [sb(st, "bad%d" % i, [5, 512], F32) for i in range(2)]
            wa = [sb(st, "wa%d" % i, [128, 8, 512], F32) for i in range(2)]
            sel_t = sb(st, "sel_t", [5, 2, 128], F32)
            ng = [sb(st, "ng%d" % i, [128, D], F32) for i in range(2)]
            bct = [sb(st, "bct%d" % i, [128, D], F32) for i in range(2)]
            for r in range(5):
                P.dma("sp", lambda e, r=r: e.dma_start(out=cT[:, :, r], in_=c_all[r, :].rearrange("(c p) -> p c", p=128), allow_slow_non_contiguous=True), writes=[cT.b])
            P.dma("sp", lambda e: e.dma_start(out=sel_t[:], in_=sel_d.rearrange("s r p -> r s p")), writes=[sel_t.b])
            P.dma("sp", lambda e: e.dma_start(out=ng[0][:], in_=norm1_g[0:1, :].partition_broadcast(128)), writes=[ng[0].b])
            P.dma("sp", lambda e: e.dma_start(out=ng[1][:], in_=norm2_g[0:1, :].partition_broadcast(128)), writes=[ng[1].b])
            for c in range(8):
                P.dma("pool", lambda e, c=c: e.dma_start(out=win_b[c * 128:(c + 1) * 128, 0:1442], in_=w_in[c * 128:(c + 1) * 128, 0:1442]), writes=[b_winb])
                P.dma("pool", lambda e, c=c: e.dma_start(out=win_b[c * 128:(c + 1) * 128, 1442:INW], in_=w_in[c * 128:(c + 1) * 128, 1442:INW]), writes=[b_winb])
            P.op("act", lambda e: e.activation(out=sT[:], in_=cT[:], func=AF.Silu), reads=[cT.b], writes=[sT.b])
            for nt in range(12):
                w = wa[nt % 2]
                P.dma("sp", lambda e, w=w, nt=nt: e.dma_start(out=w[:], in_=w_ada[:, nt * 512:(nt + 1) * 512].rearrange("(c p) n -> p c n", p=128)), writes=[w.b])
                bd = bad[nt % 2]
                P.dma("sp", lambda e, bd=bd, nt=nt: e.dma_start(out=bd[:], in_=b_ada[0:1, nt * 512:(nt + 1) * 512].partition_broadcast(5)), writes=[bd.b])
                ps = PS[nt % 2]
                for c in range(8):
                    P.op("pe", lambda e, w=w, c=c, ps=ps: e.matmul(ps[0:5, :], lhsT=sT[:, c, :], rhs=w[:, c, :], start=(c == 0), stop=(c == 7)),
                         reads=[sT.b, w.b], writes=[ps.b])
                P.op("dve", lambda e, ps=ps, nt=nt, bd=bd: e.tensor_tensor(out=mod5[0:5, nt * 512:(nt + 1) * 512], in0=ps[0:5, :], in1=bd[0:5, :], op=ALU.add),
                     reads=[ps.b, bd.b], writes=[mod5.b])
            for s in range(2):
                for k in range(6):
                    for half in range(2):
                        ps = PS[2 + half]
                        P.op("pe", lambda e, ps=ps, s=s, k=k, half=half: e.matmul(ps[:, :], lhsT=sel_t[0:5, s, :], rhs=mod5[0:5, k * D + half * 512:k * D + half * 512 + 512], start=True, stop=True),
                             reads=[sel_t.b, mod5.b], writes=[ps.b])
                    pv = psum[:, 2:4, :].rearrange("p a b -> p (a b)")
                    rb = [PS[2].b, PS[3].b]
                    if k in (1, 4):
                        dst = bct[s]
                        g = ng[0] if k == 1 else ng[1]
                        P.op("dve", lambda e, dst=dst, g=g: e.scalar_tensor_tensor(out=dst[:], in0=pv, scalar=1.0, in1=g[:], op0=ALU.add, op1=ALU.mult),
                             reads=rb + [g.b], writes=[dst.b])
                    else:
                        dst = bct[s]
                        P.op("act", lambda e, dst=dst: e.activation(out=dst[:], in_=pv, func=AF.Identity), reads=rb, writes=[dst.b])
                    if True:
                        slot = {2: 0, 3: 2, 4: 1, 5: 3, 1: 4, 0: 5}[k]
                        P.dma("pool", lambda e, s=s, slot=slot, dst=dst: e.dma_start(out=bc_scr[s, slot, :, :], in_=dst[:]), reads=[dst.b], writes=[b_bcscr])
            run_block()
        if cfg.stop == 0:
            return nc

        KT = sb(mid, "KT", [128, 4, KMAX], BF16)
        KIT = sb(mid, "KIT", [128, KMAX], BF16)
        qt_scr = nc.dram_tensor("qt_scr", [NB1, 128, 512], BF16).ap()
        qit_scr = nc.dram_tensor("qit_scr", [NB1, 128, 256], BF16).ap()
        b_qt = Buf("qt_scr")
        b_qit = Buf("qit_scr")

        precast = []
        for r in range(8):
            precast.append(lambda r=r: P.dma("pool", lambda e: e.dma_start(out=wo_b[r * 128:(r + 1) * 128, :], in_=w_out[r * 128:(r + 1) * 128, :]), writes=[b_wob]))
        for r in range(8):
            for q in range(4):
                precast.append(lambda r=r, q=q: P.dma("pool", lambda e: e.dma_start(out=wf1_b[r * 128:(r + 1) * 128, q * 1024:(q + 1) * 1024], in_=w_ff1[r * 128:(r + 1) * 128, q * 1024:(q + 1) * 1024]), writes=[b_wf1b]))
        for r in range(32):
            precast.append(lambda r=r: P.dma("pool", lambda e: e.dma_start(out=wf2_b[r * 128:(r + 1) * 128, :], in_=w_ff2[r * 128:(r + 1) * 128, :]), writes=[b_wf2b]))

        with ExitStack() as st:
            Wb = sb(st, "Wb", [128, 8, INW], BF16)
            xt = [sb(st, "xt%d" % i, [128, D], F32) for i in range(2)]
            rp = [sb(st, "rp%d" % i, [128, 64], F32) for i in range(2)]
            tmpf2 = [sb(st, "tmpf%d" % i, [128, D], F32) for i in range(2)]
            hb2 = [sb(st, "hb%d" % i, [128, D], BF16) for i in range(2)]
            hT2 = [sb(st, "hT%d" % i, [128, 8, 128], BF16) for i in range(2)]
            st12 = [sb(st, "st1%d" % i, [128, 4], F32) for i in range(2)]
            sq = sb(st, "sq", [128, 512], F32)
            kg = sb(st, "kg", [128, 512], F32)
            ta = sb(st, "ta", [128, 512], F32)
            tb = sb(st, "tb", [128, 512], F32)
            kr = sb(st, "kr", [128, 512], F32)
            kn = [sb(st, "kn0", [128, 512], F32)] * 2
            knb = sb(st, "knb", [128, 512], BF16)
            st8 = sb(st, "st8", [128, 3, 8], F32)
            vb = [sb(st, "vb%d" % i, [128, 8, 65], BF16) for i in range(2)]
            vf = [sb(st, "vf0", [128, 512], F32)] * 2
            dt_ = sb(st, "dt_", [128, 324], F32)
            kif = [sb(st, "kif%d" % i, [128, 64], F32) for i in range(2)]
            kib = sb(st, "kib", [128, 128], BF16)
            qib = sb(st, "qib", [128, 256], BF16)
            ug = sb(st, "ug", [128, 512], F32)
            vn = sb(st, "vn", [128, 512], F32)
            vnb = sb(st, "vnb", [128, 512], BF16)
            gmb = sb(st, "gmb", [128, 512], BF16)
            mgT = sb(st, "mgT", [128, 512], BF16)
            bnst = sb(st, "bnst", [128, 8], F32)
            lng_bc = sb(st, "lng_bc", [128, 512], F32)
            lnb_bc = sb(st, "lnb_bc", [128, 512], F32)
            WsT = [sb(st, "WsT%d" % s, [128, 4, 128], BF16) for s in range(2)]
            bs_t = [sb(st, "bs_t%d" % s, [128, 4], F32) for s in range(2)]
            wsn = sb(st, "wsn", [128, 128], F32)
            wsb = sb(st, "wsb", [128, 128], BF16)
            tril_t = sb(st, "tril_t", [128, 2, 128], F32)
            qts = sb(st, "qts", [128, 4, 128], BF16)
            G1 = [sb(st, "G1_0", [128, D], F32)]
            S1 = [sb(st, "S1_0", [128, D], F32)]
            gx = sb(st, "gx", [128, 512], F32)
            gs = sb(st, "gs", [128, 512], F32)
            P.dma("sp", lambda e: e.dma_start(out=G1[0][:], in_=bc_scr[0, 4, :, :]), reads=[b_bcscr], writes=[G1[0].b])
            P.dma("sp", lambda e: e.dma_start(out=S1[0][:], in_=bc_scr[0, 5, :, :]), reads=[b_bcscr], writes=[S1[0].b])
            qits = sb(st, "qits", [128, 2, 128], BF16)

            for c in range(8):
                P.dma("sp", lambda e, c=c: e.dma_start(out=Wb[:, c, :], in_=win_b[c * 128:(c + 1) * 128, :]), reads=[b_winb], writes=[Wb.b])
            P.dma("sp", lambda e: e.dma_start(out=lng_bc[:], in_=ln_g[0:1, :].partition_broadcast(128)), writes=[lng_bc.b])
            P.dma("sp", lambda e: e.dma_start(out=lnb_bc[:], in_=ln_b[0:1, :].partition_broadcast(128)), writes=[lnb_bc.b])
            P.dma("sp", lambda e: e.dma_start(out=tril_t[:], in_=tril_d.rearrange("s p q -> p s q")), writes=[tril_t.b])
            for v_ in vb:
                P.op("pool", lambda e, v_=v_: e.memset(v_[:], 1.0), writes=[v_.b])
            P.op("pool", lambda e: e.memset(vsn_b[:], 1.0), writes=[vsn_b.b])
            P.dma("sp", lambda e: e.dma_start(out=bs_t[0][:], in_=gbs.rearrange("g t -> t g"), allow_slow_non_contiguous=True), writes=[bs_t[0].b])
            for q4 in range(4):
                P.dma("sp", lambda e, q4=q4: e.dma_start(out=bs_t[1][q4 * 32:(q4 + 1) * 32, :], in_=gbs[:, 0:32].rearrange("g t -> t g"), allow_slow_non_contiguous=True), writes=[bs_t[1].b])
            for s in range(2):
                for g in range(4):
                    if s == 0:
                        P.dma("sp", lambda e, g=g: e.dma_start(out=wsn[:], in_=gws[g, :, :]), writes=[wsn.b])
                    else:
                        P.op("dve", lambda e: e.memset(wsn[:], 0.0), writes=[wsn.b])
                        for q4 in range(4):
                            P.dma("sp", lambda e, g=g, q4=q4: e.dma_start(out=wsn[q4 * 32:(q4 + 1) * 32, q4 * 32:(q4 + 1) * 32], in_=gws[g, 0:32, 0:32]), writes=[wsn.b])
                    P.op("dve", lambda e, s=s: e.tensor_tensor(out=wsb[:], in0=wsn[:], in1=tril_t[:, s, :], op=ALU.mult), reads=[wsn.b, tril_t.b], writes=[wsb.b])
                    ps = PS[7]
                    pb = ps[:, :].bitcast(BF16)
                    P.op("pe", lambda e, pb=pb: e.transpose(out=pb[:, 0:128], in_=wsb[:], identity=identb[:]), reads=[wsb.b, identb.b], writes=[ps.b])
                    P.op("act", lambda e, pb=pb, s=s, g=g: e.activation(out=WsT[s][:, g, :], in_=pb[:, 0:128], func=AF.Identity), reads=[ps.b], writes=[WsT[s].b])

            def rope4(eng, src, dst, H, rpt):
                n = H * 64
                cosb = rpt[:, 0:32].unsqueeze(1).unsqueeze(1).broadcast_to([128, H, 2, 32])
                sinb = rpt[:, 32:64].unsqueeze(1).unsqueeze(1).broadcast_to([128, H, 2, 32])
                s4 = src[:, 0:n].rearrange("p (h t d) -> p h t d", h=H, t=2)
                a4 = ta[:, 0:n].rearrange("p (h t d) -> p h t d", h=H, t=2)
                b4 = tb[:, 0:n].rearrange("p (h t d) -> p h t d", h=H, t=2)
                d4 = dst[:, 0:n].rearrange("p (h t d) -> p h t d", h=H, t=2)
                P.op(eng, lambda e: e.tensor_tensor(out=a4, in0=s4, in1=cosb, op=ALU.mult), reads=[src.b, rpt.b], writes=[ta.b])
                P.op(eng, lambda e: e.tensor_tensor(out=b4, in0=s4, in1=sinb, op=ALU.mult), reads=[src.b, rpt.b], writes=[tb.b])
                P.op(eng, lambda e: e.tensor_tensor(out=d4[:, :, 0, :], in0=a4[:, :, 0, :], in1=b4[:, :, 1, :], op=ALU.subtract), reads=[ta.b, tb.b], writes=[dst.b])
                P.op(eng, lambda e: e.tensor_tensor(out=d4[:, :, 1, :], in0=a4[:, :, 1, :], in1=b4[:, :, 0, :], op=ALU.add), reads=[ta.b, tb.b], writes=[dst.b])

            def norm_to_hT(xtile, Gt, St, par):
                tmpf, hb, hT, st1 = tmpf2[par], hb2[par], hT2[par], st12[par]
                P.op("act", lambda e: e.activation(out=hb[:], in_=xtile[:], func=AF.Square, accum_out=st1[:, 0:1]), reads=[xtile.b], writes=[hb.b, st1.b])
                P.op("act", lambda e: e.activation(out=st1[:, 1:2], in_=st1[:, 0:1], func=AF.Sqrt, bias=eps_t[:, 0:1], scale=1.0 / D), reads=[st1.b, eps_t.b], writes=[st1.b])
                P.op("dve", lambda e: e.reciprocal(out=st1[:, 2:3], in_=st1[:, 1:2]), reads=[st1.b], writes=[st1.b])
                P.op("dve", lambda e: e.scalar_tensor_tensor(out=tmpf[:], in0=xtile[:], scalar=st1[:, 2:3], in1=Gt[:], op0=ALU.mult, op1=ALU.mult), reads=[xtile.b, st1.b, Gt.b], writes=[tmpf.b])
                P.op("dve", lambda e: e.tensor_tensor(out=hb[:], in0=tmpf[:], in1=St[:], op=ALU.add), reads=[tmpf.b, St.b], writes=[hb.b])
                ps = PS[0]
                pb = ps[:, :].bitcast(BF16)
                for c in range(8):
                    P.op("pe", lambda e, c=c: e.transpose(out=pb[:, c * 128:(c + 1) * 128], in_=hb[:, c * 128:(c + 1) * 128], identity=identb[:]), reads=[hb.b, identb.b], writes=[ps.b])
                P.op("act", lambda e: e.activation(out=hT[:].rearrange("p c t -> p (c t)"), in_=pb, func=AF.Identity), reads=[ps.b], writes=[hT.b])

            def proj(ps, c0, c1, par):
                n = c1 - c0
                hT = hT2[par]
                for c in range(8):
                    P.op("pe", lambda e, c=c: e.matmul(ps[:, 0:n], lhsT=hT[:, c, :], rhs=Wb[:, c, c0:c1], start=(c == 0), stop=(c == 7)), reads=[hT.b, Wb.b], writes=[ps.b])

            def qk_post(ps, g_bc, rpt, dst_ap, dst_b):
                P.op("act", lambda e: e.activation(out=sq[:], in_=ps[:, :], func=AF.Square), reads=[ps.b], writes=[sq.b])
                P.op("dve", lambda e: e.tensor_tensor(out=kg[:].rearrange("p (h d) -> p h d", h=8), in0=ps[:, :].rearrange("p (h d) -> p h d", h=8),
                                                      in1=g_bc[:, :].unsqueeze(1).broadcast_to([128, 8, 64]), op=ALU.mult), reads=[ps.b, g_bc.b], writes=[kg.b])
                P.op("dve", lambda e: e.tensor_reduce(out=st8[:, 0, :], in_=sq[:].rearrange("p (h d) -> p h d", h=8), axis=AX.X, op=ALU.add), reads=[sq.b], writes=[st8.b])
                P.op("act", lambda e: e.activation(out=st8[:, 1, :], in_=st8[:, 0, :], func=AF.Sqrt, bias=eps_t[:, 0:1], scale=1.0 / 64), reads=[st8.b, eps_t.b], writes=[st8.b])
                P.op("dve", lambda e: e.reciprocal(out=st8[:, 2, :], in_=st8[:, 1, :]), reads=[st8.b], writes=[st8.b])
                rope4("dve", kg, kr, 8, rpt)
                P.op("dve", lambda e: e.tensor_tensor(out=dst_ap.rearrange("p (h d) -> p h d", h=8), in0=kr[:].rearrange("p (h d) -> p h d", h=8),
                                                      in1=st8[:, 2, :].unsqueeze(2).broadcast_to([128, 8, 64]), op=ALU.mult), reads=[kr.b, st8.b], writes=[dst_b])

            def transposes_to(src, ncol, dst_ap, dst_b, psk=1):
                ps = PS[psk]
                pb = ps[:, :].bitcast(BF16)
                for c in range(ncol):
                    P.op("pe", lambda e, c=c: e.transpose(out=pb[:, c * 128:(c + 1) * 128], in_=src[:, c * 128:(c + 1) * 128], identity=identb[:]), reads=[src.b, identb.b], writes=[ps.b])
                P.op("act", lambda e: e.activation(out=dst_ap, in_=pb[:, 0:ncol * 128].rearrange("p (c t) -> p c t", c=ncol), func=AF.Identity), reads=[ps.b], writes=[dst_b])

            def gelu(src_ps, dst):
                P.op("act", lambda e: e.activation(out=kg[:], in_=src_ps[:, :], func=AF.Identity), reads=[src_ps.b], writes=[kg.b])
                P.op("act", lambda e: e.activation(out=sq[:], in_=src_ps[:, :], func=AF.Square), reads=[src_ps.b], writes=[sq.b])
                P.op("dve", lambda e: e.tensor_scalar(out=sq[:], in0=sq[:], scalar1=0.044715, scalar2=1.0, op0=ALU.mult, op1=ALU.add), reads=[sq.b], writes=[sq.b])
                P.op("dve", lambda e: e.tensor_tensor(out=ta[:], in0=sq[:], in1=kg[:], op=ALU.mult), reads=[sq.b, kg.b], writes=[ta.b])
                P.op("act", lambda e: e.activation(out=ta[:], in_=ta[:], func=AF.Tanh, scale=0.7978845608028654), reads=[ta.b], writes=[ta.b])
                P.op("dve", lambda e: e.tensor_scalar(out=ta[:], in0=ta[:], scalar1=1.0, scalar2=0.5, op0=ALU.add, op1=ALU.mult), reads=[ta.b], writes=[ta.b])
                P.op("dve", lambda e: e.tensor_tensor(out=dst[:], in0=ta[:], in1=kg[:], op=ALU.mult), reads=[ta.b, kg.b], writes=[dst.b])

            if cfg.stop == 1:
                run_block()
                return nc
            def load_x(src, rsrc, r0, i):
                P.dma("sp", lambda e: e.dma_start(out=xt[i % 2][:], in_=src[r0:r0 + 128, :]), writes=[xt[i % 2].b])
                P.dma("sp", lambda e: e.dma_start(out=rp[i % 2][:], in_=rsrc[r0:r0 + 128, :]), writes=[rp[i % 2].b])

            def all_s12(t):
                par = t % 2
                norm_to_hT(xt[par], G1[0], S1[0], par)
                proj(PS[2 + 3 * par], 512, 1024, par)
                proj(PS[3 + 3 * par], 1024, 1536, par)
                proj(PS[4 + 3 * par], 1792, 1856, par)

            def all_s34(t):
                par = t % 2
                rpt = rp[par]
                psK, psV, psKI = PS[2 + 3 * par], PS[3 + 3 * par], PS[4 + 3 * par]
                qk_post(psK, gk_bc, rpt, knb[:], knb.b)
                transposes_to(knb, 4, KT[:, :, t * 128:(t + 1) * 128], KT.b)
                v_ = vb[par]
                P.op("act", lambda e: e.activation(out=v_[:, :, 1:65], in_=psV[:, :].rearrange("p (h d) -> p h d", h=8), func=AF.Identity), reads=[psV.b], writes=[v_.b])
                P.dma("pool", lambda e: e.dma_start(out=v_scr[t * 128:(t + 1) * 128, :], in_=v_[:].rearrange("p h d -> p (h d)")), reads=[v_.b], writes=[b_vscr])
                if precast:
                    precast.pop(0)()
                kf_ = kif[par]
                P.op("act", lambda e: e.activation(out=kg[:, 0:64], in_=psKI[:, 0:64], func=AF.Identity), reads=[psKI.b], writes=[kg.b])
                rope4("dve", kg, kf_, 1, rpt)
                P.op("dve", lambda e: e.tensor_copy(out=kib[:].rearrange("p (a d) -> p a d", a=2), in_=kf_[:, :].unsqueeze(1).broadcast_to([128, 2, 64])), reads=[kf_.b], writes=[kib.b])
                ps = PS[1]
                pb = ps[:, :].bitcast(BF16)
                P.op("pe", lambda e: e.transpose(out=pb[:, 512:640], in_=kib[:], identity=identb[:]), reads=[kib.b, identb.b], writes=[ps.b])
                P.op("act", lambda e: e.activation(out=KIT[:, t * 128:(t + 1) * 128], in_=pb[:, 512:640], func=AF.Identity), reads=[ps.b], writes=[KIT.b])

            load_x(x_seq, rope_seq, 0, 0)
            if NBLK > 1:
                load_x(x_seq, rope_seq, 128, 1)
            all_s12(0)
            for t in range(NBLK):
                if t + 1 < NBLK:
                    all_s12(t + 1)
                all_s34(t)
                if t + 2 < NBLK:
                    load_x(x_seq, rope_seq, (t + 2) * 128, t + 2)

            if cfg.stop == 2:
                run_block()
                return nc
            def gelu_p1(src_ps, dst, tmp):
                P.op("act", lambda e: e.activation(out=dst[:], in_=src_ps[:, :], func=AF.Identity), reads=[src_ps.b], writes=[dst.b])
                P.op("act", lambda e: e.activation(out=tmp[:], in_=src_ps[:, :], func=AF.Square), reads=[src_ps.b], writes=[tmp.b])
                P.op("pool", lambda e: e.tensor_scalar(out=tmp[:], in0=tmp[:], scalar1=0.044715, scalar2=1.0, op0=ALU.mult, op1=ALU.add), reads=[tmp.b], writes=[tmp.b])
                P.op("pool", lambda e: e.tensor_tensor(out=tmp[:], in0=tmp[:], in1=dst[:], op=ALU.mult), reads=[tmp.b, dst.b], writes=[tmp.b])

            def gelu_p2(dst, tmp):
                P.op("act", lambda e: e.activation(out=tmp[:], in_=tmp[:], func=AF.Tanh, scale=0.7978845608028654), reads=[tmp.b], writes=[tmp.b])
                P.op("pool", lambda e: e.tensor_scalar(out=tmp[:], in0=tmp[:], scalar1=1.0, scalar2=0.5, op0=ALU.add, op1=ALU.mult), reads=[tmp.b], writes=[tmp.b])
                P.op("pool", lambda e: e.tensor_tensor(out=dst[:], in0=tmp[:], in1=dst[:], op=ALU.mult), reads=[tmp.b, dst.b], writes=[dst.b])

            bK, bV, bD, bQ, bU, bVG = PS[2], PS[3], PS[4], PS[5], PS[6], PS[7]
            load_x(x_own, rope_own, 0, 0)
            if NB1 > 1:
                load_x(x_own, rope_own, 128, 1)
            if NI == 0:
                P.dma("sp", lambda e: e.dma_start(out=G1[0][:], in_=bc_scr[1, 4, :, :]), reads=[b_bcscr], writes=[G1[0].b])
                P.dma("sp", lambda e: e.dma_start(out=S1[0][:], in_=bc_scr[1, 5, :, :]), reads=[b_bcscr], writes=[S1[0].b])
            norm_to_hT(xt[0], G1[0], S1[0], 0)
            proj(bU, 1860, 2372, 0)
            proj(bVG, 2372, 2884, 0)
            proj(bK, 512, 1024, 0)
            proj(bV, 1024, 1536, 0)
            proj(bD, 1536, 1860, 0)
            proj(bQ, 0, 512, 0)
            for i in range(NB1):
                s = 0 if i < NI else 1
                par = i % 2
                nxt = i + 1 < NB1
                pn = (i + 1) % 2
                if nxt:
                    if i + 1 == NI:
                        P.dma("sp", lambda e: e.dma_start(out=G1[0][:], in_=bc_scr[1, 4, :, :]), reads=[b_bcscr], writes=[G1[0].b])
                        P.dma("sp", lambda e: e.dma_start(out=S1[0][:], in_=bc_scr[1, 5, :, :]), reads=[b_bcscr], writes=[S1[0].b])
                    norm_to_hT(xt[pn], G1[0], S1[0], pn)
                xtile, rpt = xt[i % 2], rp[i % 2]
                r0 = i * 128
                gelu_p1(bU, ug, gx)
                gelu_p1(bVG, vn, gs)
                if nxt:
                    proj(bU, 1860, 2372, pn)
                    proj(bVG, 2372, 2884, pn)
                kn_ = kn[i % 2]
                qk_post(bK, gk_bc, rpt, kn_[:], kn_.b)
                if nxt:
                    proj(bK, 512, 1024, pn)
                P.dma("pool", lambda e, kn_=kn_, r0=r0: e.dma_start(out=k_own[r0:r0 + 128, :], in_=kn_[:]), reads=[kn_.b])
                if s == 1:
                    P.op("dve", lambda e, kn_=kn_: e.tensor_copy(out=ksn_b[:], in_=kn_[:]), reads=[kn_.b], writes=[ksn_b.b])
                vf_ = vf[i % 2]
                P.op("act", lambda e, vf_=vf_: e.activation(out=vf_[:], in_=bV[:, :], func=AF.Identity), reads=[bV.b], writes=[vf_.b])
                if nxt:
                    proj(bV, 1024, 1536, pn)
                P.dma("pool", lambda e, vf_=vf_, r0=r0: e.dma_start(out=v_own[r0:r0 + 128, :], in_=vf_[:]), reads=[vf_.b])
                if s == 1:
                    P.op("dve", lambda e, vf_=vf_: e.tensor_copy(out=vsn_b[:, :, 1:65], in_=vf_[:].rearrange("p (h d) -> p h d", h=8)), reads=[vf_.b], writes=[vsn_b.b])
                P.op("act", lambda e: e.activation(out=dt_[:], in_=bD[:, 0:324], func=AF.Identity), reads=[bD.b], writes=[dt_.b])
                if nxt:
                    proj(bD, 1536, 1860, pn)
                P.op("dve", lambda e, i=i: e.tensor_scalar(out=wis[:, i, :], in0=dt_[:, 320:324], scalar1=0.0625, scalar2=None, op0=ALU.mult), reads=[dt_.b], writes=[wis.b])
                kf_ = kif[i % 2]
                P.op("dve", lambda e: e.tensor_copy(out=kg[:, 0:64], in_=dt_[:, 256:320]), reads=[dt_.b], writes=[kg.b])
                rope4("dve", kg, kf_, 1, rpt)
                P.dma("pool", lambda e, kf_=kf_, r0=r0: e.dma_start(out=ki_own[r0:r0 + 128, :], in_=kf_[:]), reads=[kf_.b])
                if s == 1:
                    P.op("dve", lambda e, kf_=kf_: e.tensor_copy(out=kisn_b[:].rearrange("p (a d) -> p a d", a=2), in_=kf_[:, :].unsqueeze(1).broadcast_to([128, 2, 64])), reads=[kf_.b], writes=[kisn_b.b])
                P.op("dve", lambda e: e.tensor_copy(out=kg[:, 0:256], in_=dt_[:, 0:256]), reads=[dt_.b], writes=[kg.b])
                rope4("dve", kg, kr, 4, rpt)
                P.op("dve", lambda e: e.tensor_copy(out=qib[:], in_=kr[:, 0:256]), reads=[kr.b], writes=[qib.b])
                transposes_to(qib, 2, qits[:], qits.b)
                P.dma("pool", lambda e, i=i: e.dma_start(out=qit_scr[i, :, :], in_=qits[:].rearrange("p c t -> p (c t)")), reads=[qits.b], writes=[b_qit])
                gelu_p2(ug, gx)
                gelu_p2(vn, gs)
                qk_post(bQ, gq_bc, rpt, knb[:], knb.b)
                if nxt:
                    proj(bQ, 0, 512, pn)
                transposes_to(knb, 4, qts[:], qts.b)
                P.dma("pool", lambda e, i=i: e.dma_start(out=qt_scr[i, :, :], in_=qts[:].rearrange("p c t -> p (c t)")), reads=[qts.b], writes=[b_qt])
                P.op("dve", lambda e: e.bn_stats(out=bnst[:, 0:6], in_=vn[:]), reads=[vn.b], writes=[bnst.b])
                P.op("dve", lambda e: e.bn_aggr(out=bnst[:, 6:8], in_=bnst[:, 0:6]), reads=[bnst.b], writes=[bnst.b])
                P.op("act", lambda e: e.activation(out=bnst[:, 0:1], in_=bnst[:, 7:8], func=AF.Sqrt, bias=eps_t[:, 0:1], scale=1.0), reads=[bnst.b, eps_t.b], writes=[bnst.b])
                P.op("dve", lambda e: e.reciprocal(out=bnst[:, 1:2], in_=bnst[:, 0:1]), reads=[bnst.b], writes=[bnst.b])
                P.op("dve", lambda e: e.tensor_scalar(out=vn[:], in0=vn[:], scalar1=bnst[:, 6:7], scalar2=bnst[:, 1:2], op0=ALU.subtract, op1=ALU.mult), reads=[vn.b, bnst.b], writes=[vn.b])
                P.op("dve", lambda e: e.tensor_tensor(out=vn[:], in0=vn[:], in1=lng_bc[:], op=ALU.mult), reads=[vn.b, lng_bc.b], writes=[vn.b])
                P.op("dve", lambda e: e.tensor_tensor(out=vn[:], in0=vn[:], in1=lnb_bc[:], op=ALU.add), reads=[vn.b, lnb_bc.b], writes=[vn.b])
                if s == 1:
                    P.dma("pool", lambda e: e.dma_start(out=gv_own[:, :], in_=vn[:]), reads=[vn.b])
                P.op("dve", lambda e: e.tensor_copy(out=vnb[:], in_=vn[:]), reads=[vn.b], writes=[vnb.b])
                ps = PS[1]
                for g in range(4):
                    P.op("pe", lambda e, g=g, s=s, ps=ps: e.matmul(ps[:, g * 128:(g + 1) * 128], lhsT=WsT[s][:, g, :], rhs=vnb[:, g * 128:(g + 1) * 128], start=True, stop=True), reads=[WsT[s].b, vnb.b], writes=[ps.b])
                for g in range(4):
                    P.op("dve", lambda e, g=g, s=s, ps=ps: e.scalar_tensor_tensor(out=gmb[:, g * 128:(g + 1) * 128], in0=ps[:, g * 128:(g + 1) * 128], scalar=bs_t[s][:, g:g + 1], in1=ug[:, g * 128:(g + 1) * 128], op0=ALU.add, op1=ALU.mult),
                         reads=[ps.b, bs_t[s].b, ug.b], writes=[gmb.b])
                transposes_to(gmb, 4, mgT[:].rearrange("p (c t) -> p c t", c=4), mgT.b)
                P.dma("pool", lambda e, i=i: e.dma_start(out=mixg_scr[i, :, :], in_=mgT[:]), reads=[mgT.b], writes=[b_mixg])
                if i + 2 < NB1:
                    load_x(x_own, rope_own, (i + 2) * 128, i + 2)
                if precast:
                    precast.pop(0)()
            run_block()
        if cfg.stop == 3:
            mid.close()
            return nc

        with ExitStack() as st:
            Isc = sb(st, "Isc", [128, KMAX], F32)
            negm2 = [sb(st, "negm%d" % i, [128, KMAX], BF16) for i in range(2)]
            qbd = [sb(st, "qbd%d" % i, [128, 4, 256], BF16) for i in range(2)]
            sm2 = [sb(st, "sm%d" % i, [128, 8], F32) for i in range(2)]
            sa2 = [sb(st, "sa%d" % i, [128, 1], F32) for i in range(2)]
            cd2 = [sb(st, "cd%d" % i, [128, 1], F32) for i in range(2)]
            negm_b2 = [Buf("negm_b2_%d" % i) for i in range(2)]
            dl2 = [sb(st, "dl%d" % i, [128, NITER], F32) for i in range(2)]
            rr = [sb(st, "rr%d" % i, [128, 2, 512], F32) for i in range(2)]
            Vb = [sb(st, "Vb%d" % i, [128, 4, 520], BF16) for i in range(2)]
            Pb = [sb(st, "Pb%d" % i, [128, 512], BF16) for i in range(4)]
            qitb = [sb(st, "qitb%d" % i, [128, 2, 128], BF16) for i in range(2)]
            cm_p = sb(st, "cm_p", [128, 512], F32)
            cm_s = sb(st, "cm_s", [128, 128], F32)
            pw2 = sb(st, "pw2", [128, NITER], F32)
            lnd = sb(st, "lnd", [1, 1024], F32)
            rden = sb(st, "rden", [1, 1024], F32)
            bcs = sb(st, "bcs", [65, 1024], F32)
            mixa = sb(st, "mixa", [65, 8, 128], BF16)
            zl = sb(st, "zl", [128, 65], BF16)
            zr = sb(st, "zr", [128, 512], BF16)
            ckf = [sb(st, "ckf%d" % i, [128, 512], F32) for i in range(2)]
            cvf = [sb(st, "cvf%d" % i, [128, 512], F32) for i in range(2)]
            ckif = [sb(st, "ckif%d" % i, [128, 64], F32) for i in range(2)]
            ckb = sb(st, "ckb", [128, 512], BF16)
            cvb = [sb(st, "cvb%d" % i, [128, 8, 65], BF16) for i in range(2)]
            ckib = sb(st, "ckib", [128, 128], BF16)
            zv = sb(st, "zv", [128, 520], BF16)

            while precast:
                precast.pop(0)()
            P.dma("sp", lambda e: e.dma_start(out=cm_p[:], in_=cmask_p_d[:, :]), writes=[cm_p.b])
            P.dma("sp", lambda e: e.dma_start(out=cm_s[:], in_=cmask_s_d[:, :]), writes=[cm_s.b])
            P.dma("sp", lambda e: e.dma_start(out=pw2[:], in_=pow2_d[:, :]), writes=[pw2.b])
            P.op("pool", lambda e: e.memset(zl[:], 0.0), writes=[zl.b])
            P.op("pool", lambda e: e.memset(zr[:], 0.0), writes=[zr.b])
            P.op("pool", lambda e: e.memset(zv[:], 0.0), writes=[zv.b])
            for c_ in cvb:
                P.op("pool", lambda e, c_=c_: e.memset(c_[:], 1.0), writes=[c_.b])
            for q_ in qbd:
                P.op("pool", lambda e, q_=q_: e.memset(q_[:], 0.0), writes=[q_.b])
            cnts = {"l": 0, "p": 0, "sel": 0}
            LRING = [4, 5, 0, 1, 2, 3]
            LA = 5

            def idxsel(i, nkeys, topk, cm, cm_w):
                par = cnts["sel"] % 2
                cnts["sel"] += 1
                negm, qbd_, qit_, sm, dl = negm2[par], qbd[par], qitb[par], sm2[par], dl2[par]
                P.dma("sp", lambda e: e.dma_start(out=qbd_[0:64, :, 0:128], in_=qt_scr[i, 0:64, :].rearrange("p (c t) -> p c t", c=4)), reads=[b_qt], writes=[qbd_.b])
                P.dma("sp", lambda e: e.dma_start(out=qbd_[64:128, :, 128:256], in_=qt_scr[i, 64:128, :].rearrange("p (c t) -> p c t", c=4)), reads=[b_qt], writes=[qbd_.b])
                P.dma("sp", lambda e: e.dma_start(out=qit_[:].rearrange("p c t -> p (c t)"), in_=qit_scr[i, :, :]), reads=[b_qit], writes=[qit_.b])
                tiles = [(k0, min(512, nkeys - k0)) for k0 in range(0, nkeys, 512)]
                gi = 0
                for (k0, w) in tiles:
                    for grp in range(2):
                        gs = gi % 2
                        gi += 1
                        for hh in range(2):
                            ps = PS[2 * gs + hh]
                            P.op("pe", lambda e, ps=ps, hh=hh, grp=grp, k0=k0, w=w: e.matmul(ps[:, 0:w], lhsT=qit_[64 * hh:64 * hh + 64, grp, :], rhs=KIT[64 * hh:64 * hh + 64, k0:k0 + w], start=True, stop=True),
                                 reads=[qit_.b, KIT.b], writes=[ps.b])
                        r_ = rr[gs]
                        P.op("act", lambda e, gs=gs, w=w, r_=r_: e.activation(out=r_[:, :, 0:w], in_=psum[:, 2 * gs:2 * gs + 2, 0:w], func=AF.Relu),
                             reads=[PS[2 * gs].b, PS[2 * gs + 1].b], writes=[r_.b])
                        for hh in range(2):
                            h = 2 * grp + hh
                            if h == 0:
                                P.op("dve", lambda e, r_=r_, k0=k0, w=w: e.tensor_scalar(out=Isc[:, k0:k0 + w], in0=r_[:, 0, 0:w], scalar1=wis[:, i, 0:1], scalar2=None, op0=ALU.mult),
                                     reads=[r_.b, wis.b], writes=[Isc.b])
                            else:
                                P.op("dve", lambda e, r_=r_, k0=k0, w=w, hh=hh, h=h: e.scalar_tensor_tensor(out=Isc[:, k0:k0 + w], in0=r_[:, hh, 0:w], scalar=wis[:, i, h:h + 1], in1=Isc[:, k0:k0 + w], op0=ALU.mult, op1=ALU.add),
                                     reads=[r_.b, wis.b, Isc.b], writes=[Isc.b])
                P.op("dve", lambda e: e.tensor_reduce(out=sm[:, 0:1], in_=Isc[:, 0:nkeys], axis=AX.X, op=ALU.max, apply_absolute_value=True), reads=[Isc.b], writes=[sm.b])
                P.op("dve", lambda e: e.tensor_tensor(out=Isc[:, nkeys - cm_w:nkeys], in0=Isc[:, nkeys - cm_w:nkeys], in1=cm[:, 0:cm_w], op=ALU.add), reads=[Isc.b, cm.b], writes=[Isc.b])
                P.op("dve", lambda e: e.tensor_scalar(out=sm[:, 1:2], in0=sm[:, 0:1], scalar1=-1.001, scalar2=-1e-20, op0=ALU.mult, op1=ALU.add), reads=[sm.b], writes=[sm.b])
                P.op("dve", lambda e: e.tensor_scalar(out=sm[:, 2:3], in0=sm[:, 0:1], scalar1=2.003, scalar2=3e-20, op0=ALU.mult, op1=ALU.add), reads=[sm.b], writes=[sm.b])
                P.op("dve", lambda e: e.tensor_scalar(out=dl[:], in0=pw2[:], scalar1=sm[:, 2:3], scalar2=None, op0=ALU.mult), reads=[sm.b, pw2.b], writes=[dl.b])
                a_split = ((nkeys * 11 // 16) // 128) * 128
                if nkeys - a_split < 256:
                    a_split = nkeys
                stt = dict(par=par, nkeys=nkeys, topk=topk, a=a_split, m=0)
                return stt

            def bis_iter(stt):
                par, nkeys, topk, a, m = stt["par"], stt["nkeys"], stt["topk"], stt["a"], stt["m"]
                negm, sm, dl, sa, cd = negm2[par], sm2[par], dl2[par], sa2[par], cd2[par]
                P.op("dve", lambda e: e.tensor_tensor(out=cd[:, 0:1], in0=sm[:, 1:2], in1=dl[:, m:m + 1], op=ALU.add), reads=[sm.b, dl.b], writes=[cd.b])
                if a < nkeys:
                    P.op("act", lambda e: e.activation(out=negm[:, a:nkeys], in_=Isc[:, a:nkeys], func=AF.Sign, scale=-1.0, bias=cd[:, 0:1], accum_out=sa[:, 0:1]),
                         reads=[Isc.b, cd.b], writes=[negm_b2[par], sa.b])
                P.op("dve", lambda e: e.tensor_scalar(out=negm[:, 0:a], in0=Isc[:, 0:a], scalar1=cd[:, 0:1], scalar2=None, op0=ALU.is_ge, op1=ALU.add, accum_out=sm[:, 4:5]),
                     reads=[Isc.b, cd.b], writes=[negm.b, sm.b])
                if a < nkeys:
                    L = nkeys - a
                    P.op("dve", lambda e: e.scalar_tensor_tensor(out=sm[:, 6:7], in0=sa[:, 0:1], scalar=-0.5, in1=sm[:, 4:5], op0=ALU.mult, op1=ALU.add), reads=[sa.b, sm.b], writes=[sm.b])
                    P.op("dve", lambda e: e.tensor_scalar(out=sm[:, 5:6], in0=sm[:, 6:7], scalar1=float(topk) - 0.5 - 0.5 * L, scalar2=None, op0=ALU.is_ge), reads=[sm.b], writes=[sm.b])
                else:
                    P.op("dve", lambda e: e.tensor_scalar(out=sm[:, 5:6], in0=sm[:, 4:5], scalar1=float(topk) - 0.5, scalar2=None, op0=ALU.is_ge), reads=[sm.b], writes=[sm.b])
                P.op("dve", lambda e: e.scalar_tensor_tensor(out=sm[:, 1:2], in0=sm[:, 5:6], scalar=dl[:, m:m + 1], in1=sm[:, 1:2], op0=ALU.mult, op1=ALU.add), reads=[sm.b, dl.b], writes=[sm.b])
                stt["m"] += 1

            def idxsel_end(stt):
                while stt["m"] < NITER:
                    bis_iter(stt)
                par, nkeys = stt["par"], stt["nkeys"]
                negm, sm = negm2[par], sm2[par]
                P.op("dve", lambda e: e.tensor_scalar(out=negm[:, 0:nkeys], in0=Isc[:, 0:nkeys], scalar1=sm[:, 1:2], scalar2=MNEG, op0=ALU.is_lt, op1=ALU.mult), reads=[Isc.b, sm.b], writes=[negm.b, negm_b2[par]])
                return par


            def attend(i, nkeys, par, col_sel, pending=None):
                NK = nkeys // 128
                negm, qbd_ = negm2[par], qbd[par]
                negm_rb = [negm.b, negm_b2[par]]
                for hq in range(2):
                    ps = PS[6 + hq]
                    P.op("pe", lambda e, ps=ps: e.matmul(ps[0:65, :], lhsT=zl[:, 0:65], rhs=zr[:, :], start=True, stop=False, skip_group_check=True), reads=[zl.b, zr.b], writes=[ps.b])
                steps = []
                for kt4 in range((NK + 3) // 4):
                    nk_here = min(4, NK - 4 * kt4)
                    for kk in range(nk_here):
                        for hq in range(2):
                            steps.append((kt4, nk_here, kk, hq, kk == 0 and hq == 0))
                state = {}

                def emit_qk(n):
                    kt4, nk_here, kk, hq, first = steps[n]
                    vb_ = Vb[kt4 % 2]
                    if first:
                        P.dma("sp", lambda e: e.dma_start(out=vb_[:, 0:nk_here, :], in_=v_scr[kt4 * 512:kt4 * 512 + nk_here * 128, :].rearrange("(k p) c -> p k c", p=128)),
                              reads=[b_vscr], writes=[vb_.b])
                    t128 = kt4 * 4 + kk
                    psL = PS[LRING[cnts["l"] % len(LRING)]]
                    cnts["l"] += 1
                    P.op("pe", lambda e: e.matmul(psL[:, 0:512], lhsT=negm[:, t128 * 128:(t128 + 1) * 128], rhs=ident4[:, :], start=True, stop=False),
                         reads=negm_rb + [ident4.b], writes=[psL.b])
                    for pp in range(2):
                        pair = 2 * hq + pp
                        P.op("pe", lambda e, pp=pp, pair=pair: e.matmul(psL[:, pp * 256:(pp + 1) * 256], lhsT=KT[:, pair, t128 * 128:(t128 + 1) * 128], rhs=qbd_[:, pair, :], start=False, stop=(pp == 1)),
                             reads=[KT.b, qbd_.b], writes=[psL.b])
                    state[n] = psL

                def emit_pv(n):
                    kt4, nk_here, kk, hq, first = steps[n]
                    vb_ = Vb[kt4 % 2]
                    t128 = kt4 * 4 + kk
                    psL = state.pop(n)
                    pb_ = Pb[cnts["p"] % len(Pb)]
                    cnts["p"] += 1
                    P.op("act", lambda e: e.activation(out=pb_[:], in_=psL[:, :], func=AF.Exp, scale=0.125), reads=[psL.b], writes=[pb_.b])
                    pso = PS[6 + hq]
                    for h4 in range(4):
                        h = hq * 4 + h4
                        P.op("pe", lambda e, h4=h4, h=h: e.matmul(pso[0:65, h4 * 128:(h4 + 1) * 128], lhsT=vb_[:, kk, h * 65:(h + 1) * 65], rhs=pb_[:, h4 * 128:(h4 + 1) * 128], start=False, stop=(t128 == NK - 1), skip_group_check=True),
                             reads=[vb_.b, pb_.b], writes=[pso.b])

                stride = max(1, len(steps) // NITER)
                for n0 in range(min(LA, len(steps))):
                    emit_qk(n0)
                for n in range(len(steps)):
                    if n + LA < len(steps):
                        emit_qk(n + LA)
                    emit_pv(n)
                    if pending is not None and n % stride == stride - 1 and pending["m"] < NITER:
                        bis_iter(pending)
                if pending is not None:
                    idxsel_end(pending)
                rb = [PS[6].b, PS[7].b]
                P.op("act", lambda e: e.activation(out=lnd[0:1, :], in_=psum[0:1, 6:8, :].rearrange("p a b -> p (a b)"), func=AF.Ln), reads=rb, writes=[lnd.b])
                P.op("act", lambda e: e.activation(out=rden[0:1, :], in_=lnd[0:1, :], func=AF.Exp, scale=-1.0), reads=[lnd.b], writes=[rden.b])
                for half in range(2):
                    ps = PS[half]
                    P.op("pe", lambda e, ps=ps, half=half: e.matmul(ps[0:65, :], lhsT=ones_f[0:1, 0:65], rhs=rden[0:1, half * 512:(half + 1) * 512], start=True, stop=True), reads=[ones_f.b, rden.b], writes=[ps.b])
                P.op("act", lambda e: e.activation(out=bcs[0:65, :], in_=psum[0:65, 0:2, :].rearrange("p a b -> p (a b)"), func=AF.Identity), reads=[PS[0].b, PS[1].b], writes=[bcs.b])
                P.op("dve", lambda e: e.tensor_tensor(out=mixa[0:65, :, :].rearrange("p h t -> p (h t)"), in0=psum[0:65, 6:8, :].rearrange("p a b -> p (a b)"), in1=bcs[0:65, :], op=ALU.mult), reads=rb + [bcs.b], writes=[mixa.b])
                if col_sel is None:
                    P.dma("pool", lambda e: e.dma_start(out=mixa_scr[i, :, :], in_=mixa[0:65, :, :].rearrange("p h t -> p (h t)")), reads=[mixa.b], writes=[b_mixa])
                else:
                    c0 = col_sel
                    P.dma("pool", lambda e: e.dma_start(out=mixa_scr[i, :, :].rearrange("p (h t) -> p h t", h=8)[:, :, c0:c0 + 32], in_=mixa[0:65, :, c0:c0 + 32]), reads=[mixa.b], writes=[b_mixa])

            if NI > 0:
                st_cur = idxsel(0, 512, cfg.TOPK_P, cm_p, 512)
                idxsel_end(st_cur)
            for i in range(NI):
                st_next = None
                if i + 1 < NI:
                    st_next = idxsel(i + 1, 512 * (i + 2), cfg.TOPK_P, cm_p, 512)
                attend(i, 512 * (i + 1), st_cur["par"], None, pending=st_next)
                st_cur = st_next

            for bq in range(4):
                for kt in range(PAST // 128):
                    f_, v_, ki_ = ckf[kt % 2], cvf[kt % 2], ckif[kt % 2]
                    rows = slice(kt * 128, (kt + 1) * 128)
                    P.dma("sp", lambda e, f_=f_, rows=rows, bq=bq: e.dma_start(out=f_[:], in_=cache_k[bq, rows, :]), writes=[f_.b])
                    P.dma("sp", lambda e, v_=v_, rows=rows, bq=bq: e.dma_start(out=v_[:], in_=cache_v[bq, rows, :]), writes=[v_.b])
                    P.dma("sp", lambda e, ki_=ki_, rows=rows, bq=bq: e.dma_start(out=ki_[:], in_=cache_ki[bq, rows, :]), writes=[ki_.b])
                    P.op("dve", lambda e, f_=f_: e.tensor_copy(out=ckb[:], in_=f_[:]), reads=[f_.b], writes=[ckb.b])
                    ps = PS[0]
                    pb = ps[:, :].bitcast(BF16)
                    for c in range(4):
                        P.op("pe", lambda e, c=c, pb=pb: e.transpose(out=pb[:, c * 128:(c + 1) * 128], in_=ckb[:, c * 128:(c + 1) * 128], identity=identb[:]), reads=[ckb.b, identb.b], writes=[ps.b])
                    P.op("act", lambda e, pb=pb, kt=kt: e.activation(out=KT[:, :, kt * 128:(kt + 1) * 128], in_=pb[:, 0:512].rearrange("p (c t) -> p c t", c=4), func=AF.Identity), reads=[ps.b], writes=[KT.b])
                    cv_ = cvb[kt % 2]
                    P.op("dve", lambda e, cv_=cv_, v_=v_: e.tensor_copy(out=cv_[:, :, 1:65], in_=v_[:].rearrange("p (h d) -> p h d", h=8)), reads=[v_.b], writes=[cv_.b])
                    P.dma("pool", lambda e, cv_=cv_, rows=rows: e.dma_start(out=v_scr[rows, :], in_=cv_[:].rearrange("p h d -> p (h d)")), reads=[cv_.b], writes=[b_vscr])
                    P.op("dve", lambda e, ki_=ki_: e.tensor_copy(out=ckib[:].rearrange("p (a d) -> p a d", a=2), in_=ki_[:, :].unsqueeze(1).broadcast_to([128, 2, 64])), reads=[ki_.b], writes=[ckib.b])
                    ps = PS[1]
                    pb = ps[:, :].bitcast(BF16)
                    P.op("pe", lambda e, pb=pb: e.transpose(out=pb[:, 0:128], in_=ckib[:], identity=identb[:]), reads=[ckib.b, identb.b], writes=[ps.b])
                    P.op("act", lambda e, pb=pb, kt=kt: e.activation(out=KIT[:, kt * 128:(kt + 1) * 128], in_=pb[:, 0:128], func=AF.Identity), reads=[ps.b], writes=[KIT.b])
                P.op("pool", lambda e: e.memset(KT[:, :, PAST:PAST + 128], 0.0), writes=[KT.b])
                P.op("pool", lambda e: e.memset(KIT[:, PAST:PAST + 128], 0.0), writes=[KIT.b])
                ps = PS[0]
                pb = ps[:, :].bitcast(BF16)
                for c in range(4):
                    P.op("pe", lambda e, c=c, pb=pb: e.transpose(out=pb[:, c * 128:(c + 1) * 128], in_=ksn_b[:, c * 128:(c + 1) * 128], identity=identb[:]), reads=[ksn_b.b, identb.b], writes=[ps.b])
                P.op("act", lambda e, pb=pb, bq=bq: e.activation(out=KT[:, :, PAST:PAST + 32], in_=pb[:, 0:512].rearrange("p (c t) -> p c t", c=4)[:, :, 32 * bq:32 * bq + 32], func=AF.Identity), reads=[ps.b], writes=[KT.b])
                ps = PS[1]
                pb = ps[:, :].bitcast(BF16)
                P.op("pe", lambda e, pb=pb: e.transpose(out=pb[:, 0:128], in_=kisn_b[:], identity=identb[:]), reads=[kisn_b.b, identb.b], writes=[ps.b])
                P.op("act", lambda e, pb=pb, bq=bq: e.activation(out=KIT[:, PAST:PAST + 32], in_=pb[:, 32 * bq:32 * bq + 32], func=AF.Identity), reads=[ps.b], writes=[KIT.b])
                P.dma("pool", lambda e, bq=bq: e.dma_start(out=v_scr[PAST:PAST + 32, :], in_=vsn_b[32 * bq:32 * bq + 32, :, :].rearrange("p h d -> p (h d)")), reads=[vsn_b.b], writes=[b_vscr])
                P.dma("pool", lambda e: e.dma_start(out=v_scr[PAST + 32:PAST + 128, :], in_=zv[0:96, :]), reads=[zv.b], writes=[b_vscr])
                st_s = idxsel(NI, SPAD, cfg.TOPK_S, cm_s, 128)
                idxsel_end(st_s)
                attend(NI, SPAD, st_s["par"], 32 * bq)
            run_block()
        mid.close()
        if cfg.stop == 6:
            return nc

        with ExitStack() as st:
            woa = sb(st, "woa", [65, 8, 1024], BF16)
            wog = sb(st, "wog", [128, 4, 1024], BF16)
            wf1 = sb(st, "wf1", [128, 8, 4096], BF16)
            wf2 = sb(st, "wf2", [128, 32, 1024], BF16)
            bcD = [sb(st, "bcD%d" % k, [128, D], F32) for k in range(4)]
            xd = sb(st, "xd", [128, D], F32)
            x1 = sb(st, "x1", [128, D], F32)
            tmpf = sb(st, "tmpfD", [128, D], F32)
            hb = sb(st, "hbD", [128, D], BF16)
            hT = sb(st, "hTD", [128, 8, 128], BF16)
            aT = sb(st, "aT", [128, 32, 128], BF16)
            rl2 = [sb(st, "rl%d" % i, [128, 512], F32) for i in range(2)]
            ab = [sb(st, "ab%d" % i, [128, 512], BF16) for i in range(2)]
            mxa = sb(st, "mxa", [65, 8, 128], BF16)
            mxg = sb(st, "mxg", [128, 4, 128], BF16)
            st1 = sb(st, "st1D", [128, 4], F32)

            P.op("pool", lambda e: e.memset(woa[0:1, :, :], 0.0), writes=[woa.b])
            for h in range(8):
                P.dma("sp", lambda e, h=h: e.dma_start(out=woa[1:65, h, :], in_=wo_b[h * 64:(h + 1) * 64, :]), reads=[b_wob], writes=[woa.b])
            P.dma("sp", lambda e: e.dma_start(out=wog[:], in_=wo_b[512:1024, :].rearrange("(g p) n -> p g n", p=128)), reads=[b_wob], writes=[wog.b])
            for c in range(8):
                P.dma("sp", lambda e, c=c: e.dma_start(out=wf1[:, c, :], in_=wf1_b[c * 128:(c + 1) * 128, :]), reads=[b_wf1b], writes=[wf1.b])
            for f4 in range(8):
                P.dma("sp", lambda e, f4=f4: e.dma_start(out=wf2[:, f4 * 4:(f4 + 1) * 4, :], in_=wf2_b[f4 * 512:(f4 + 1) * 512, :].rearrange("(f p) n -> p f n", p=128)), reads=[b_wf2b], writes=[wf2.b])

            def loadD(i):
                r0 = i * 128
                P.dma("sp", lambda e: e.dma_start(out=xd[:], in_=x_own[r0:r0 + 128, :]), writes=[xd.b])
                P.dma("sp", lambda e: e.dma_start(out=mxa[0:65, :, :].rearrange("p h t -> p (h t)"), in_=mixa_scr[i, :, :]), reads=[b_mixa], writes=[mxa.b])
                P.dma("sp", lambda e: e.dma_start(out=mxg[:].rearrange("p c t -> p (c t)"), in_=mixg_scr[i, :, :]), reads=[b_mixg], writes=[mxg.b])

            for i in range(NB1):
                s = 0 if i < NI else 1
                if i == 0 or i == NI:
                    for k in range(4):
                        P.dma("sp", lambda e, k=k, s=s: e.dma_start(out=bcD[k][:], in_=bc_scr[s, k, :, :]), reads=[b_bcscr], writes=[bcD[k].b])
                r0 = i * 128
                if i == 0:
                    loadD(0)
                for nt in range(2):
                    ps = PS[nt]
                    for h in range(8):
                        P.op("pe", lambda e, ps=ps, h=h, nt=nt: e.matmul(ps[:, :], lhsT=mxa[0:65, h, :], rhs=woa[0:65, h, nt * 512:(nt + 1) * 512], start=(h == 0), stop=False), reads=[mxa.b, woa.b], writes=[ps.b])
                    for g in range(4):
                        P.op("pe", lambda e, ps=ps, g=g, nt=nt: e.matmul(ps[:, :], lhsT=mxg[:, g, :], rhs=wog[:, g, nt * 512:(nt + 1) * 512], start=False, stop=(g == 3)), reads=[mxg.b, wog.b], writes=[ps.b])
                    P.op("dve", lambda e, ps=ps, nt=nt: e.tensor_tensor(out=tmpf[:, nt * 512:(nt + 1) * 512], in0=ps[:, :], in1=bcD[0][:, nt * 512:(nt + 1) * 512], op=ALU.mult), reads=[ps.b, bcD[0].b], writes=[tmpf.b])
                P.op("dve", lambda e: e.tensor_tensor(out=x1[:], in0=tmpf[:], in1=xd[:], op=ALU.add), reads=[tmpf.b, xd.b], writes=[x1.b])
                if i + 1 < NB1:
                    loadD(i + 1)
                P.op("act", lambda e: e.activation(out=hb[:], in_=x1[:], func=AF.Square, accum_out=st1[:, 0:1]), reads=[x1.b], writes=[hb.b, st1.b])
                P.op("act", lambda e: e.activation(out=st1[:, 1:2], in_=st1[:, 0:1], func=AF.Sqrt, bias=eps_t[:, 0:1], scale=1.0 / D), reads=[st1.b, eps_t.b], writes=[st1.b])
                P.op("dve", lambda e: e.reciprocal(out=st1[:, 2:3], in_=st1[:, 1:2]), reads=[st1.b], writes=[st1.b])
                P.op("dve", lambda e: e.scalar_tensor_tensor(out=tmpf[:], in0=x1[:], scalar=st1[:, 2:3], in1=bcD[1][:], op0=ALU.mult, op1=ALU.mult), reads=[x1.b, st1.b, bcD[1].b], writes=[tmpf.b])
                P.op("dve", lambda e: e.tensor_tensor(out=hb[:], in0=tmpf[:], in1=bcD[2][:], op=ALU.add), reads=[tmpf.b, bcD[2].b], writes=[hb.b])
                ps = PS[2]
                pb = ps[:, :].bitcast(BF16)
                for c in range(8):
                    P.op("pe", lambda e, c=c, pb=pb: e.transpose(out=pb[:, c * 128:(c + 1) * 128], in_=hb[:, c * 128:(c + 1) * 128], identity=identb[:]), reads=[hb.b, identb.b], writes=[ps.b])
                P.op("act", lambda e, pb=pb: e.activation(out=hT[:].rearrange("p c t -> p (c t)"), in_=pb, func=AF.Identity), reads=[ps.b], writes=[hT.b])
                def ff1_mm(nt8):
                    ps = PS[3 + nt8 % 2]
                    for c in range(8):
                        P.op("pe", lambda e, ps=ps, c=c: e.matmul(ps[:, :], lhsT=hT[:, c, :], rhs=wf1[:, c, nt8 * 512:(nt8 + 1) * 512], start=(c == 0), stop=(c == 7)), reads=[hT.b, wf1.b], writes=[ps.b])

                def ff1_post(nt8):
                    ps = PS[3 + nt8 % 2]
                    rl_ = rl2[nt8 % 2]
                    P.op("act", lambda e: e.activation(out=rl_[:], in_=ps[:, :], func=AF.Relu), reads=[ps.b], writes=[rl_.b])
                    ab_ = ab[nt8 % 2]
                    P.op("dve", lambda e: e.tensor_tensor(out=ab_[:], in0=rl_[:], in1=rl_[:], op=ALU.mult), reads=[rl_.b], writes=[ab_.b])
                    pt = PS[7]
                    ptb = pt[:, :].bitcast(BF16)
                    for fc in range(4):
                        P.op("pe", lambda e, fc=fc: e.transpose(out=ptb[:, fc * 128:(fc + 1) * 128], in_=ab_[:, fc * 128:(fc + 1) * 128], identity=identb[:]), reads=[ab_.b, identb.b], writes=[pt.b])
                    P.op("act", lambda e: e.activation(out=aT[:, nt8 * 4:(nt8 + 1) * 4, :].rearrange("p c t -> p (c t)"), in_=ptb[:, 0:512], func=AF.Identity), reads=[pt.b], writes=[aT.b])

                ff1_mm(0)
                for nt8 in range(8):
                    if nt8 + 1 < 8:
                        ff1_mm(nt8 + 1)
                    ff1_post(nt8)
                for nt in range(2):
                    ps = PS[5 + nt]
                    for f in range(32):
                        P.op("pe", lambda e, ps=ps, f=f, nt=nt: e.matmul(ps[:, :], lhsT=aT[:, f, :], rhs=wf2[:, f, nt * 512:(nt + 1) * 512], start=(f == 0), stop=(f == 31)), reads=[aT.b, wf2.b], writes=[ps.b])
                    P.op("dve", lambda e, ps=ps, nt=nt: e.tensor_tensor(out=tmpf[:, nt * 512:(nt + 1) * 512], in0=ps[:, :], in1=bcD[3][:, nt * 512:(nt + 1) * 512], op=ALU.mult), reads=[ps.b, bcD[3].b], writes=[tmpf.b])
                P.op("dve", lambda e: e.tensor_tensor(out=tmpf[:], in0=tmpf[:], in1=x1[:], op=ALU.add), reads=[tmpf.b, x1.b], writes=[tmpf.b])
                P.dma("pool", lambda e, r0=r0: e.dma_start(out=y_own[r0:r0 + 128, :], in_=tmpf[:]), reads=[tmpf.b])
            run_block()

        return nc


def _rope_table(pos):
    half = 32
    inv = np.power(np.float32(10000.0), -np.arange(half, dtype=np.float32) / np.float32(half)).astype(np.float32)
    ang = pos.astype(np.float32)[:, None] * inv[None, :]
    return np.concatenate([np.cos(ang), np.sin(ang)], axis=1).astype(np.float32)


def make_in_maps(cfg, inp):
    SEQ, PAST, NI, NB1 = cfg.SEQ, cfg.PAST, cfg.NI, cfg.NB1
    f = lambda a: np.ascontiguousarray(np.asarray(a, dtype=np.float32))
    xp, xs = f(inp["x_prompt"]), f(inp["x_sample"])
    cp, cs = f(inp["c_prompt"]), f(inp["c_sample"])
    ck, cv, cki = f(inp["cache_k"])[0], f(inp["cache_v"])[0], f(inp["cache_kidx"])[0]
    shared = {
        "w_ada": f(inp["w_ada"])[0], "b_ada": f(inp["b_ada"]), "norm1_g": f(inp["norm1_g"]), "norm2_g": f(inp["norm2_g"]),
        "w_in": f(inp["w_in"])[0], "q_norm_g": f(inp["q_norm_g"]), "k_norm_g": f(inp["k_norm_g"]),
        "gmlp_ln_g": f(inp["gmlp_ln_g"]), "gmlp_ln_b": f(inp["gmlp_ln_b"]), "gmlp_ws": f(inp["gmlp_ws"])[0],
        "gmlp_bs": f(inp["gmlp_bs"])[0], "w_out": f(inp["w_out"])[0], "w_ff1": f(inp["w_ff1"])[0], "w_ff2": f(inp["w_ff2"])[0],
        "ident": np.eye(128, dtype=np.float32),
        "pow2": np.tile((2.0 ** -(np.arange(cfg.NITER) + 1.0)).astype(np.float32)[None, :], (128, 1)),
    }
    sel = np.zeros((2, 5, 128), np.float32)
    sel[0, 0, :] = 1.0
    for p in range(128):
        sel[1, 1 + p // 32, p] = 1.0
    shared["sel"] = sel
    tril = np.zeros((2, 128, 128), np.float32)
    tril[0] = np.tril(np.ones((128, 128), np.float32))
    for q in range(4):
        tril[1, q * 32:(q + 1) * 32, q * 32:(q + 1) * 32] = np.tril(np.ones((32, 32), np.float32))
    shared["tril"] = tril
    cm_s = np.full((128, 128), NEG, np.float32)
    cm_s[:, 0:32] = 0.0
    shared["cmask_s"] = cm_s
    rope_seq = _rope_table(np.arange(SEQ))
    maps = []
    for c in range(8):
        b, j = c // 4, c % 4
        blks = [4 * i + j for i in range(NI)]
        rows = np.concatenate([np.arange(bl * 128, (bl + 1) * 128) for bl in blks])
        x_own = np.concatenate([xp[b][rows], xs[4 * c:4 * c + 4].reshape(128, D)], axis=0)
        pos_own = np.concatenate([rows, PAST + (np.arange(128) % 32)])
        tl = 128 * j + np.arange(128)
        lim = (tl // 64 + 1) * 64
        cm = np.where(np.arange(512)[None, :] < lim[:, None], 0.0, NEG).astype(np.float32)
        m = dict(shared)
        m.update({
            "x_seq": xp[b], "x_own": np.ascontiguousarray(x_own), "c_all": np.ascontiguousarray(np.concatenate([cp[b:b + 1], cs[4 * c:4 * c + 4]], axis=0)),
            "rope_seq": rope_seq, "rope_own": _rope_table(pos_own),
            "cache_k": np.ascontiguousarray(ck[4 * c:4 * c + 4].reshape(4, PAST, 512)),
            "cache_v": np.ascontiguousarray(cv[4 * c:4 * c + 4].reshape(4, PAST, 512)),
            "cache_ki": np.ascontiguousarray(cki[4 * c:4 * c + 4]),
            "cmask_p": cm,
        })
        maps.append(m)
    return maps


def assemble(cfg, results):
    SEQ, NI = cfg.SEQ, cfg.NI
    yp = np.zeros((2, SEQ, D), np.float32)
    ys = np.zeros((32, 32, D), np.float32)
    kp = np.zeros((1, 2, SEQ, 8, 64), np.float32)
    vp = np.zeros((1, 2, SEQ, 8, 64), np.float32)
    kip = np.zeros((1, 2, SEQ, 64), np.float32)
    ks = np.zeros((1, 32, 32, 8, 64), np.float32)
    vs = np.zeros((1, 32, 32, 8, 64), np.float32)
    kis = np.zeros((1, 32, 32, 64), np.float32)
    gvs = np.zeros((1, 32, 32, 512), np.float32)
    for c in range(8):
        r = results[c]
        b, j = c // 4, c % 4
        for i in range(NI):
            bl = 4 * i + j
            sl = slice(bl * 128, (bl + 1) * 128)
            o = slice(i * 128, (i + 1) * 128)
            yp[b, sl] = r["y_own"][o]
            kp[0, b, sl] = r["k_own"][o].reshape(128, 8, 64)
            vp[0, b, sl] = r["v_own"][o].reshape(128, 8, 64)
            kip[0, b, sl] = r["ki_own"][o]
        o = slice(NI * 128, (NI + 1) * 128)
        ys[4 * c:4 * c + 4] = r["y_own"][o].reshape(4, 32, D)
        ks[0, 4 * c:4 * c + 4] = r["k_own"][o].reshape(4, 32, 8, 64)
        vs[0, 4 * c:4 * c + 4] = r["v_own"][o].reshape(4, 32, 8, 64)
        kis[0, 4 * c:4 * c + 4] = r["ki_own"][o].reshape(4, 32, 64)
        gvs[0, 4 * c:4 * c + 4] = r["gv_own"].reshape(4, 32, 512)
    return (yp, ys, kp, vp, kip, ks, vs, kis, gvs)


_CACHE = {}


def kernel(**inputs):
    cfg = Cfg(SEQ=int(np.asarray(inputs["x_prompt"]).shape[1]), PAST=int(np.asarray(inputs["cache_k"]).shape[2]))
    import os
    cfg.stop = int(os.environ.get("KSTOP", "99"))
    cfg.sub = int(os.environ.get("KSUB", "99"))
    key = (cfg.SEQ, cfg.PAST)
    if key not in _CACHE:
        _CACHE[key] = build(cfg)
    nc = _CACHE[key]
    maps = make_in_maps(cfg, inputs)
    res = run_bass_kernel_spmd(nc, maps, core_ids=list(range(8)))
    return assemble(cfg, res.results)
```

```python
import numpy as np
import concourse.bass as bass
import concourse.mybir as mybir
from concourse.bass_utils import run_bass_kernel_spmd

F32 = mybir.dt.float32
BF16 = mybir.dt.bfloat16
U32 = mybir.dt.uint32
AF = mybir.ActivationFunctionType
ALU = mybir.AluOpType
AX = mybir.AxisListType


class Buf:
    __slots__ = ("name", "w", "r", "psum")

    def __init__(self, name):
        self.name = name
        self.w = None
        self.r = []
        self.psum = False


class Prog:
    ENGS = ("pe", "act", "dve", "pool", "sp")
    NDMA = 8

    def __init__(self, nc, stack):
        self.nc = nc
        self.ops = {e: [] for e in self.ENGS}
        self.cnt = {e: 0 for e in self.ENGS}
        self.sem = {}
        for e in self.ENGS:
            self.sem[e] = stack.enter_context(nc.semaphore("s_" + e))
        self.dsem = {}
        self.dcnt = {}
        for q in ("sp", "pool", "act"):
            self.dsem[q] = [stack.enter_context(nc.semaphore("d_%s%d" % (q, i))) for i in range(self.NDMA)]
            self.dcnt[q] = 0
        self.seen = {e: {} for e in self.ENGS}
        self.out_waits = []

    def _semobj(self, key):
        if isinstance(key, str):
            return self.sem[key]
        q, i = key
        return self.dsem[q][i]

    def _need(self, eng, waits, key, val):
        if self.seen[eng].get(key, 0) >= val:
            return
        self.seen[eng][key] = val
        waits.append((key, val))

    def _deps(self, eng, reads, writes, waits, same_engine_raw=True, strict=False):
        for b in reads:
            if b.w is not None:
                k, v = b.w
                if k != eng or same_engine_raw or strict:
                    self._need(eng, waits, k, v)
            if b.psum:
                for (k, v) in b.r:
                    if k != eng:
                        self._need(eng, waits, k, v)
        for b in writes:
            if b.w is not None:
                k, v = b.w
                if k != eng or strict:
                    self._need(eng, waits, k, v)
            for (k, v) in b.r:
                if k != eng or strict:
                    self._need(eng, waits, k, v)

    def op(self, eng, fn, reads=(), writes=(), raw_same=True):
        waits = []
        self._deps(eng, reads, writes, waits, strict=(eng != "pe"))
        self.cnt[eng] += 1
        v = self.cnt[eng]
        self.ops[eng].append((waits, fn, (eng, v)))
        for b in reads:
            b.r.append((eng, v))
        for b in writes:
            b.w = (eng, v)
            b.r = []
        return v

    def dma(self, q, fn, reads=(), writes=()):
        waits = []
        self._deps(q, reads, writes, waits, strict=True)
        n = self.dcnt[q]
        self.dcnt[q] += 1
        slot = n % self.NDMA
        key = (q, slot)
        prev = 16 * (n // self.NDMA)
        if prev > 0:
            self._need(q, waits, key, prev)
        val = prev + 16
        self.ops[q].append((waits, fn, (key, 16)))
        for b in reads:
            b.r.append((key, val))
        for b in writes:
            b.w = (key, val)
            b.r = []
        return (key, val)

    def barrier(self):
        targets = [(e, self.cnt[e]) for e in self.ENGS if self.cnt[e] > 0]
        for q in self.dcnt:
            n = self.dcnt[q]
            for slot in range(min(n, self.NDMA)):
                uses = (n - slot + self.NDMA - 1) // self.NDMA
                targets.append(((q, slot), 16 * uses))
        for e in self.ENGS:
            waits = []
            for (k, v) in targets:
                if k != e:
                    self._need(e, waits, k, v)
            if waits:
                self.ops[e].append((waits, None, None))

    def emit(self, block):
        nc = self.nc
        prog = self

        def run(engine, name):
            for (waits, fn, inc) in prog.ops[name]:
                for (k, v) in waits:
                    engine.wait_ge(prog._semobj(k), v)
                if fn is None:
                    continue
                ins = fn(engine)
                key, amt = inc
                if isinstance(key, str):
                    ins.then_inc(prog.sem[key], 1)
                else:
                    ins.then_inc(prog._semobj(key), 16)
            prog.ops[name] = []

        @block.tensor
        def _(t):
            run(t, "pe")

        @block.scalar
        def _(s):
            run(s, "act")

        @block.vector
        def _(v):
            run(v, "dve")

        @block.gpsimd
        def _(g):
            run(g, "pool")

        @block.sync
        def _(s):
            run(s, "sp")


from contextlib import ExitStack

D = 1024
INW = 2884
EPS = 1e-6
NEG = -1.0e30
MNEG = -30000.0


class T:
    def __init__(self, t, name):
        self.t = t
        self.b = Buf(name)

    def __getitem__(self, k):
        return self.t[k]


class Cfg:
    def __init__(self, SEQ=8192, PAST=2048, NITER=14):
        self.SEQ = SEQ
        self.PAST = PAST
        self.NITER = NITER
        self.NBLK = SEQ // 128
        self.NI = self.NBLK // 4
        self.NB1 = self.NI + 1
        self.TOPK_P = min(256, SEQ // 4)
        self.TOPK_S = min(256, (PAST + 32) // 4)
        self.NKS = PAST // 128 + 1
        self.SPAD = self.NKS * 128
        self.KMAX = max(SEQ, self.SPAD)
        self.stop = 99
        self.sub = 99


def build(cfg):
    SEQ, PAST, NITER = cfg.SEQ, cfg.PAST, cfg.NITER
    NBLK, NI, NB1 = cfg.NBLK, cfg.NI, cfg.NB1
    NKS, SPAD, KMAX = cfg.NKS, cfg.SPAD, cfg.KMAX
    nc = bass.Bass("TRN2", target_bir_lowering=False)

    def din(name, shape, dt=F32):
        return nc.dram_tensor(name, list(shape), dt, kind="ExternalInput").ap()

    def dout(name, shape, dt=F32):
        return nc.dram_tensor(name, list(shape), dt, kind="ExternalOutput").ap()

    x_seq = din("x_seq", [SEQ, D])
    x_own = din("x_own", [NB1 * 128, D])
    c_all = din("c_all", [5, D])
    rope_seq = din("rope_seq", [SEQ, 64])
    rope_own = din("rope_own", [NB1 * 128, 64])
    cache_k = din("cache_k", [4, PAST, 512])
    cache_v = din("cache_v", [4, PAST, 512])
    cache_ki = din("cache_ki", [4, PAST, 64])
    w_ada = din("w_ada", [D, 6 * D])
    b_ada = din("b_ada", [1, 6 * D])
    norm1_g = din("norm1_g", [1, D])
    norm2_g = din("norm2_g", [1, D])
    w_in = din("w_in", [D, INW])
    q_norm_g = din("q_norm_g", [1, 64])
    k_norm_g = din("k_norm_g", [1, 64])
    ln_g = din("gmlp_ln_g", [1, 512])
    ln_b = din("gmlp_ln_b", [1, 512])
    gws = din("gmlp_ws", [4, 128, 128])
    gbs = din("gmlp_bs", [4, 128])
    w_out = din("w_out", [1024, 1024])
    w_ff1 = din("w_ff1", [1024, 4096])
    w_ff2 = din("w_ff2", [4096, 1024])
    ident_d = din("ident", [128, 128])
    sel_d = din("sel", [2, 5, 128])
    cmask_p_d = din("cmask_p", [128, 512])
    cmask_s_d = din("cmask_s", [128, 128])
    tril_d = din("tril", [2, 128, 128])
    pow2_d = din("pow2", [128, NITER])

    y_own = dout("y_own", [NB1 * 128, D])
    k_own = dout("k_own", [NB1 * 128, 512])
    v_own = dout("v_own", [NB1 * 128, 512])
    ki_own = dout("ki_own", [NB1 * 128, 64])
    gv_own = dout("gv_own", [128, 512])

    v_scr = nc.dram_tensor("v_scr", [KMAX, 520], BF16).ap()
    mixa_scr = nc.dram_tensor("mixa_scr", [NB1, 65, 1024], BF16).ap()
    mixg_scr = nc.dram_tensor("mixg_scr", [NB1, 128, 512], BF16).ap()
    bc_scr = nc.dram_tensor("bc_scr", [2, 6, 128, 1024], F32).ap()
    win_b = nc.dram_tensor("win_b", [1024, INW], BF16).ap()
    b_winb = Buf("win_b")
    wf1_b = nc.dram_tensor("wf1_b", [1024, 4096], BF16).ap()
    wf2_b = nc.dram_tensor("wf2_b", [4096, 1024], BF16).ap()
    wo_b = nc.dram_tensor("wo_b", [1024, 1024], BF16).ap()
    b_wf1b = Buf("wf1_b")
    b_wf2b = Buf("wf2_b")
    b_wob = Buf("wo_b")
    b_vscr = Buf("v_scr")
    b_mixa = Buf("mixa_scr")
    b_mixg = Buf("mixg_scr")
    b_bcscr = Buf("bc_scr")

    with ExitStack() as outer:
        P = Prog(nc, outer)

        def sb(st, name, shape, dt=F32):
            return T(st.enter_context(nc.sbuf_tensor(name, list(shape), dt)), name)

        psum = outer.enter_context(nc.psum_tensor("psum", [128, 8, 512], F32))
        PS = [T(psum[:, k, :], "ps%d" % k) for k in range(8)]
        for p_ in PS:
            p_.b.psum = True

        def run_block():
            P.barrier()
            with nc.Block() as blk:
                P.emit(blk)

        identb = sb(outer, "identb", [128, 128], BF16)
        ones_f = sb(outer, "ones_f", [128, 128], F32)
        eps_t = sb(outer, "eps_t", [128, 1], F32)
        wis = sb(outer, "wis", [128, NB1, 4], F32)
        ksn_b = sb(outer, "ksn_b", [128, 512], BF16)
        kisn_b = sb(outer, "kisn_b", [128, 128], BF16)
        vsn_b = sb(outer, "vsn_b", [128, 8, 65], BF16)
        gk_bc = sb(outer, "gk_bc", [128, 64], F32)
        gq_bc = sb(outer, "gq_bc", [128, 64], F32)
        mid = ExitStack()
        ident4 = sb(outer, "ident4", [128, 512], BF16)

        P.dma("pool", lambda e: e.dma_start(out=identb[:], in_=ident_d[:, :]), writes=[identb.b])
        for r4 in range(4):
            P.dma("pool", lambda e, r4=r4: e.dma_start(out=ident4[:, r4 * 128:(r4 + 1) * 128], in_=ident_d[:, :]), writes=[ident4.b])
        P.op("dve", lambda e: e.memset(ones_f[:], 1.0), writes=[ones_f.b])
        P.op("dve", lambda e: e.memset(eps_t[:], EPS), writes=[eps_t.b])
        P.dma("sp", lambda e: e.dma_start(out=gk_bc[:], in_=k_norm_g[0:1, :].partition_broadcast(128)), writes=[gk_bc.b])
        P.dma("sp", lambda e: e.dma_start(out=gq_bc[:], in_=q_norm_g[0:1, :].partition_broadcast(128)), writes=[gq_bc.b])

        with ExitStack() as st:
            cT = sb(st, "cT", [128, 8, 5], F32)
            sT = sb(st, "sT", [128, 8, 5], F32)
            mod5 = sb(st, "mod5", [5, 6 * D], F32)
            bad = [sb(st, "bad%d" % i, [5, 512], F32) for i in range(2)]
            wa = [sb(st, "wa%d" % i, [128, 8, 512], F32) for i in range(2)]
            sel_t = sb(st, "sel_t", [5, 2, 128], F32)
            ng = [sb(st, "ng%d" % i, [128, D], F32) for i in range(2)]
            bct = [sb(st, "bct%d" % i, [128, D], F32) for i in range(2)]
            for r in range(5):
                P.dma("sp", lambda e, r=r: e.dma_start(out=cT[:, :, r], in_=c_all[r, :].rearrange("(c p) -> p c", p=128), allow_slow_non_contiguous=True), writes=[cT.b])
            P.dma("sp", lambda e: e.dma_start(out=sel_t[:], in_=sel_d.rearrange("s r p -> r s p")), writes=[sel_t.b])
            P.dma("sp", lambda e: e.dma_start(out=ng[0][:], in_=norm1_g[0:1, :].partition_broadcast(128)), writes=[ng[0].b])
            P.dma("sp", lambda e: e.dma_start(out=ng[1][:], in_=norm2_g[0:1, :].partition_broadcast(128)), writes=[ng[1].b])
            for c in range(8):
                P.dma("pool", lambda e, c=c: e.dma_start(out=win_b[c * 128:(c + 1) * 128, 0:1442], in_=w_in[c * 128:(c + 1) * 128, 0:1442]), writes=[b_winb])
                P.dma("pool", lambda e, c=c: e.dma_start(out=win_b[c * 128:(c + 1) * 128, 1442:INW], in_=w_in[c * 128:(c + 1) * 128, 1442:INW]), writes=[b_winb])
            P.op("act", lambda e: e.activation(out=sT[:], in_=cT[:], func=AF.Silu), reads=[cT.b], writes=[sT.b])
            for nt in range(12):
                w = wa[nt % 2]
                P.dma("sp", lambda e, w=w, nt=nt: e.dma_start(out=w[:], in_=w_ada[:, nt * 512:(nt + 1) * 512].rearrange("(c p) n -> p c n", p=128)), writes=[w.b])
                bd = bad[nt % 2]
                P.dma("sp", lambda e, bd=bd, nt=nt: e.dma_start(out=bd[:], in_=b_ada[0:1, nt * 512:(nt + 1) * 512].partition_broadcast(5)), writes=[bd.b])
                ps = PS[nt % 2]
                for c in range(8):
                    P.op("pe", lambda e, w=w, c=c, ps=ps: e.matmul(ps[0:5, :], lhsT=sT[:, c, :], rhs=w[:, c, :], start=(c == 0), stop=(c == 7)),
                         reads=[sT.b, w.b], writes=[ps.b])
                P.op("dve", lambda e, ps=ps, nt=nt, bd=bd: e.tensor_tensor(out=mod5[0:5, nt * 512:(nt + 1) * 512], in0=ps[0:5, :], in1=bd[0:5, :], op=ALU.add),
                     reads=[ps.b, bd.b], writes=[mod5.b])
            for s in range(2):
                for k in range(6):
                    for half in range(2):
                        ps = PS[2 + half]
                        P.op("pe", lambda e, ps=ps, s=s, k=k, half=half: e.matmul(ps[:, :], lhsT=sel_t[0:5, s, :], rhs=mod5[0:5, k * D + half * 512:k * D + half * 512 + 512], start=True, stop=True),
                             reads=[sel_t.b, mod5.b], writes=[ps.b])
                    pv = psum[:, 2:4, :].rearrange("p a b -> p (a b)")
                    rb = [PS[2].b, PS[3].b]
                    if k in (1, 4):
                        dst = bct[s]
                        g = ng[0] if k == 1 else ng[1]
                        P.op("dve", lambda e, dst=dst, g=g: e.scalar_tensor_tensor(out=dst[:], in0=pv, scalar=1.0, in1=g[:], op0=ALU.add, op1=ALU.mult),
                             reads=rb + [g.b], writes=[dst.b])
                    else:
                        dst = bct[s]
                        P.op("act", lambda e, dst=dst: e.activation(out=dst[:], in_=pv, func=AF.Identity), reads=rb, writes=[dst.b])
                    if True:
                        slot = {2: 0, 3: 2, 4: 1, 5: 3, 1: 4, 0: 5}[k]
                        P.dma("pool", lambda e, s=s, slot=slot, dst=dst: e.dma_start(out=bc_scr[s, slot, :, :], in_=dst[:]), reads=[dst.b], writes=[b_bcscr])
            run_block()
        if cfg.stop == 0:
            return nc

        KT = sb(mid, "KT", [128, 4, KMAX], BF16)
        KIT = sb(mid, "KIT", [128, KMAX], BF16)
        qt_scr = nc.dram_tensor("qt_scr", [NB1, 128, 512], BF16).ap()
        qit_scr = nc.dram_tensor("qit_scr", [NB1, 128, 256], BF16).ap()
        b_qt = Buf("qt_scr")
        b_qit = Buf("qit_scr")

        precast = []
        for r in range(8):
            precast.append(lambda r=r: P.dma("pool", lambda e: e.dma_start(out=wo_b[r * 128:(r + 1) * 128, :], in_=w_out[r * 128:(r + 1) * 128, :]), writes=[b_wob]))
        for r in range(8):
            for q in range(4):
                precast.append(lambda r=r, q=q: P.dma("pool", lambda e: e.dma_start(out=wf1_b[r * 128:(r + 1) * 128, q * 1024:(q + 1) * 1024], in_=w_ff1[r * 128:(r + 1) * 128, q * 1024:(q + 1) * 1024]), writes=[b_wf1b]))
        for r in range(32):
            precast.append(lambda r=r: P.dma("pool", lambda e: e.dma_start(out=wf2_b[r * 128:(r + 1) * 128, :], in_=w_ff2[r * 128:(r + 1) * 128, :]), writes=[b_wf2b]))

        with ExitStack() as st:
            Wb = sb(st, "Wb", [128, 8, INW], BF16)
            xt = [sb(st, "xt%d" % i, [128, D], F32) for i in range(2)]
            rp = [sb(st, "rp%d" % i, [128, 64], F32) for i in range(2)]
            tmpf2 = [sb(st, "tmpf%d" % i, [128, D], F32) for i in range(2)]
            hb2 = [sb(st, "hb%d" % i, [128, D], BF16) for i in range(2)]
            hT2 = [sb(st, "hT%d" % i, [128, 8, 128], BF16) for i in range(2)]
            st12 = [sb(st, "st1%d" % i, [128, 4], F32) for i in range(2)]
            sq = sb(st, "sq", [128, 512], F32)
            kg = sb(st, "kg", [128, 512], F32)
            ta = sb(st, "ta", [128, 512], F32)
            tb = sb(st, "tb", [128, 512], F32)
            kr = sb(st, "kr", [128, 512], F32)
            kn = [sb(st, "kn0", [128, 512], F32)] * 2
            knb = sb(st, "knb", [128, 512], BF16)
            st8 = sb(st, "st8", [128, 3, 8], F32)
            vb = [sb(st, "vb%d" % i, [128, 8, 65], BF16) for i in range(2)]
            vf = [sb(st, "vf0", [128, 512], F32)] * 2
            dt_ = sb(st, "dt_", [128, 324], F32)
            kif = [sb(st, "kif%d" % i, [128, 64], F32) for i in range(2)]
            kib = sb(st, "kib", [128, 128], BF16)
            kiraw = sb(st, "kiraw", [128, 64], F32)
            qib = sb(st, "qib", [128, 256], BF16)
            ug = sb(st, "ug", [128, 512], F32)
            vn = sb(st, "vn", [128, 512], F32)
            vnb = sb(st, "vnb", [128, 512], BF16)
            gmb = sb(st, "gmb", [128, 512], BF16)
            mgT = sb(st, "mgT", [128, 512], BF16)
            bnst = sb(st, "bnst", [128, 8], F32)
            lng_bc = sb(st, "lng_bc", [128, 512], F32)
            lnb_bc = sb(st, "lnb_bc", [128, 512], F32)
            WsT = [sb(st, "WsT%d" % s, [128, 4, 128], BF16) for s in range(2)]
            bs_t = [sb(st, "bs_t%d" % s, [128, 4], F32) for s in range(2)]
            wsn = sb(st, "wsn", [128, 128], F32)
            wsb = sb(st, "wsb", [128, 128], BF16)
            tril_t = sb(st, "tril_t", [128, 2, 128], F32)
            qts = sb(st, "qts", [128, 4, 128], BF16)
            G1 = [sb(st, "G1_0", [128, D], F32)]
            S1 = [sb(st, "S1_0", [128, D], F32)]
            gx = sb(st, "gx", [128, 512], F32)
            gs = sb(st, "gs", [128, 512], F32)
            P.dma("sp", lambda e: e.dma_start(out=G1[0][:], in_=bc_scr[0, 4, :, :]), reads=[b_bcscr], writes=[G1[0].b])
            P.dma("sp", lambda e: e.dma_start(out=S1[0][:], in_=bc_scr[0, 5, :, :]), reads=[b_bcscr], writes=[S1[0].b])
            qits = sb(st, "qits", [128, 2, 128], BF16)

            for c in range(8):
                P.dma("sp", lambda e, c=c: e.dma_start(out=Wb[:, c, :], in_=win_b[c * 128:(c + 1) * 128, :]), reads=[b_winb], writes=[Wb.b])
            P.dma("sp", lambda e: e.dma_start(out=lng_bc[:], in_=ln_g[0:1, :].partition_broadcast(128)), writes=[lng_bc.b])
            P.dma("sp", lambda e: e.dma_start(out=lnb_bc[:], in_=ln_b[0:1, :].partition_broadcast(128)), writes=[lnb_bc.b])
            P.dma("sp", lambda e: e.dma_start(out=tril_t[:], in_=tril_d.rearrange("s p q -> p s q")), writes=[tril_t.b])
            for v_ in vb:
                P.op("pool", lambda e, v_=v_: e.memset(v_[:], 1.0), writes=[v_.b])
            P.op("pool", lambda e: e.memset(vsn_b[:], 1.0), writes=[vsn_b.b])
            P.dma("sp", lambda e: e.dma_start(out=bs_t[0][:], in_=gbs.rearrange("g t -> t g"), allow_slow_non_contiguous=True), writes=[bs_t[0].b])
            for q4 in range(4):
                P.dma("sp", lambda e, q4=q4: e.dma_start(out=bs_t[1][q4 * 32:(q4 + 1) * 32, :], in_=gbs[:, 0:32].rearrange("g t -> t g"), allow_slow_non_contiguous=True), writes=[bs_t[1].b])
            for s in range(2):
                for g in range(4):
                    if s == 0:
                        P.dma("sp", lambda e, g=g: e.dma_start(out=wsn[:], in_=gws[g, :, :]), writes=[wsn.b])
                    else:
                        P.op("dve", lambda e: e.memset(wsn[:], 0.0), writes=[wsn.b])
                        for q4 in range(4):
                            P.dma("sp", lambda e, g=g, q4=q4: e.dma_start(out=wsn[q4 * 32:(q4 + 1) * 32, q4 * 32:(q4 + 1) * 32], in_=gws[g, 0:32, 0:32]), writes=[wsn.b])
                    P.op("dve", lambda e, s=s: e.tensor_tensor(out=wsb[:], in0=wsn[:], in1=tril_t[:, s, :], op=ALU.mult), reads=[wsn.b, tril_t.b], writes=[wsb.b])
                    ps = PS[7]
                    pb = ps[:, :].bitcast(BF16)
                    P.op("pe", lambda e, pb=pb: e.transpose(out=pb[:, 0:128], in_=wsb[:], identity=identb[:]), reads=[wsb.b, identb.b], writes=[ps.b])
                    P.op("act", lambda e, pb=pb, s=s, g=g: e.activation(out=WsT[s][:, g, :], in_=pb[:, 0:128], func=AF.Identity), reads=[ps.b], writes=[WsT[s].b])

            def rope4(eng, src, dst, H, rpt):
                n = H * 64
                cosb = rpt[:, 0:32].unsqueeze(1).unsqueeze(1).broadcast_to([128, H, 2, 32])
                sinb = rpt[:, 32:64].unsqueeze(1).unsqueeze(1).broadcast_to([128, H, 2, 32])
                s4 = src[:, 0:n].rearrange("p (h t d) -> p h t d", h=H, t=2)
                a4 = ta[:, 0:n].rearrange("p (h t d) -> p h t d", h=H, t=2)
                b4 = tb[:, 0:n].rearrange("p (h t d) -> p h t d", h=H, t=2)
                d4 = dst[:, 0:n].rearrange("p (h t d) -> p h t d", h=H, t=2)
                P.op(eng, lambda e: e.tensor_tensor(out=a4, in0=s4, in1=cosb, op=ALU.mult), reads=[src.b, rpt.b], writes=[ta.b])
                P.op(eng, lambda e: e.tensor_tensor(out=b4, in0=s4, in1=sinb, op=ALU.mult), reads=[src.b, rpt.b], writes=[tb.b])
                P.op(eng, lambda e: e.tensor_tensor(out=d4[:, :, 0, :], in0=a4[:, :, 0, :], in1=b4[:, :, 1, :], op=ALU.subtract), reads=[ta.b, tb.b], writes=[dst.b])
                P.op(eng, lambda e: e.tensor_tensor(out=d4[:, :, 1, :], in0=a4[:, :, 1, :], in1=b4[:, :, 0, :], op=ALU.add), reads=[ta.b, tb.b], writes=[dst.b])

            def norm_to_hT(xtile, Gt, St, par):
                tmpf, hb, hT, st1 = tmpf2[par], hb2[par], hT2[par], st12[par]
                P.op("act", lambda e: e.activation(out=hb[:], in_=xtile[:], func=AF.Square, accum_out=st1[:, 0:1]), reads=[xtile.b], writes=[hb.b, st1.b])
                P.op("act", lambda e: e.activation(out=st1[:, 1:2], in_=st1[:, 0:1], func=AF.Sqrt, bias=eps_t[:, 0:1], scale=1.0 / D), reads=[st1.b, eps_t.b], writes=[st1.b])
                P.op("dve", lambda e: e.reciprocal(out=st1[:, 2:3], in_=st1[:, 1:2]), reads=[st1.b], writes=[st1.b])
                P.op("dve", lambda e: e.scalar_tensor_tensor(out=tmpf[:], in0=xtile[:], scalar=st1[:, 2:3], in1=Gt[:], op0=ALU.mult, op1=ALU.mult), reads=[xtile.b, st1.b, Gt.b], writes=[tmpf.b])
                P.op("dve", lambda e: e.tensor_tensor(out=hb[:], in0=tmpf[:], in1=St[:], op=ALU.add), reads=[tmpf.b, St.b], writes=[hb.b])
                ps = PS[0]
                pb = ps[:, :].bitcast(BF16)
                for c in range(8):
                    P.op("pe", lambda e, c=c: e.transpose(out=pb[:, c * 128:(c + 1) * 128], in_=hb[:, c * 128:(c + 1) * 128], identity=identb[:]), reads=[hb.b, identb.b], writes=[ps.b])
                P.op("act", lambda e: e.activation(out=hT[:].rearrange("p c t -> p (c t)"), in_=pb, func=AF.Identity), reads=[ps.b], writes=[hT.b])

            def proj(ps, c0, c1, par):
                n = c1 - c0
                hT = hT2[par]
                for c in range(8):
                    P.op("pe", lambda e, c=c: e.matmul(ps[:, 0:n], lhsT=hT[:, c, :], rhs=Wb[:, c, c0:c1], start=(c == 0), stop=(c == 7)), reads=[hT.b, Wb.b], writes=[ps.b])

            def qk_post(ps, g_bc, rpt, dst_ap, dst_b):
                P.op("act", lambda e: e.activation(out=sq[:], in_=ps[:, :], func=AF.Square), reads=[ps.b], writes=[sq.b])
                P.op("dve", lambda e: e.tensor_tensor(out=kg[:].rearrange("p (h d) -> p h d", h=8), in0=ps[:, :].rearrange("p (h d) -> p h d", h=8),
                                                      in1=g_bc[:, :].unsqueeze(1).broadcast_to([128, 8, 64]), op=ALU.mult), reads=[ps.b, g_bc.b], writes=[kg.b])
                P.op("dve", lambda e: e.tensor_reduce(out=st8[:, 0, :], in_=sq[:].rearrange("p (h d) -> p h d", h=8), axis=AX.X, op=ALU.add), reads=[sq.b], writes=[st8.b])
                P.op("act", lambda e: e.activation(out=st8[:, 1, :], in_=st8[:, 0, :], func=AF.Sqrt, bias=eps_t[:, 0:1], scale=1.0 / 64), reads=[st8.b, eps_t.b], writes=[st8.b])
                P.op("dve", lambda e: e.reciprocal(out=st8[:, 2, :], in_=st8[:, 1, :]), reads=[st8.b], writes=[st8.b])
                rope4("dve", kg, kr, 8, rpt)
                P.op("dve", lambda e: e.tensor_tensor(out=dst_ap.rearrange("p (h d) -> p h d", h=8), in0=kr[:].rearrange("p (h d) -> p h d", h=8),
                                                      in1=st8[:, 2, :].unsqueeze(2).broadcast_to([128, 8, 64]), op=ALU.mult), reads=[kr.b, st8.b], writes=[dst_b])

            def transposes_to(src, ncol, dst_ap, dst_b, psk=1):
                ps = PS[psk]
                pb = ps[:, :].bitcast(BF16)
                for c in range(ncol):
                    P.op("pe", lambda e, c=c: e.transpose(out=pb[:, c * 128:(c + 1) * 128], in_=src[:, c * 128:(c + 1) * 128], identity=identb[:]), reads=[src.b, identb.b], writes=[ps.b])
                P.op("act", lambda e: e.activation(out=dst_ap, in_=pb[:, 0:ncol * 128].rearrange("p (c t) -> p c t", c=ncol), func=AF.Identity), reads=[ps.b], writes=[dst_b])

            def gelu(src_ps, dst):
                P.op("act", lambda e: e.activation(out=kg[:], in_=src_ps[:, :], func=AF.Identity), reads=[src_ps.b], writes=[kg.b])
                P.op("act", lambda e: e.activation(out=sq[:], in_=src_ps[:, :], func=AF.Square), reads=[src_ps.b], writes=[sq.b])
                P.op("dve", lambda e: e.tensor_scalar(out=sq[:], in0=sq[:], scalar1=0.044715, scalar2=1.0, op0=ALU.mult, op1=ALU.add), reads=[sq.b], writes=[sq.b])
                P.op("dve", lambda e: e.tensor_tensor(out=ta[:], in0=sq[:], in1=kg[:], op=ALU.mult), reads=[sq.b, kg.b], writes=[ta.b])
                P.op("act", lambda e: e.activation(out=ta[:], in_=ta[:], func=AF.Tanh, scale=0.7978845608028654), reads=[ta.b], writes=[ta.b])
                P.op("dve", lambda e: e.tensor_scalar(out=ta[:], in0=ta[:], scalar1=1.0, scalar2=0.5, op0=ALU.add, op1=ALU.mult), reads=[ta.b], writes=[ta.b])
                P.op("dve", lambda e: e.tensor_tensor(out=dst[:], in0=ta[:], in1=kg[:], op=ALU.mult), reads=[ta.b, kg.b], writes=[dst.b])

            if cfg.stop == 1:
                run_block()
                return nc
            def load_x(src, rsrc, r0, i):
                P.dma("sp", lambda e: e.dma_start(out=xt[i % 2][:], in_=src[r0:r0 + 128, :]), writes=[xt[i % 2].b])
                P.dma("sp", lambda e: e.dma_start(out=rp[i % 2][:], in_=rsrc[r0:r0 + 128, :]), writes=[rp[i % 2].b])

            def all_s3a(t):
                par = t % 2
                psK, psV, psKI = PS[2 + 3 * par], PS[3 + 3 * par], PS[4 + 3 * par]
                P.op("act", lambda e: e.activation(out=sq[:], in_=psK[:, :], func=AF.Square), reads=[psK.b], writes=[sq.b])
                P.op("dve", lambda e: e.tensor_tensor(out=kg[:].rearrange("p (h d) -> p h d", h=8), in0=psK[:, :].rearrange("p (h d) -> p h d", h=8),
                                                      in1=gk_bc[:, :].unsqueeze(1).broadcast_to([128, 8, 64]), op=ALU.mult), reads=[psK.b, gk_bc.b], writes=[kg.b])
                P.op("dve", lambda e: e.tensor_reduce(out=st8[:, 0, :], in_=sq[:].rearrange("p (h d) -> p h d", h=8), axis=AX.X, op=ALU.add), reads=[sq.b], writes=[st8.b])
                P.op("act", lambda e: e.activation(out=st8[:, 1, :], in_=st8[:, 0, :], func=AF.Sqrt, bias=eps_t[:, 0:1], scale=1.0 / 64), reads=[st8.b, eps_t.b], writes=[st8.b])
                P.op("dve", lambda e: e.reciprocal(out=st8[:, 2, :], in_=st8[:, 1, :]), reads=[st8.b], writes=[st8.b])
                v_ = vb[par]
                P.op("act", lambda e: e.activation(out=v_[:, :, 1:65], in_=psV[:, :].rearrange("p (h d) -> p h d", h=8), func=AF.Identity), reads=[psV.b], writes=[v_.b])
                P.dma("pool", lambda e: e.dma_start(out=v_scr[t * 128:(t + 1) * 128, :], in_=v_[:].rearrange("p h d -> p (h d)")), reads=[v_.b], writes=[b_vscr])
                if precast:
                    precast.pop(0)()
                P.op("act", lambda e: e.activation(out=kiraw[:, 0:64], in_=psKI[:, 0:64], func=AF.Identity), reads=[psKI.b], writes=[kiraw.b])

            def all_s12(t):
                par = t % 2
                norm_to_hT(xt[par], G1[0], S1[0], par)
                proj(PS[2 + 3 * par], 512, 1024, par)
                proj(PS[3 + 3 * par], 1024, 1536, par)
                proj(PS[4 + 3 * par], 1792, 1856, par)

            def all_s3b(t):
                par = t % 2
                rpt = rp[par]
                rope4("dve", kg, kr, 8, rpt)
                P.op("dve", lambda e: e.tensor_tensor(out=knb[:].rearrange("p (h d) -> p h d", h=8), in0=kr[:].rearrange("p (h d) -> p h d", h=8),
                                                      in1=st8[:, 2, :].unsqueeze(2).broadcast_to([128, 8, 64]), op=ALU.mult), reads=[kr.b, st8.b], writes=[knb.b])
                transposes_to(knb, 4, KT[:, :, t * 128:(t + 1) * 128], KT.b)
                kf_ = kif[par]
                rope4("dve", kiraw, kf_, 1, rpt)
                P.op("dve", lambda e: e.tensor_copy(out=kib[:].rearrange("p (a d) -> p a d", a=2), in_=kf_[:, :].unsqueeze(1).broadcast_to([128, 2, 64])), reads=[kf_.b], writes=[kib.b])
                ps = PS[1]
                pb = ps[:, :].bitcast(BF16)
                P.op("pe", lambda e: e.transpose(out=pb[:, 512:640], in_=kib[:], identity=identb[:]), reads=[kib.b, identb.b], writes=[ps.b])
                P.op("act", lambda e: e.activation(out=KIT[:, t * 128:(t + 1) * 128], in_=pb[:, 512:640], func=AF.Identity), reads=[ps.b], writes=[KIT.b])

            load_x(x_seq, rope_seq, 0, 0)
            if NBLK > 1:
                load_x(x_seq, rope_seq, 128, 1)
            all_s12(0)
            for t in range(NBLK):
                all_s3a(t)
                if t + 1 < NBLK:
                    all_s12(t + 1)
                all_s3b(t)
                if t + 2 < NBLK:
                    load_x(x_seq, rope_seq, (t + 2) * 128, t + 2)

            if cfg.stop == 2:
                run_block()
                return nc
            def gelu_p1(src_ps, dst, tmp):
                P.op("act", lambda e: e.activation(out=dst[:], in_=src_ps[:, :], func=AF.Identity), reads=[src_ps.b], writes=[dst.b])
                P.op("act", lambda e: e.activation(out=tmp[:], in_=src_ps[:, :], func=AF.Square), reads=[src_ps.b], writes=[tmp.b])
                P.op("pool", lambda e: e.tensor_scalar(out=tmp[:], in0=tmp[:], scalar1=0.044715, scalar2=1.0, op0=ALU.mult, op1=ALU.add), reads=[tmp.b], writes=[tmp.b])
                P.op("pool", lambda e: e.tensor_tensor(out=tmp[:], in0=tmp[:], in1=dst[:], op=ALU.mult), reads=[tmp.b, dst.b], writes=[tmp.b])

            def gelu_p2(dst, tmp):
                P.op("act", lambda e: e.activation(out=tmp[:], in_=tmp[:], func=AF.Tanh, scale=0.7978845608028654), reads=[tmp.b], writes=[tmp.b])
                P.op("pool", lambda e: e.tensor_scalar(out=tmp[:], in0=tmp[:], scalar1=1.0, scalar2=0.5, op0=ALU.add, op1=ALU.mult), reads=[tmp.b], writes=[tmp.b])
                P.op("pool", lambda e: e.tensor_tensor(out=dst[:], in0=tmp[:], in1=dst[:], op=ALU.mult), reads=[tmp.b, dst.b], writes=[dst.b])

            bK, bV, bD, bQ, bU, bVG = PS[2], PS[3], PS[4], PS[5], PS[6], PS[7]
            load_x(x_own, rope_own, 0, 0)
            if NB1 > 1:
                load_x(x_own, rope_own, 128, 1)
            if NI == 0:
                P.dma("sp", lambda e: e.dma_start(out=G1[0][:], in_=bc_scr[1, 4, :, :]), reads=[b_bcscr], writes=[G1[0].b])
                P.dma("sp", lambda e: e.dma_start(out=S1[0][:], in_=bc_scr[1, 5, :, :]), reads=[b_bcscr], writes=[S1[0].b])
            norm_to_hT(xt[0], G1[0], S1[0], 0)
            proj(bU, 1860, 2372, 0)
            proj(bVG, 2372, 2884, 0)
            proj(bK, 512, 1024, 0)
            proj(bV, 1024, 1536, 0)
            proj(bD, 1536, 1860, 0)
            proj(bQ, 0, 512, 0)
            for i in range(NB1):
                s = 0 if i < NI else 1
                par = i % 2
                nxt = i + 1 < NB1
                pn = (i + 1) % 2
                if nxt:
                    if i + 1 == NI:
                        P.dma("sp", lambda e: e.dma_start(out=G1[0][:], in_=bc_scr[1, 4, :, :]), reads=[b_bcscr], writes=[G1[0].b])
                        P.dma("sp", lambda e: e.dma_start(out=S1[0][:], in_=bc_scr[1, 5, :, :]), reads=[b_bcscr], writes=[S1[0].b])
                    norm_to_hT(xt[pn], G1[0], S1[0], pn)
                xtile, rpt = xt[i % 2], rp[i % 2]
                r0 = i * 128
                gelu_p1(bU, ug, gx)
                gelu_p1(bVG, vn, gs)
                if nxt:
                    proj(bU, 1860, 2372, pn)
                    proj(bVG, 2372, 2884, pn)
                kn_ = kn[i % 2]
                qk_post(bK, gk_bc, rpt, kn_[:], kn_.b)
                if nxt:
                    proj(bK, 512, 1024, pn)
                P.dma("pool", lambda e, kn_=kn_, r0=r0: e.dma_start(out=k_own[r0:r0 + 128, :], in_=kn_[:]), reads=[kn_.b])
                if s == 1:
                    P.op("dve", lambda e, kn_=kn_: e.tensor_copy(out=ksn_b[:], in_=kn_[:]), reads=[kn_.b], writes=[ksn_b.b])
                vf_ = vf[i % 2]
                P.op("act", lambda e, vf_=vf_: e.activation(out=vf_[:], in_=bV[:, :], func=AF.Identity), reads=[bV.b], writes=[vf_.b])
                if nxt:
                    proj(bV, 1024, 1536, pn)
                P.dma("pool", lambda e, vf_=vf_, r0=r0: e.dma_start(out=v_own[r0:r0 + 128, :], in_=vf_[:]), reads=[vf_.b])
                if s == 1:
                    P.op("dve", lambda e, vf_=vf_: e.tensor_copy(out=vsn_b[:, :, 1:65], in_=vf_[:].rearrange("p (h d) -> p h d", h=8)), reads=[vf_.b], writes=[vsn_b.b])
                P.op("act", lambda e: e.activation(out=dt_[:], in_=bD[:, 0:324], func=AF.Identity), reads=[bD.b], writes=[dt_.b])
                if nxt:
                    proj(bD, 1536, 1860, pn)
                P.op("dve", lambda e, i=i: e.tensor_scalar(out=wis[:, i, :], in0=dt_[:, 320:324], scalar1=0.0625, scalar2=None, op0=ALU.mult), reads=[dt_.b], writes=[wis.b])
                kf_ = kif[i % 2]
                P.op("dve", lambda e: e.tensor_copy(out=kg[:, 0:64], in_=dt_[:, 256:320]), reads=[dt_.b], writes=[kg.b])
                rope4("dve", kg, kf_, 1, rpt)
                P.dma("pool", lambda e, kf_=kf_, r0=r0: e.dma_start(out=ki_own[r0:r0 + 128, :], in_=kf_[:]), reads=[kf_.b])
                if s == 1:
                    P.op("dve", lambda e, kf_=kf_: e.tensor_copy(out=kisn_b[:].rearrange("p (a d) -> p a d", a=2), in_=kf_[:, :].unsqueeze(1).broadcast_to([128, 2, 64])), reads=[kf_.b], writes=[kisn_b.b])
                P.op("dve", lambda e: e.tensor_copy(out=kg[:, 0:256], in_=dt_[:, 0:256]), reads=[dt_.b], writes=[kg.b])
                rope4("dve", kg, kr, 4, rpt)
                P.op("dve", lambda e: e.tensor_copy(out=qib[:], in_=kr[:, 0:256]), reads=[kr.b], writes=[qib.b])
                transposes_to(qib, 2, qits[:], qits.b)
                P.dma("pool", lambda e, i=i: e.dma_start(out=qit_scr[i, :, :], in_=qits[:].rearrange("p c t -> p (c t)")), reads=[qits.b], writes=[b_qit])
                gelu_p2(ug, gx)
                gelu_p2(vn, gs)
                qk_post(bQ, gq_bc, rpt, knb[:], knb.b)
                if nxt:
                    proj(bQ, 0, 512, pn)
                transposes_to(knb, 4, qts[:], qts.b)
                P.dma("pool", lambda e, i=i: e.dma_start(out=qt_scr[i, :, :], in_=qts[:].rearrange("p c t -> p (c t)")), reads=[qts.b], writes=[b_qt])
                P.op("dve", lambda e: e.bn_stats(out=bnst[:, 0:6], in_=vn[:]), reads=[vn.b], writes=[bnst.b])
                P.op("dve", lambda e: e.bn_aggr(out=bnst[:, 6:8], in_=bnst[:, 0:6]), reads=[bnst.b], writes=[bnst.b])
                P.op("act", lambda e: e.activation(out=bnst[:, 0:1], in_=bnst[:, 7:8], func=AF.Sqrt, bias=eps_t[:, 0:1], scale=1.0), reads=[bnst.b, eps_t.b], writes=[bnst.b])
                P.op("dve", lambda e: e.reciprocal(out=bnst[:, 1:2], in_=bnst[:, 0:1]), reads=[bnst.b], writes=[bnst.b])
                P.op("dve", lambda e: e.tensor_scalar(out=vn[:], in0=vn[:], scalar1=bnst[:, 6:7], scalar2=bnst[:, 1:2], op0=ALU.subtract, op1=ALU.mult), reads=[vn.b, bnst.b], writes=[vn.b])
                P.op("dve", lambda e: e.tensor_tensor(out=vn[:], in0=vn[:], in1=lng_bc[:], op=ALU.mult), reads=[vn.b, lng_bc.b], writes=[vn.b])
                P.op("dve", lambda e: e.tensor_tensor(out=vn[:], in0=vn[:], in1=lnb_bc[:], op=ALU.add), reads=[vn.b, lnb_bc.b], writes=[vn.b])
                if s == 1:
                    P.dma("pool", lambda e: e.dma_start(out=gv_own[:, :], in_=vn[:]), reads=[vn.b])
                P.op("dve", lambda e: e.tensor_copy(out=vnb[:], in_=vn[:]), reads=[vn.b], writes=[vnb.b])
                ps = PS[1]
                for g in range(4):
                    P.op("pe", lambda e, g=g, s=s, ps=ps: e.matmul(ps[:, g * 128:(g + 1) * 128], lhsT=WsT[s][:, g, :], rhs=vnb[:, g * 128:(g + 1) * 128], start=True, stop=True), reads=[WsT[s].b, vnb.b], writes=[ps.b])
                for g in range(4):
                    P.op("dve", lambda e, g=g, s=s, ps=ps: e.scalar_tensor_tensor(out=gmb[:, g * 128:(g + 1) * 128], in0=ps[:, g * 128:(g + 1) * 128], scalar=bs_t[s][:, g:g + 1], in1=ug[:, g * 128:(g + 1) * 128], op0=ALU.add, op1=ALU.mult),
                         reads=[ps.b, bs_t[s].b, ug.b], writes=[gmb.b])
                transposes_to(gmb, 4, mgT[:].rearrange("p (c t) -> p c t", c=4), mgT.b)
                P.dma("pool", lambda e, i=i: e.dma_start(out=mixg_scr[i, :, :], in_=mgT[:]), reads=[mgT.b], writes=[b_mixg])
                if i + 2 < NB1:
                    load_x(x_own, rope_own, (i + 2) * 128, i + 2)
                if precast:
                    precast.pop(0)()
            run_block()
        if cfg.stop == 3:
            mid.close()
            return nc

        with ExitStack() as st:
            Isc = sb(st, "Isc", [128, KMAX], F32)
            negm2 = [sb(st, "negm%d" % i, [128, KMAX], BF16) for i in range(2)]
            qbd = [sb(st, "qbd%d" % i, [128, 4, 256], BF16) for i in range(2)]
            sm2 = [sb(st, "sm%d" % i, [128, 8], F32) for i in range(2)]
            sa2 = [sb(st, "sa%d" % i, [128, 1], F32) for i in range(2)]
            cd2 = [sb(st, "cd%d" % i, [128, 1], F32) for i in range(2)]
            negm_b2 = [Buf("negm_b2_%d" % i) for i in range(2)]
            dl2 = [sb(st, "dl%d" % i, [128, NITER], F32) for i in range(2)]
            rr = [sb(st, "rr%d" % i, [128, 2, 512], F32) for i in range(2)]
            Vb = [sb(st, "Vb%d" % i, [128, 4, 520], BF16) for i in range(2)]
            Pb = [sb(st, "Pb%d" % i, [128, 512], BF16) for i in range(4)]
            qitb = [sb(st, "qitb%d" % i, [128, 2, 128], BF16) for i in range(2)]
            cm_p = sb(st, "cm_p", [128, 512], F32)
            cm_s = sb(st, "cm_s", [128, 128], F32)
            pw2 = sb(st, "pw2", [128, NITER], F32)
            lnd = sb(st, "lnd", [1, 1024], F32)
            rden = sb(st, "rden", [1, 1024], F32)
            bcs = sb(st, "bcs", [65, 1024], F32)
            mixa = sb(st, "mixa", [65, 8, 128], BF16)
            zl = sb(st, "zl", [128, 65], BF16)
            zr = sb(st, "zr", [128, 512], BF16)
            ckf = [sb(st, "ckf%d" % i, [128, 512], F32) for i in range(2)]
            cvf = [sb(st, "cvf%d" % i, [128, 512], F32) for i in range(2)]
            ckif = [sb(st, "ckif%d" % i, [128, 64], F32) for i in range(2)]
            ckb = sb(st, "ckb", [128, 512], BF16)
            cvb = [sb(st, "cvb%d" % i, [128, 8, 65], BF16) for i in range(2)]
            ckib = sb(st, "ckib", [128, 128], BF16)
            zv = sb(st, "zv", [128, 520], BF16)

            while precast:
                precast.pop(0)()
            P.dma("sp", lambda e: e.dma_start(out=cm_p[:], in_=cmask_p_d[:, :]), writes=[cm_p.b])
            P.dma("sp", lambda e: e.dma_start(out=cm_s[:], in_=cmask_s_d[:, :]), writes=[cm_s.b])
            P.dma("sp", lambda e: e.dma_start(out=pw2[:], in_=pow2_d[:, :]), writes=[pw2.b])
            P.op("pool", lambda e: e.memset(zl[:], 0.0), writes=[zl.b])
            P.op("pool", lambda e: e.memset(zr[:], 0.0), writes=[zr.b])
            P.op("pool", lambda e: e.memset(zv[:], 0.0), writes=[zv.b])
            for c_ in cvb:
                P.op("pool", lambda e, c_=c_: e.memset(c_[:], 1.0), writes=[c_.b])
            for q_ in qbd:
                P.op("pool", lambda e, q_=q_: e.memset(q_[:], 0.0), writes=[q_.b])
            cnts = {"l": 0, "p": 0, "sel": 0}
            LRING = [4, 5, 0, 1, 2, 3]
            LA = 5

            def idxsel(i, nkeys, topk, cm, cm_w):
                par = cnts["sel"] % 2
                cnts["sel"] += 1
                negm, qbd_, qit_, sm, dl = negm2[par], qbd[par], qitb[par], sm2[par], dl2[par]
                P.dma("sp", lambda e: e.dma_start(out=qbd_[0:64, :, 0:128], in_=qt_scr[i, 0:64, :].rearrange("p (c t) -> p c t", c=4)), reads=[b_qt], writes=[qbd_.b])
                P.dma("sp", lambda e: e.dma_start(out=qbd_[64:128, :, 128:256], in_=qt_scr[i, 64:128, :].rearrange("p (c t) -> p c t", c=4)), reads=[b_qt], writes=[qbd_.b])
                P.dma("sp", lambda e: e.dma_start(out=qit_[:].rearrange("p c t -> p (c t)"), in_=qit_scr[i, :, :]), reads=[b_qit], writes=[qit_.b])
                tiles = [(k0, min(512, nkeys - k0)) for k0 in range(0, nkeys, 512)]
                gi = 0
                for (k0, w) in tiles:
                    for grp in range(2):
                        gs = gi % 2
                        gi += 1
                        for hh in range(2):
                            ps = PS[2 * gs + hh]
                            P.op("pe", lambda e, ps=ps, hh=hh, grp=grp, k0=k0, w=w: e.matmul(ps[:, 0:w], lhsT=qit_[64 * hh:64 * hh + 64, grp, :], rhs=KIT[64 * hh:64 * hh + 64, k0:k0 + w], start=True, stop=True),
                                 reads=[qit_.b, KIT.b], writes=[ps.b])
                        r_ = rr[gs]
                        P.op("act", lambda e, gs=gs, w=w, r_=r_: e.activation(out=r_[:, :, 0:w], in_=psum[:, 2 * gs:2 * gs + 2, 0:w], func=AF.Relu),
                             reads=[PS[2 * gs].b, PS[2 * gs + 1].b], writes=[r_.b])
                        for hh in range(2):
                            h = 2 * grp + hh
                            if h == 0:
                                P.op("dve", lambda e, r_=r_, k0=k0, w=w: e.tensor_scalar(out=Isc[:, k0:k0 + w], in0=r_[:, 0, 0:w], scalar1=wis[:, i, 0:1], scalar2=None, op0=ALU.mult),
                                     reads=[r_.b, wis.b], writes=[Isc.b])
                            else:
                                P.op("dve", lambda e, r_=r_, k0=k0, w=w, hh=hh, h=h: e.scalar_tensor_tensor(out=Isc[:, k0:k0 + w], in0=r_[:, hh, 0:w], scalar=wis[:, i, h:h + 1], in1=Isc[:, k0:k0 + w], op0=ALU.mult, op1=ALU.add),
                                     reads=[r_.b, wis.b, Isc.b], writes=[Isc.b])
                P.op("dve", lambda e: e.tensor_reduce(out=sm[:, 0:1], in_=Isc[:, 0:nkeys], axis=AX.X, op=ALU.max, apply_absolute_value=True), reads=[Isc.b], writes=[sm.b])
                P.op("dve", lambda e: e.tensor_tensor(out=Isc[:, nkeys - cm_w:nkeys], in0=Isc[:, nkeys - cm_w:nkeys], in1=cm[:, 0:cm_w], op=ALU.add), reads=[Isc.b, cm.b], writes=[Isc.b])
                P.op("dve", lambda e: e.tensor_scalar(out=sm[:, 1:2], in0=sm[:, 0:1], scalar1=-1.001, scalar2=-1e-20, op0=ALU.mult, op1=ALU.add), reads=[sm.b], writes=[sm.b])
                P.op("dve", lambda e: e.tensor_scalar(out=sm[:, 2:3], in0=sm[:, 0:1], scalar1=2.003, scalar2=3e-20, op0=ALU.mult, op1=ALU.add), reads=[sm.b], writes=[sm.b])
                P.op("dve", lambda e: e.tensor_scalar(out=dl[:], in0=pw2[:], scalar1=sm[:, 2:3], scalar2=None, op0=ALU.mult), reads=[sm.b, pw2.b], writes=[dl.b])
                a_split = ((nkeys * 9 // 16) // 128) * 128
                if nkeys - a_split < 256:
                    a_split = nkeys
                stt = dict(par=par, nkeys=nkeys, topk=topk, a=a_split, m=0)
                return stt

            def bis_iter(stt):
                par, nkeys, topk, a, m = stt["par"], stt["nkeys"], stt["topk"], stt["a"], stt["m"]
                negm, sm, dl, sa, cd = negm2[par], sm2[par], dl2[par], sa2[par], cd2[par]
                P.op("dve", lambda e: e.tensor_tensor(out=cd[:, 0:1], in0=sm[:, 1:2], in1=dl[:, m:m + 1], op=ALU.add), reads=[sm.b, dl.b], writes=[cd.b])
                if a < nkeys:
                    P.op("act", lambda e: e.activation(out=negm[:, a:nkeys], in_=Isc[:, a:nkeys], func=AF.Sign, scale=-1.0, bias=cd[:, 0:1], accum_out=sa[:, 0:1]),
                         reads=[Isc.b, cd.b], writes=[negm_b2[par], sa.b])
                P.op("dve", lambda e: e.tensor_scalar(out=negm[:, 0:a], in0=Isc[:, 0:a], scalar1=cd[:, 0:1], scalar2=None, op0=ALU.is_ge, op1=ALU.add, accum_out=sm[:, 4:5]),
                     reads=[Isc.b, cd.b], writes=[negm.b, sm.b])
                if a < nkeys:
                    L = nkeys - a
                    P.op("dve", lambda e: e.scalar_tensor_tensor(out=sm[:, 6:7], in0=sa[:, 0:1], scalar=-0.5, in1=sm[:, 4:5], op0=ALU.mult, op1=ALU.add), reads=[sa.b, sm.b], writes=[sm.b])
                    P.op("dve", lambda e: e.tensor_scalar(out=sm[:, 5:6], in0=sm[:, 6:7], scalar1=float(topk) - 0.5 - 0.5 * L, scalar2=None, op0=ALU.is_ge), reads=[sm.b], writes=[sm.b])
                else:
                    P.op("dve", lambda e: e.tensor_scalar(out=sm[:, 5:6], in0=sm[:, 4:5], scalar1=float(topk) - 0.5, scalar2=None, op0=ALU.is_ge), reads=[sm.b], writes=[sm.b])
                P.op("dve", lambda e: e.scalar_tensor_tensor(out=sm[:, 1:2], in0=sm[:, 5:6], scalar=dl[:, m:m + 1], in1=sm[:, 1:2], op0=ALU.mult, op1=ALU.add), reads=[sm.b, dl.b], writes=[sm.b])
                stt["m"] += 1

            def idxsel_end(stt):
                while stt["m"] < NITER:
                    bis_iter(stt)
                par, nkeys = stt["par"], stt["nkeys"]
                negm, sm = negm2[par], sm2[par]
                P.op("dve", lambda e: e.tensor_scalar(out=negm[:, 0:nkeys], in0=Isc[:, 0:nkeys], scalar1=sm[:, 1:2], scalar2=MNEG, op0=ALU.is_lt, op1=ALU.mult), reads=[Isc.b, sm.b], writes=[negm.b, negm_b2[par]])
                return par


            def attend(i, nkeys, par, col_sel, pending=None):
                NK = nkeys // 128
                negm, qbd_ = negm2[par], qbd[par]
                negm_rb = [negm.b, negm_b2[par]]
                for hq in range(2):
                    ps = PS[6 + hq]
                    P.op("pe", lambda e, ps=ps: e.matmul(ps[0:65, :], lhsT=zl[:, 0:65], rhs=zr[:, :], start=True, stop=False, skip_group_check=True), reads=[zl.b, zr.b], writes=[ps.b])
                steps = []
                for kt4 in range((NK + 3) // 4):
                    nk_here = min(4, NK - 4 * kt4)
                    for kk in range(nk_here):
                        for hq in range(2):
                            steps.append((kt4, nk_here, kk, hq, kk == 0 and hq == 0))
                state = {}

                def emit_qk(n):
                    kt4, nk_here, kk, hq, first = steps[n]
                    vb_ = Vb[kt4 % 2]
                    if first:
                        P.dma("sp", lambda e: e.dma_start(out=vb_[:, 0:nk_here, :], in_=v_scr[kt4 * 512:kt4 * 512 + nk_here * 128, :].rearrange("(k p) c -> p k c", p=128)),
                              reads=[b_vscr], writes=[vb_.b])
                    t128 = kt4 * 4 + kk
                    psL = PS[LRING[cnts["l"] % len(LRING)]]
                    cnts["l"] += 1
                    P.op("pe", lambda e: e.matmul(psL[:, 0:512], lhsT=negm[:, t128 * 128:(t128 + 1) * 128], rhs=ident4[:, :], start=True, stop=False),
                         reads=negm_rb + [ident4.b], writes=[psL.b])
                    for pp in range(2):
                        pair = 2 * hq + pp
                        P.op("pe", lambda e, pp=pp, pair=pair: e.matmul(psL[:, pp * 256:(pp + 1) * 256], lhsT=KT[:, pair, t128 * 128:(t128 + 1) * 128], rhs=qbd_[:, pair, :], start=False, stop=(pp == 1)),
                             reads=[KT.b, qbd_.b], writes=[psL.b])
                    state[n] = psL

                def emit_pv(n):
                    kt4, nk_here, kk, hq, first = steps[n]
                    vb_ = Vb[kt4 % 2]
                    t128 = kt4 * 4 + kk
                    psL = state.pop(n)
                    pb_ = Pb[cnts["p"] % len(Pb)]
                    cnts["p"] += 1
                    P.op("act", lambda e: e.activation(out=pb_[:], in_=psL[:, :], func=AF.Exp, scale=0.125), reads=[psL.b], writes=[pb_.b])
                    pso = PS[6 + hq]
                    for h4 in range(4):
                        h = hq * 4 + h4
                        P.op("pe", lambda e, h4=h4, h=h: e.matmul(pso[0:65, h4 * 128:(h4 + 1) * 128], lhsT=vb_[:, kk, h * 65:(h + 1) * 65], rhs=pb_[:, h4 * 128:(h4 + 1) * 128], start=False, stop=(t128 == NK - 1), skip_group_check=True),
                             reads=[vb_.b, pb_.b], writes=[pso.b])

                stride = max(1, len(steps) // NITER)
                for n0 in range(min(LA, len(steps))):
                    emit_qk(n0)
                for n in range(len(steps)):
                    if n + LA < len(steps):
                        emit_qk(n + LA)
                    emit_pv(n)
                    if pending is not None and n % stride == stride - 1 and pending["m"] < NITER:
                        bis_iter(pending)
                if pending is not None:
                    idxsel_end(pending)
                rb = [PS[6].b, PS[7].b]
                P.op("act", lambda e: e.activation(out=lnd[0:1, :], in_=psum[0:1, 6:8, :].rearrange("p a b -> p (a b)"), func=AF.Ln), reads=rb, writes=[lnd.b])
                P.op("act", lambda e: e.activation(out=rden[0:1, :], in_=lnd[0:1, :], func=AF.Exp, scale=-1.0), reads=[lnd.b], writes=[rden.b])
                for half in range(2):
                    ps = PS[half]
                    P.op("pe", lambda e, ps=ps, half=half: e.matmul(ps[0:65, :], lhsT=ones_f[0:1, 0:65], rhs=rden[0:1, half * 512:(half + 1) * 512], start=True, stop=True), reads=[ones_f.b, rden.b], writes=[ps.b])
                P.op("act", lambda e: e.activation(out=bcs[0:65, :], in_=psum[0:65, 0:2, :].rearrange("p a b -> p (a b)"), func=AF.Identity), reads=[PS[0].b, PS[1].b], writes=[bcs.b])
                P.op("dve", lambda e: e.tensor_tensor(out=mixa[0:65, :, :].rearrange("p h t -> p (h t)"), in0=psum[0:65, 6:8, :].rearrange("p a b -> p (a b)"), in1=bcs[0:65, :], op=ALU.mult), reads=rb + [bcs.b], writes=[mixa.b])
                if col_sel is None:
                    P.dma("pool", lambda e: e.dma_start(out=mixa_scr[i, :, :], in_=mixa[0:65, :, :].rearrange("p h t -> p (h t)")), reads=[mixa.b], writes=[b_mixa])
                else:
                    c0 = col_sel
                    P.dma("pool", lambda e: e.dma_start(out=mixa_scr[i, :, :].rearrange("p (h t) -> p h t", h=8)[:, :, c0:c0 + 32], in_=mixa[0:65, :, c0:c0 + 32]), reads=[mixa.b], writes=[b_mixa])

            if NI > 0:
                st_cur = idxsel(0, 512, cfg.TOPK_P, cm_p, 512)
                idxsel_end(st_cur)
            for i in range(NI):
                st_next = None
                if i + 1 < NI:
                    st_next = idxsel(i + 1, 512 * (i + 2), cfg.TOPK_P, cm_p, 512)
                attend(i, 512 * (i + 1), st_cur["par"], None, pending=st_next)
                st_cur = st_next

            for bq in range(4):
                for kt in range(PAST // 128):
                    f_, v_, ki_ = ckf[kt % 2], cvf[kt % 2], ckif[kt % 2]
                    rows = slice(kt * 128, (kt + 1) * 128)
                    P.dma("sp", lambda e, f_=f_, rows=rows, bq=bq: e.dma_start(out=f_[:], in_=cache_k[bq, rows, :]), writes=[f_.b])
                    P.dma("sp", lambda e, v_=v_, rows=rows, bq=bq: e.dma_start(out=v_[:], in_=cache_v[bq, rows, :]), writes=[v_.b])
                    P.dma("sp", lambda e, ki_=ki_, rows=rows, bq=bq: e.dma_start(out=ki_[:], in_=cache_ki[bq, rows, :]), writes=[ki_.b])
                    P.op("dve", lambda e, f_=f_: e.tensor_copy(out=ckb[:], in_=f_[:]), reads=[f_.b], writes=[ckb.b])
                    ps = PS[0]
                    pb = ps[:, :].bitcast(BF16)
                    for c in range(4):
                        P.op("pe", lambda e, c=c, pb=pb: e.transpose(out=pb[:, c * 128:(c + 1) * 128], in_=ckb[:, c * 128:(c + 1) * 128], identity=identb[:]), reads=[ckb.b, identb.b], writes=[ps.b])
                    P.op("act", lambda e, pb=pb, kt=kt: e.activation(out=KT[:, :, kt * 128:(kt + 1) * 128], in_=pb[:, 0:512].rearrange("p (c t) -> p c t", c=4), func=AF.Identity), reads=[ps.b], writes=[KT.b])
                    cv_ = cvb[kt % 2]
                    P.op("dve", lambda e, cv_=cv_, v_=v_: e.tensor_copy(out=cv_[:, :, 1:65], in_=v_[:].rearrange("p (h d) -> p h d", h=8)), reads=[v_.b], writes=[cv_.b])
                    P.dma("pool", lambda e, cv_=cv_, rows=rows: e.dma_start(out=v_scr[rows, :], in_=cv_[:].rearrange("p h d -> p (h d)")), reads=[cv_.b], writes=[b_vscr])
                    P.op("dve", lambda e, ki_=ki_: e.tensor_copy(out=ckib[:].rearrange("p (a d) -> p a d", a=2), in_=ki_[:, :].unsqueeze(1).broadcast_to([128, 2, 64])), reads=[ki_.b], writes=[ckib.b])
                    ps = PS[1]
                    pb = ps[:, :].bitcast(BF16)
                    P.op("pe", lambda e, pb=pb: e.transpose(out=pb[:, 0:128], in_=ckib[:], identity=identb[:]), reads=[ckib.b, identb.b], writes=[ps.b])
                    P.op("act", lambda e, pb=pb, kt=kt: e.activation(out=KIT[:, kt * 128:(kt + 1) * 128], in_=pb[:, 0:128], func=AF.Identity), reads=[ps.b], writes=[KIT.b])
                P.op("pool", lambda e: e.memset(KT[:, :, PAST:PAST + 128], 0.0), writes=[KT.b])
                P.op("pool", lambda e: e.memset(KIT[:, PAST:PAST + 128], 0.0), writes=[KIT.b])
                ps = PS[0]
                pb = ps[:, :].bitcast(BF16)
                for c in range(4):
                    P.op("pe", lambda e, c=c, pb=pb: e.transpose(out=pb[:, c * 128:(c + 1) * 128], in_=ksn_b[:, c * 128:(c + 1) * 128], identity=identb[:]), reads=[ksn_b.b, identb.b], writes=[ps.b])
                P.op("act", lambda e, pb=pb, bq=bq: e.activation(out=KT[:, :, PAST:PAST + 32], in_=pb[:, 0:512].rearrange("p (c t) -> p c t", c=4)[:, :, 32 * bq:32 * bq + 32], func=AF.Identity), reads=[ps.b], writes=[KT.b])
                ps = PS[1]
                pb = ps[:, :].bitcast(BF16)
                P.op("pe", lambda e, pb=pb: e.transpose(out=pb[:, 0:128], in_=kisn_b[:], identity=identb[:]), reads=[kisn_b.b, identb.b], writes=[ps.b])
                P.op("act", lambda e, pb=pb, bq=bq: e.activation(out=KIT[:, PAST:PAST + 32], in_=pb[:, 32 * bq:32 * bq + 32], func=AF.Identity), reads=[ps.b], writes=[KIT.b])
                P.dma("pool", lambda e, bq=bq: e.dma_start(out=v_scr[PAST:PAST + 32, :], in_=vsn_b[32 * bq:32 * bq + 32, :, :].rearrange("p h d -> p (h d)")), reads=[vsn_b.b], writes=[b_vscr])
                P.dma("pool", lambda e: e.dma_start(out=v_scr[PAST + 32:PAST + 128, :], in_=zv[0:96, :]), reads=[zv.b], writes=[b_vscr])
                st_s = idxsel(NI, SPAD, cfg.TOPK_S, cm_s, 128)
                idxsel_end(st_s)
                attend(NI, SPAD, st_s["par"], 32 * bq)
            run_block()
        mid.close()
        if cfg.stop == 6:
            return nc

        with ExitStack() as st:
            woa = sb(st, "woa", [65, 8, 1024], BF16)
            wog = sb(st, "wog", [128, 4, 1024], BF16)
            wf1 = sb(st, "wf1", [128, 8, 4096], BF16)
            wf2 = sb(st, "wf2", [128, 32, 1024], BF16)
            bcD = [sb(st, "bcD%d" % k, [128, D], F32) for k in range(4)]
            xd = sb(st, "xd", [128, D], F32)
            x1 = sb(st, "x1", [128, D], F32)
            tmpf = sb(st, "tmpfD", [128, D], F32)
            hb = sb(st, "hbD", [128, D], BF16)
            hT = sb(st, "hTD", [128, 8, 128], BF16)
            aT = sb(st, "aT", [128, 32, 128], BF16)
            rl2 = [sb(st, "rl%d" % i, [128, 512], F32) for i in range(2)]
            ab = [sb(st, "ab%d" % i, [128, 512], BF16) for i in range(2)]
            mxa = sb(st, "mxa", [65, 8, 128], BF16)
            mxg = sb(st, "mxg", [128, 4, 128], BF16)
            st1 = sb(st, "st1D", [128, 4], F32)

            P.op("pool", lambda e: e.memset(woa[0:1, :, :], 0.0), writes=[woa.b])
            for h in range(8):
                P.dma("sp", lambda e, h=h: e.dma_start(out=woa[1:65, h, :], in_=wo_b[h * 64:(h + 1) * 64, :]), reads=[b_wob], writes=[woa.b])
            P.dma("sp", lambda e: e.dma_start(out=wog[:], in_=wo_b[512:1024, :].rearrange("(g p) n -> p g n", p=128)), reads=[b_wob], writes=[wog.b])
            for c in range(8):
                P.dma("sp", lambda e, c=c: e.dma_start(out=wf1[:, c, :], in_=wf1_b[c * 128:(c + 1) * 128, :]), reads=[b_wf1b], writes=[wf1.b])
            for f4 in range(8):
                P.dma("sp", lambda e, f4=f4: e.dma_start(out=wf2[:, f4 * 4:(f4 + 1) * 4, :], in_=wf2_b[f4 * 512:(f4 + 1) * 512, :].rearrange("(f p) n -> p f n", p=128)), reads=[b_wf2b], writes=[wf2.b])

            def loadD(i):
                r0 = i * 128
                P.dma("sp", lambda e: e.dma_start(out=xd[:], in_=x_own[r0:r0 + 128, :]), writes=[xd.b])
                P.dma("sp", lambda e: e.dma_start(out=mxa[0:65, :, :].rearrange("p h t -> p (h t)"), in_=mixa_scr[i, :, :]), reads=[b_mixa], writes=[mxa.b])
                P.dma("sp", lambda e: e.dma_start(out=mxg[:].rearrange("p c t -> p (c t)"), in_=mixg_scr[i, :, :]), reads=[b_mixg], writes=[mxg.b])

            for i in range(NB1):
                s = 0 if i < NI else 1
                if i == 0 or i == NI:
                    for k in range(4):
                        P.dma("sp", lambda e, k=k, s=s: e.dma_start(out=bcD[k][:], in_=bc_scr[s, k, :, :]), reads=[b_bcscr], writes=[bcD[k].b])
                r0 = i * 128
                if i == 0:
                    loadD(0)
                for nt in range(2):
                    ps = PS[nt]
                    for h in range(8):
                        P.op("pe", lambda e, ps=ps, h=h, nt=nt: e.matmul(ps[:, :], lhsT=mxa[0:65, h, :], rhs=woa[0:65, h, nt * 512:(nt + 1) * 512], start=(h == 0), stop=False), reads=[mxa.b, woa.b], writes=[ps.b])
                    for g in range(4):
                        P.op("pe", lambda e, ps=ps, g=g, nt=nt: e.matmul(ps[:, :], lhsT=mxg[:, g, :], rhs=wog[:, g, nt * 512:(nt + 1) * 512], start=False, stop=(g == 3)), reads=[mxg.b, wog.b], writes=[ps.b])
                    P.op("dve", lambda e, ps=ps, nt=nt: e.tensor_tensor(out=tmpf[:, nt * 512:(nt + 1) * 512], in0=ps[:, :], in1=bcD[0][:, nt * 512:(nt + 1) * 512], op=ALU.mult), reads=[ps.b, bcD[0].b], writes=[tmpf.b])
                P.op("dve", lambda e: e.tensor_tensor(out=x1[:], in0=tmpf[:], in1=xd[:], op=ALU.add), reads=[tmpf.b, xd.b], writes=[x1.b])
                if i + 1 < NB1:
                    loadD(i + 1)
                P.op("act", lambda e: e.activation(out=hb[:], in_=x1[:], func=AF.Square, accum_out=st1[:, 0:1]), reads=[x1.b], writes=[hb.b, st1.b])
                P.op("act", lambda e: e.activation(out=st1[:, 1:2], in_=st1[:, 0:1], func=AF.Sqrt, bias=eps_t[:, 0:1], scale=1.0 / D), reads=[st1.b, eps_t.b], writes=[st1.b])
                P.op("dve", lambda e: e.reciprocal(out=st1[:, 2:3], in_=st1[:, 1:2]), reads=[st1.b], writes=[st1.b])
                P.op("dve", lambda e: e.scalar_tensor_tensor(out=tmpf[:], in0=x1[:], scalar=st1[:, 2:3], in1=bcD[1][:], op0=ALU.mult, op1=ALU.mult), reads=[x1.b, st1.b, bcD[1].b], writes=[tmpf.b])
                P.op("dve", lambda e: e.tensor_tensor(out=hb[:], in0=tmpf[:], in1=bcD[2][:], op=ALU.add), reads=[tmpf.b, bcD[2].b], writes=[hb.b])
                ps = PS[2]
                pb = ps[:, :].bitcast(BF16)
                for c in range(8):
                    P.op("pe", lambda e, c=c, pb=pb: e.transpose(out=pb[:, c * 128:(c + 1) * 128], in_=hb[:, c * 128:(c + 1) * 128], identity=identb[:]), reads=[hb.b, identb.b], writes=[ps.b])
                P.op("act", lambda e, pb=pb: e.activation(out=hT[:].rearrange("p c t -> p (c t)"), in_=pb, func=AF.Identity), reads=[ps.b], writes=[hT.b])
                def ff1_mm(nt8):
                    ps = PS[3 + nt8 % 2]
                    for c in range(8):
                        P.op("pe", lambda e, ps=ps, c=c: e.matmul(ps[:, :], lhsT=hT[:, c, :], rhs=wf1[:, c, nt8 * 512:(nt8 + 1) * 512], start=(c == 0), stop=(c == 7)), reads=[hT.b, wf1.b], writes=[ps.b])

                def ff1_post(nt8):
                    ps = PS[3 + nt8 % 2]
                    rl_ = rl2[nt8 % 2]
                    P.op("act", lambda e: e.activation(out=rl_[:], in_=ps[:, :], func=AF.Relu), reads=[ps.b], writes=[rl_.b])
                    ab_ = ab[nt8 % 2]
                    P.op("dve", lambda e: e.tensor_tensor(out=ab_[:], in0=rl_[:], in1=rl_[:], op=ALU.mult), reads=[rl_.b], writes=[ab_.b])
                    pt = PS[7]
                    ptb = pt[:, :].bitcast(BF16)
                    for fc in range(4):
                        P.op("pe", lambda e, fc=fc: e.transpose(out=ptb[:, fc * 128:(fc + 1) * 128], in_=ab_[:, fc * 128:(fc + 1) * 128], identity=identb[:]), reads=[ab_.b, identb.b], writes=[pt.b])
                    P.op("act", lambda e: e.activation(out=aT[:, nt8 * 4:(nt8 + 1) * 4, :].rearrange("p c t -> p (c t)"), in_=ptb[:, 0:512], func=AF.Identity), reads=[pt.b], writes=[aT.b])

                ff1_mm(0)
                for nt8 in range(8):
                    if nt8 + 1 < 8:
                        ff1_mm(nt8 + 1)
                    ff1_post(nt8)
                for nt in range(2):
                    ps = PS[5 + nt]
                    for f in range(32):
                        P.op("pe", lambda e, ps=ps, f=f, nt=nt: e.matmul(ps[:, :], lhsT=aT[:, f, :], rhs=wf2[:, f, nt * 512:(nt + 1) * 512], start=(f == 0), stop=(f == 31)), reads=[aT.b, wf2.b], writes=[ps.b])
                    P.op("dve", lambda e, ps=ps, nt=nt: e.tensor_tensor(out=tmpf[:, nt * 512:(nt + 1) * 512], in0=ps[:, :], in1=bcD[3][:, nt * 512:(nt + 1) * 512], op=ALU.mult), reads=[ps.b, bcD[3].b], writes=[tmpf.b])
                P.op("dve", lambda e: e.tensor_tensor(out=tmpf[:], in0=tmpf[:], in1=x1[:], op=ALU.add), reads=[tmpf.b, x1.b], writes=[tmpf.b])
                P.dma("pool", lambda e, r0=r0: e.dma_start(out=y_own[r0:r0 + 128, :], in_=tmpf[:]), reads=[tmpf.b])
            run_block()

        return nc


def _rope_table(pos):
    half = 32
    inv = np.power(np.float32(10000.0), -np.arange(half, dtype=np.float32) / np.float32(half)).astype(np.float32)
    ang = pos.astype(np.float32)[:, None] * inv[None, :]
    return np.concatenate([np.cos(ang), np.sin(ang)], axis=1).astype(np.float32)


def make_in_maps(cfg, inp):
    SEQ, PAST, NI, NB1 = cfg.SEQ, cfg.PAST, cfg.NI, cfg.NB1
    f = lambda a: np.ascontiguousarray(np.asarray(a, dtype=np.float32))
    xp, xs = f(inp["x_prompt"]), f(inp["x_sample"])
    cp, cs = f(inp["c_prompt"]), f(inp["c_sample"])
    ck, cv, cki = f(inp["cache_k"])[0], f(inp["cache_v"])[0], f(inp["cache_kidx"])[0]
    shared = {
        "w_ada": f(inp["w_ada"])[0], "b_ada": f(inp["b_ada"]), "norm1_g": f(inp["norm1_g"]), "norm2_g": f(inp["norm2_g"]),
        "w_in": f(inp["w_in"])[0], "q_norm_g": f(inp["q_norm_g"]), "k_norm_g": f(inp["k_norm_g"]),
        "gmlp_ln_g": f(inp["gmlp_ln_g"]), "gmlp_ln_b": f(inp["gmlp_ln_b"]), "gmlp_ws": f(inp["gmlp_ws"])[0],
        "gmlp_bs": f(inp["gmlp_bs"])[0], "w_out": f(inp["w_out"])[0], "w_ff1": f(inp["w_ff1"])[0], "w_ff2": f(inp["w_ff2"])[0],
        "ident": np.eye(128, dtype=np.float32),
        "pow2": np.tile((2.0 ** -(np.arange(cfg.NITER) + 1.0)).astype(np.float32)[None, :], (128, 1)),
    }
    sel = np.zeros((2, 5, 128), np.float32)
    sel[0, 0, :] = 1.0
    for p in range(128):
        sel[1, 1 + p // 32, p] = 1.0
    shared["sel"] = sel
    tril = np.zeros((2, 128, 128), np.float32)
    tril[0] = np.tril(np.ones((128, 128), np.float32))
    for q in range(4):
        tril[1, q * 32:(q + 1) * 32, q * 32:(q + 1) * 32] = np.tril(np.ones((32, 32), np.float32))
    shared["tril"] = tril
    cm_s = np.full((128, 128), NEG, np.float32)
    cm_s[:, 0:32] = 0.0
    shared["cmask_s"] = cm_s
    rope_seq = _rope_table(np.arange(SEQ))
    maps = []
    for c in range(8):
        b, j = c // 4, c % 4
        blks = [4 * i + j for i in range(NI)]
        rows = np.concatenate([np.arange(bl * 128, (bl + 1) * 128) for bl in blks])
        x_own = np.concatenate([xp[b][rows], xs[4 * c:4 * c + 4].reshape(128, D)], axis=0)
        pos_own = np.concatenate([rows, PAST + (np.arange(128) % 32)])
        tl = 128 * j + np.arange(128)
        lim = (tl // 64 + 1) * 64
        cm = np.where(np.arange(512)[None, :] < lim[:, None], 0.0, NEG).astype(np.float32)
        m = dict(shared)
        m.update({
            "x_seq": xp[b], "x_own": np.ascontiguousarray(x_own), "c_all": np.ascontiguousarray(np.concatenate([cp[b:b + 1], cs[4 * c:4 * c + 4]], axis=0)),
            "rope_seq": rope_seq, "rope_own": _rope_table(pos_own),
            "cache_k": np.ascontiguousarray(ck[4 * c:4 * c + 4].reshape(4, PAST, 512)),
            "cache_v": np.ascontiguousarray(cv[4 * c:4 * c + 4].reshape(4, PAST, 512)),
            "cache_ki": np.ascontiguousarray(cki[4 * c:4 * c + 4]),
            "cmask_p": cm,
        })
        maps.append(m)
    return maps


def assemble(cfg, results):
    SEQ, NI = cfg.SEQ, cfg.NI
    yp = np.zeros((2, SEQ, D), np.float32)
    ys = np.zeros((32, 32, D), np.float32)
    kp = np.zeros((1, 2, SEQ, 8, 64), np.float32)
    vp = np.zeros((1, 2, SEQ, 8, 64), np.float32)
    kip = np.zeros((1, 2, SEQ, 64), np.float32)
    ks = np.zeros((1, 32, 32, 8, 64), np.float32)
    vs = np.zeros((1, 32, 32, 8, 64), np.float32)
    kis = np.zeros((1, 32, 32, 64), np.float32)
    gvs = np.zeros((1, 32, 32, 512), np.float32)
    for c in range(8):
        r = results[c]
        b, j = c // 4, c % 4
        for i in range(NI):
            bl = 4 * i + j
            sl = slice(bl * 128, (bl + 1) * 128)
            o = slice(i * 128, (i + 1) * 128)
            yp[b, sl] = r["y_own"][o]
            kp[0, b, sl] = r["k_own"][o].reshape(128, 8, 64)
            vp[0, b, sl] = r["v_own"][o].reshape(128, 8, 64)
            kip[0, b, sl] = r["ki_own"][o]
        o = slice(NI * 128, (NI + 1) * 128)
        ys[4 * c:4 * c + 4] = r["y_own"][o].reshape(4, 32, D)
        ks[0, 4 * c:4 * c + 4] = r["k_own"][o].reshape(4, 32, 8, 64)
        vs[0, 4 * c:4 * c + 4] = r["v_own"][o].reshape(4, 32, 8, 64)
        kis[0, 4 * c:4 * c + 4] = r["ki_own"][o].reshape(4, 32, 64)
        gvs[0, 4 * c:4 * c + 4] = r["gv_own"].reshape(4, 32, 512)
    return (yp, ys, kp, vp, kip, ks, vs, kis, gvs)


_CACHE = {}


def kernel(**inputs):
    cfg = Cfg(SEQ=int(np.asarray(inputs["x_prompt"]).shape[1]), PAST=int(np.asarray(inputs["cache_k"]).shape[2]))
    import os
    cfg.stop = int(os.environ.get("KSTOP", "99"))
    cfg.sub = int(os.environ.get("KSUB", "99"))
    key = (cfg.SEQ, cfg.PAST)
    if key not in _CACHE:
        _CACHE[key] = build(cfg)
    nc = _CACHE[key]
    maps = make_in_maps(cfg, inputs)
    res = run_bass_kernel_spmd(nc, maps, core_ids=list(range(8)))
    return assemble(cfg, res.results)
```

```python
import numpy as np
import concourse.bass as bass
import concourse.mybir as mybir
from concourse.bass_utils import run_bass_kernel_spmd

F32 = mybir.dt.float32
BF16 = mybir.dt.bfloat16
U32 = mybir.dt.uint32
AF = mybir.ActivationFunctionType
ALU = mybir.AluOpType
AX = mybir.AxisListType


class Buf:
    __slots__ = ("name", "w", "r", "psum")

    def __init__(self, name):
        self.name = name
        self.w = None
        self.r = []
        self.psum = False


class Prog:
    ENGS = ("pe", "act", "dve", "pool", "sp")
    NDMA = 8

    def __init__(self, nc, stack):
        self.nc = nc
        self.ops = {e: [] for e in self.ENGS}
        self.cnt = {e: 0 for e in self.ENGS}
        self.sem = {}
        for e in self.ENGS:
            self.sem[e] = stack.enter_context(nc.semaphore("s_" + e))
        self.dsem = {}
        self.dcnt = {}
        for q in ("sp", "pool", "act"):
            self.dsem[q] = [stack.enter_context(nc.semaphore("d_%s%d" % (q, i))) for i in range(self.NDMA)]
            self.dcnt[q] = 0
        self.seen = {e: {} for e in self.ENGS}
        self.out_waits = []

    def _semobj(self, key):
        if isinstance(key, str):
            return self.sem[key]
        q, i = key
        return self.dsem[q][i]

    def _need(self, eng, waits, key, val):
        if self.seen[eng].get(key, 0) >= val:
            return
        self.seen[eng][key] = val
        waits.append((key, val))

    def _deps(self, eng, reads, writes, waits, same_engine_raw=True, strict=False):
        for b in reads:
            if b.w is not None:
                k, v = b.w
                if k != eng or same_engine_raw or strict:
                    self._need(eng, waits, k, v)
            if b.psum:
                for (k, v) in b.r:
                    if k != eng:
                        self._need(eng, waits, k, v)
        for b in writes:
            if b.w is not None:
                k, v = b.w
                if k != eng or strict:
                    self._need(eng, waits, k, v)
            for (k, v) in b.r:
                if k != eng or strict:
                    self._need(eng, waits, k, v)

    def op(self, eng, fn, reads=(), writes=(), raw_same=True):
        waits = []
        self._deps(eng, reads, writes, waits, strict=(eng != "pe"))
        self.cnt[eng] += 1
        v = self.cnt[eng]
        self.ops[eng].append((waits, fn, (eng, v)))
        for b in reads:
            b.r.append((eng, v))
        for b in writes:
            b.w = (eng, v)
            b.r = []
        return v

    def dma(self, q, fn, reads=(), writes=()):
        waits = []
        self._deps(q, reads, writes, waits, strict=True)
        n = self.dcnt[q]
        self.dcnt[q] += 1
        slot = n % self.NDMA
        key = (q, slot)
        prev = 16 * (n // self.NDMA)
        if prev > 0:
            self._need(q, waits, key, prev)
        val = prev + 16
        self.ops[q].append((waits, fn, (key, 16)))
        for b in reads:
            b.r.append((key, val))
        for b in writes:
            b.w = (key, val)
            b.r = []
        return (key, val)

    def barrier(self):
        targets = [(e, self.cnt[e]) for e in self.ENGS if self.cnt[e] > 0]
        for q in self.dcnt:
            n = self.dcnt[q]
            for slot in range(min(n, self.NDMA)):
                uses = (n - slot + self.NDMA - 1) // self.NDMA
                targets.append(((q, slot), 16 * uses))
        for e in self.ENGS:
            waits = []
            for (k, v) in targets:
                if k != e:
                    self._need(e, waits, k, v)
            if waits:
                self.ops[e].append((waits, None, None))

    def emit(self, block):
        nc = self.nc
        prog = self

        def run(engine, name):
            for (waits, fn, inc) in prog.ops[name]:
                for (k, v) in waits:
                    engine.wait_ge(prog._semobj(k), v)
                if fn is None:
                    continue
                ins = fn(engine)
                key, amt = inc
                if isinstance(key, str):
                    ins.then_inc(prog.sem[key], 1)
                else:
                    ins.then_inc(prog._semobj(key), 16)
            prog.ops[name] = []

        @block.tensor
        def _(t):
            run(t, "pe")

        @block.scalar
        def _(s):
            run(s, "act")

        @block.vector
        def _(v):
            run(v, "dve")

        @block.gpsimd
        def _(g):
            run(g, "pool")

        @block.sync
        def _(s):
            run(s, "sp")


from contextlib import ExitStack

D = 1024
INW = 2884
EPS = 1e-6
NEG = -1.0e30
MNEG = -30000.0


class T:
    def __init__(self, t, name):
        self.t = t
        self.b = Buf(name)

    def __getitem__(self, k):
        return self.t[k]


class Cfg:
    def __init__(self, SEQ=8192, PAST=2048, NITER=14):
        self.SEQ = SEQ
        self.PAST = PAST
        self.NITER = NITER
        self.NBLK = SEQ // 128
        self.NI = self.NBLK // 4
        self.NB1 = self.NI + 1
        self.TOPK_P = min(256, SEQ // 4)
        self.TOPK_S = min(256, (PAST + 32) // 4)
        self.NKS = PAST // 128 + 1
        self.SPAD = self.NKS * 128
        self.KMAX = max(SEQ, self.SPAD)
        self.stop = 99
        self.sub = 99


def build(cfg):
    SEQ, PAST, NITER = cfg.SEQ, cfg.PAST, cfg.NITER
    NBLK, NI, NB1 = cfg.NBLK, cfg.NI, cfg.NB1
    NKS, SPAD, KMAX = cfg.NKS, cfg.SPAD, cfg.KMAX
    nc = bass.Bass("TRN2", target_bir_lowering=False)

    def din(name, shape, dt=F32):
        return nc.dram_tensor(name, list(shape), dt, kind="ExternalInput").ap()

    def dout(name, shape, dt=F32):
        return nc.dram_tensor(name, list(shape), dt, kind="ExternalOutput").ap()

    x_seq = din("x_seq", [SEQ, D])
    x_own = din("x_own", [NB1 * 128, D])
    c_all = din("c_all", [5, D])
    rope_seq = din("rope_seq", [SEQ, 64])
    rope_own = din("rope_own", [NB1 * 128, 64])
    cache_k = din("cache_k", [4, PAST, 512])
    cache_v = din("cache_v", [4, PAST, 512])
    cache_ki = din("cache_ki", [4, PAST, 64])
    w_ada = din("w_ada", [D, 6 * D])
    b_ada = din("b_ada", [1, 6 * D])
    norm1_g = din("norm1_g", [1, D])
    norm2_g = din("norm2_g", [1, D])
    w_in = din("w_in", [D, INW])
    q_norm_g = din("q_norm_g", [1, 64])
    k_norm_g = din("k_norm_g", [1, 64])
    ln_g = din("gmlp_ln_g", [1, 512])
    ln_b = din("gmlp_ln_b", [1, 512])
    gws = din("gmlp_ws", [4, 128, 128])
    gbs = din("gmlp_bs", [4, 128])
    w_out = din("w_out", [1024, 1024])
    w_ff1 = din("w_ff1", [1024, 4096])
    w_ff2 = din("w_ff2", [4096, 1024])
    ident_d = din("ident", [128, 128])
    sel_d = din("sel", [2, 5, 128])
    cmask_p_d = din("cmask_p", [128, 512])
    cmask_s_d = din("cmask_s", [128, 128])
    tril_d = din("tril", [2, 128, 128])
    pow2_d = din("pow2", [128, NITER])

    y_own = dout("y_own", [NB1 * 128, D])
    k_own = dout("k_own", [NB1 * 128, 512])
    v_own = dout("v_own", [NB1 * 128, 512])
    ki_own = dout("ki_own", [NB1 * 128, 64])
    gv_own = dout("gv_own", [128, 512])

    v_scr = nc.dram_tensor("v_scr", [KMAX, 520], BF16).ap()
    mixa_scr = nc.dram_tensor("mixa_scr", [NB1, 65, 1024], BF16).ap()
    mixg_scr = nc.dram_tensor("mixg_scr", [NB1, 128, 512], BF16).ap()
    bc_scr = nc.dram_tensor("bc_scr", [2, 6, 128, 1024], F32).ap()
    win_b = nc.dram_tensor("win_b", [1024, INW], BF16).ap()
    b_winb = Buf("win_b")
    wf1_b = nc.dram_tensor("wf1_b", [1024, 4096], BF16).ap()
    wf2_b = nc.dram_tensor("wf2_b", [4096, 1024], BF16).ap()
    wo_b = nc.dram_tensor("wo_b", [1024, 1024], BF16).ap()
    b_wf1b = Buf("wf1_b")
    b_wf2b = Buf("wf2_b")
    b_wob = Buf("wo_b")
    b_vscr = Buf("v_scr")
    b_mixa = Buf("mixa_scr")
    b_mixg = Buf("mixg_scr")
    b_bcscr = Buf("bc_scr")

    with ExitStack() as outer:
        P = Prog(nc, outer)

        def sb(st, name, shape, dt=F32):
            return T(st.enter_context(nc.sbuf_tensor(name, list(shape), dt)), name)

        psum = outer.enter_context(nc.psum_tensor("psum", [128, 8, 512], F32))
        PS = [T(psum[:, k, :], "ps%d" % k) for k in range(8)]
        for p_ in PS:
            p_.b.psum = True

        def run_block():
            P.barrier()
            with nc.Block() as blk:
                P.emit(blk)

        identb = sb(outer, "identb", [128, 128], BF16)
        ones_f = sb(outer, "ones_f", [128, 128], F32)
        eps_t = sb(outer, "eps_t", [128, 1], F32)
        wis = sb(outer, "wis", [128, NB1, 4], F32)
        ksn_b = sb(outer, "ksn_b", [128, 512], BF16)
        kisn_b = sb(outer, "kisn_b", [128, 128], BF16)
        vsn_b = sb(outer, "vsn_b", [128, 8, 65], BF16)
        gk_bc = sb(outer, "gk_bc", [128, 64], F32)
        gq_bc = sb(outer, "gq_bc", [128, 64], F32)
        mid = ExitStack()
        ident4 = sb(outer, "ident4", [128, 512], BF16)

        P.dma("pool", lambda e: e.dma_start(out=identb[:], in_=ident_d[:, :]), writes=[identb.b])
        for r4 in range(4):
            P.dma("pool", lambda e, r4=r4: e.dma_start(out=ident4[:, r4 * 128:(r4 + 1) * 128], in_=ident_d[:, :]), writes=[ident4.b])
        P.op("dve", lambda e: e.memset(ones_f[:], 1.0), writes=[ones_f.b])
        P.op("dve", lambda e: e.memset(eps_t[:], EPS), writes=[eps_t.b])
        P.dma("sp", lambda e: e.dma_start(out=gk_bc[:], in_=k_norm_g[0:1, :].partition_broadcast(128)), writes=[gk_bc.b])
        P.dma("sp", lambda e: e.dma_start(out=gq_bc[:], in_=q_norm_g[0:1, :].partition_broadcast(128)), writes=[gq_bc.b])

        with ExitStack() as st:
            cT = sb(st, "cT", [128, 8, 5], F32)
            sT = sb(st, "sT", [128, 8, 5], F32)
            mod5 = sb(st, "mod5", [5, 6 * D], F32)
            bad = [sb(st, "bad%d" % i, [5, 512], F32) for i in range(2)]
            wa = [sb(st, "wa%d" % i, [128, 8, 512], F32) for i in range(2)]
            sel_t = sb(st, "sel_t", [5, 2, 128], F32)
            ng = [sb(st, "ng%d" % i, [128, D], F32) for i in range(2)]
            bct = [sb(st, "bct%d" % i, [128, D], F32) for i in range(2)]
            for r in range(5):
                P.dma("sp", lambda e, r=r: e.dma_start(out=cT[:, :, r], in_=c_all[r, :].rearrange("(c p) -> p c", p=128), allow_slow_non_contiguous=True), writes=[cT.b])
            P.dma("sp", lambda e: e.dma_start(out=sel_t[:], in_=sel_d.rearrange("s r p -> r s p")), writes=[sel_t.b])
            P.dma("sp", lambda e: e.dma_start(out=ng[0][:], in_=norm1_g[0:1, :].partition_broadcast(128)), writes=[ng[0].b])
            P.dma("sp", lambda e: e.dma_start(out=ng[1][:], in_=norm2_g[0:1, :].partition_broadcast(128)), writes=[ng[1].b])
            for c in range(8):
                P.dma("pool", lambda e, c=c: e.dma_start(out=win_b[c * 128:(c + 1) * 128, 0:1442], in_=w_in[c * 128:(c + 1) * 128, 0:1442]), writes=[b_winb])
                P.dma("pool", lambda e, c=c: e.dma_start(out=win_b[c * 128:(c + 1) * 128, 1442:INW], in_=w_in[c * 128:(c + 1) * 128, 1442:INW]), writes=[b_winb])
            P.op("act", lambda e: e.activation(out=sT[:], in_=cT[:], func=AF.Silu), reads=[cT.b], writes=[sT.b])
            for nt in range(12):
                w = wa[nt % 2]
                P.dma("sp", lambda e, w=w, nt=nt: e.dma_start(out=w[:], in_=w_ada[:, nt * 512:(nt + 1) * 512].rearrange("(c p) n -> p c n", p=128)), writes=[w.b])
                bd = bad[nt % 2]
                P.dma("sp", lambda e, bd=bd, nt=nt: e.dma_start(out=bd[:], in_=b_ada[0:1, nt * 512:(nt + 1) * 512].partition_broadcast(5)), writes=[bd.b])
                ps = PS[nt % 2]
                for c in range(8):
                    P.op("pe", lambda e, w=w, c=c, ps=ps: e.matmul(ps[0:5, :], lhsT=sT[:, c, :], rhs=w[:, c, :], start=(c == 0), stop=(c == 7)),
                         reads=[sT.b, w.b], writes=[ps.b])
                P.op("dve", lambda e, ps=ps, nt=nt, bd=bd: e.tensor_tensor(out=mod5[0:5, nt * 512:(nt + 1) * 512], in0=ps[0:5, :], in1=bd[0:5, :], op=ALU.add),
                     reads=[ps.b, bd.b], writes=[mod5.b])
            for s in range(2):
                for k in range(6):
                    for half in range(2):
                        ps = PS[2 + half]
                        P.op("pe", lambda e, ps=ps, s=s, k=k, half=half: e.matmul(ps[:, :], lhsT=sel_t[0:5, s, :], rhs=mod5[0:5, k * D + half * 512:k * D + half * 512 + 512], start=True, stop=True),
                             reads=[sel_t.b, mod5.b], writes=[ps.b])
                    pv = psum[:, 2:4, :].rearrange("p a b -> p (a b)")
                    rb = [PS[2].b, PS[3].b]
                    if k in (1, 4):
                        dst = bct[s]
                        g = ng[0] if k == 1 else ng[1]
                        P.op("dve", lambda e, dst=dst, g=g: e.scalar_tensor_tensor(out=dst[:], in0=pv, scalar=1.0, in1=g[:], op0=ALU.add, op1=ALU.mult),
                             reads=rb + [g.b], writes=[dst.b])
                    else:
                        dst = bct[s]
                        P.op("act", lambda e, dst=dst: e.activation(out=dst[:], in_=pv, func=AF.Identity), reads=rb, writes=[dst.b])
                    if True:
                        slot = {2: 0, 3: 2, 4: 1, 5: 3, 1: 4, 0: 5}[k]
                        P.dma("pool", lambda e, s=s, slot=slot, dst=dst: e.dma_start(out=bc_scr[s, slot, :, :], in_=dst[:]), reads=[dst.b], writes=[b_bcscr])
            run_block()
        if cfg.stop == 0:
            return nc

        KT = sb(mid, "KT", [128, 4, KMAX], BF16)
        KIT = sb(mid, "KIT", [128, KMAX], BF16)
        qt_scr = nc.dram_tensor("qt_scr", [NB1, 128, 512], BF16).ap()
        qit_scr = nc.dram_tensor("qit_scr", [NB1, 128, 256], BF16).ap()
        b_qt = Buf("qt_scr")
        b_qit = Buf("qit_scr")

        precast = []
        for r in range(8):
            precast.append(lambda r=r: P.dma("pool", lambda e: e.dma_start(out=wo_b[r * 128:(r + 1) * 128, :], in_=w_out[r * 128:(r + 1) * 128, :]), writes=[b_wob]))
        for r in range(8):
            for q in range(4):
                precast.append(lambda r=r, q=q: P.dma("pool", lambda e: e.dma_start(out=wf1_b[r * 128:(r + 1) * 128, q * 1024:(q + 1) * 1024], in_=w_ff1[r * 128:(r + 1) * 128, q * 1024:(q + 1) * 1024]), writes=[b_wf1b]))
        for r in range(32):
            precast.append(lambda r=r: P.dma("pool", lambda e: e.dma_start(out=wf2_b[r * 128:(r + 1) * 128, :], in_=w_ff2[r * 128:(r + 1) * 128, :]), writes=[b_wf2b]))

        with ExitStack() as st:
            Wb = sb(st, "Wb", [128, 8, INW], BF16)
            xt = [sb(st, "xt%d" % i, [128, D], F32) for i in range(2)]
            rp = [sb(st, "rp%d" % i, [128, 64], F32) for i in range(4)]
            tmpf2 = [sb(st, "tmpf%d" % i, [128, D], F32) for i in range(2)]
            hb2 = [sb(st, "hb%d" % i, [128, D], BF16) for i in range(2)]
            hT2 = [sb(st, "hT%d" % i, [128, 8, 128], BF16) for i in range(2)]
            st12 = [sb(st, "st1%d" % i, [128, 4], F32) for i in range(2)]
            sq = sb(st, "sq", [128, 512], F32)
            kg = sb(st, "kg", [128, 512], F32)
            ta = sb(st, "ta", [128, 512], F32)
            tb = sb(st, "tb", [128, 512], F32)
            kr = sb(st, "kr", [128, 512], F32)
            kn = [sb(st, "kn0", [128, 512], F32)] * 2
            knb = sb(st, "knb", [128, 512], BF16)
            st8 = sb(st, "st8", [128, 3, 8], F32)
            vb = [sb(st, "vb%d" % i, [128, 8, 65], BF16) for i in range(2)]
            vf = [sb(st, "vf0", [128, 512], F32)] * 2
            dt_ = sb(st, "dt_", [128, 324], F32)
            kif = [sb(st, "kif%d" % i, [128, 64], F32) for i in range(2)]
            kib = sb(st, "kib", [128, 128], BF16)
            kiraw = sb(st, "kiraw", [128, 64], F32)
            qib = sb(st, "qib", [128, 256], BF16)
            ug = sb(st, "ug", [128, 512], F32)
            vn = sb(st, "vn", [128, 512], F32)
            vnb = sb(st, "vnb", [128, 512], BF16)
            gmb = sb(st, "gmb", [128, 512], BF16)
            mgT = sb(st, "mgT", [128, 512], BF16)
            bnst = sb(st, "bnst", [128, 8], F32)
            lng_bc = sb(st, "lng_bc", [128, 512], F32)
            lnb_bc = sb(st, "lnb_bc", [128, 512], F32)
            WsT = [sb(st, "WsT%d" % s, [128, 4, 128], BF16) for s in range(2)]
            bs_t = [sb(st, "bs_t%d" % s, [128, 4], F32) for s in range(2)]
            wsn = sb(st, "wsn", [128, 128], F32)
            wsb = sb(st, "wsb", [128, 128], BF16)
            tril_t = sb(st, "tril_t", [128, 2, 128], F32)
            qts = sb(st, "qts", [128, 4, 128], BF16)
            G1 = [sb(st, "G1_0", [128, D], F32)]
            S1 = [sb(st, "S1_0", [128, D], F32)]
            gx = sb(st, "gx", [128, 512], F32)
            gs = sb(st, "gs", [128, 512], F32)
            P.dma("sp", lambda e: e.dma_start(out=G1[0][:], in_=bc_scr[0, 4, :, :]), reads=[b_bcscr], writes=[G1[0].b])
            P.dma("sp", lambda e: e.dma_start(out=S1[0][:], in_=bc_scr[0, 5, :, :]), reads=[b_bcscr], writes=[S1[0].b])
            qits = sb(st, "qits", [128, 2, 128], BF16)

            for c in range(8):
                P.dma("sp", lambda e, c=c: e.dma_start(out=Wb[:, c, :], in_=win_b[c * 128:(c + 1) * 128, :]), reads=[b_winb], writes=[Wb.b])
            P.dma("sp", lambda e: e.dma_start(out=lng_bc[:], in_=ln_g[0:1, :].partition_broadcast(128)), writes=[lng_bc.b])
            P.dma("sp", lambda e: e.dma_start(out=lnb_bc[:], in_=ln_b[0:1, :].partition_broadcast(128)), writes=[lnb_bc.b])
            P.dma("sp", lambda e: e.dma_start(out=tril_t[:], in_=tril_d.rearrange("s p q -> p s q")), writes=[tril_t.b])
            for v_ in vb:
                P.op("pool", lambda e, v_=v_: e.memset(v_[:], 1.0), writes=[v_.b])
            P.op("pool", lambda e: e.memset(vsn_b[:], 1.0), writes=[vsn_b.b])
            P.dma("sp", lambda e: e.dma_start(out=bs_t[0][:], in_=gbs.rearrange("g t -> t g"), allow_slow_non_contiguous=True), writes=[bs_t[0].b])
            for q4 in range(4):
                P.dma("sp", lambda e, q4=q4: e.dma_start(out=bs_t[1][q4 * 32:(q4 + 1) * 32, :], in_=gbs[:, 0:32].rearrange("g t -> t g"), allow_slow_non_contiguous=True), writes=[bs_t[1].b])
            for s in range(2):
                for g in range(4):
                    if s == 0:
                        P.dma("sp", lambda e, g=g: e.dma_start(out=wsn[:], in_=gws[g, :, :]), writes=[wsn.b])
                    else:
                        P.op("dve", lambda e: e.memset(wsn[:], 0.0), writes=[wsn.b])
                        for q4 in range(4):
                            P.dma("sp", lambda e, g=g, q4=q4: e.dma_start(out=wsn[q4 * 32:(q4 + 1) * 32, q4 * 32:(q4 + 1) * 32], in_=gws[g, 0:32, 0:32]), writes=[wsn.b])
                    P.op("dve", lambda e, s=s: e.tensor_tensor(out=wsb[:], in0=wsn[:], in1=tril_t[:, s, :], op=ALU.mult), reads=[wsn.b, tril_t.b], writes=[wsb.b])
                    ps = PS[7]
                    pb = ps[:, :].bitcast(BF16)
                    P.op("pe", lambda e, pb=pb: e.transpose(out=pb[:, 0:128], in_=wsb[:], identity=identb[:]), reads=[wsb.b, identb.b], writes=[ps.b])
                    P.op("act", lambda e, pb=pb, s=s, g=g: e.activation(out=WsT[s][:, g, :], in_=pb[:, 0:128], func=AF.Identity), reads=[ps.b], writes=[WsT[s].b])

            def rope4(eng, src, dst, H, rpt):
                n = H * 64
                cosb = rpt[:, 0:32].unsqueeze(1).unsqueeze(1).broadcast_to([128, H, 2, 32])
                sinb = rpt[:, 32:64].unsqueeze(1).unsqueeze(1).broadcast_to([128, H, 2, 32])
                s4 = src[:, 0:n].rearrange("p (h t d) -> p h t d", h=H, t=2)
                a4 = ta[:, 0:n].rearrange("p (h t d) -> p h t d", h=H, t=2)
                b4 = tb[:, 0:n].rearrange("p (h t d) -> p h t d", h=H, t=2)
                d4 = dst[:, 0:n].rearrange("p (h t d) -> p h t d", h=H, t=2)
                P.op(eng, lambda e: e.tensor_tensor(out=a4, in0=s4, in1=cosb, op=ALU.mult), reads=[src.b, rpt.b], writes=[ta.b])
                P.op(eng, lambda e: e.tensor_tensor(out=b4, in0=s4, in1=sinb, op=ALU.mult), reads=[src.b, rpt.b], writes=[tb.b])
                P.op(eng, lambda e: e.tensor_tensor(out=d4[:, :, 0, :], in0=a4[:, :, 0, :], in1=b4[:, :, 1, :], op=ALU.subtract), reads=[ta.b, tb.b], writes=[dst.b])
                P.op(eng, lambda e: e.tensor_tensor(out=d4[:, :, 1, :], in0=a4[:, :, 1, :], in1=b4[:, :, 0, :], op=ALU.add), reads=[ta.b, tb.b], writes=[dst.b])

            def norm_to_hT(xtile, Gt, St, par):
                tmpf, hb, hT, st1 = tmpf2[par], hb2[par], hT2[par], st12[par]
                P.op("act", lambda e: e.activation(out=hb[:], in_=xtile[:], func=AF.Square, accum_out=st1[:, 0:1]), reads=[xtile.b], writes=[hb.b, st1.b])
                P.op("act", lambda e: e.activation(out=st1[:, 1:2], in_=st1[:, 0:1], func=AF.Sqrt, bias=eps_t[:, 0:1], scale=1.0 / D), reads=[st1.b, eps_t.b], writes=[st1.b])
                P.op("dve", lambda e: e.reciprocal(out=st1[:, 2:3], in_=st1[:, 1:2]), reads=[st1.b], writes=[st1.b])
                P.op("dve", lambda e: e.scalar_tensor_tensor(out=tmpf[:], in0=xtile[:], scalar=st1[:, 2:3], in1=Gt[:], op0=ALU.mult, op1=ALU.mult), reads=[xtile.b, st1.b, Gt.b], writes=[tmpf.b])
                P.op("dve", lambda e: e.tensor_tensor(out=hb[:], in0=tmpf[:], in1=St[:], op=ALU.add), reads=[tmpf.b, St.b], writes=[hb.b])
                ps = PS[0]
                pb = ps[:, :].bitcast(BF16)
                for c in range(8):
                    P.op("pe", lambda e, c=c: e.transpose(out=pb[:, c * 128:(c + 1) * 128], in_=hb[:, c * 128:(c + 1) * 128], identity=identb[:]), reads=[hb.b, identb.b], writes=[ps.b])
                P.op("act", lambda e: e.activation(out=hT[:].rearrange("p c t -> p (c t)"), in_=pb, func=AF.Identity), reads=[ps.b], writes=[hT.b])

            def proj(ps, c0, c1, par):
                n = c1 - c0
                hT = hT2[par]
                for c in range(8):
                    P.op("pe", lambda e, c=c: e.matmul(ps[:, 0:n], lhsT=hT[:, c, :], rhs=Wb[:, c, c0:c1], start=(c == 0), stop=(c == 7)), reads=[hT.b, Wb.b], writes=[ps.b])

            def qk_post(ps, g_bc, rpt, dst_ap, dst_b):
                P.op("act", lambda e: e.activation(out=sq[:], in_=ps[:, :], func=AF.Square), reads=[ps.b], writes=[sq.b])
                P.op("dve", lambda e: e.tensor_tensor(out=kg[:].rearrange("p (h d) -> p h d", h=8), in0=ps[:, :].rearrange("p (h d) -> p h d", h=8),
                                                      in1=g_bc[:, :].unsqueeze(1).broadcast_to([128, 8, 64]), op=ALU.mult), reads=[ps.b, g_bc.b], writes=[kg.b])
                P.op("dve", lambda e: e.tensor_reduce(out=st8[:, 0, :], in_=sq[:].rearrange("p (h d) -> p h d", h=8), axis=AX.X, op=ALU.add), reads=[sq.b], writes=[st8.b])
                P.op("act", lambda e: e.activation(out=st8[:, 1, :], in_=st8[:, 0, :], func=AF.Sqrt, bias=eps_t[:, 0:1], scale=1.0 / 64), reads=[st8.b, eps_t.b], writes=[st8.b])
                P.op("dve", lambda e: e.reciprocal(out=st8[:, 2, :], in_=st8[:, 1, :]), reads=[st8.b], writes=[st8.b])
                rope4("dve", kg, kr, 8, rpt)
                P.op("dve", lambda e: e.tensor_tensor(out=dst_ap.rearrange("p (h d) -> p h d", h=8), in0=kr[:].rearrange("p (h d) -> p h d", h=8),
                                                      in1=st8[:, 2, :].unsqueeze(2).broadcast_to([128, 8, 64]), op=ALU.mult), reads=[kr.b, st8.b], writes=[dst_b])

            def transposes_to(src, ncol, dst_ap, dst_b, psk=1):
                ps = PS[psk]
                pb = ps[:, :].bitcast(BF16)
                for c in range(ncol):
                    P.op("pe", lambda e, c=c: e.transpose(out=pb[:, c * 128:(c + 1) * 128], in_=src[:, c * 128:(c + 1) * 128], identity=identb[:]), reads=[src.b, identb.b], writes=[ps.b])
                P.op("act", lambda e: e.activation(out=dst_ap, in_=pb[:, 0:ncol * 128].rearrange("p (c t) -> p c t", c=ncol), func=AF.Identity), reads=[ps.b], writes=[dst_b])

            def gelu(src_ps, dst):
                P.op("act", lambda e: e.activation(out=kg[:], in_=src_ps[:, :], func=AF.Identity), reads=[src_ps.b], writes=[kg.b])
                P.op("act", lambda e: e.activation(out=sq[:], in_=src_ps[:, :], func=AF.Square), reads=[src_ps.b], writes=[sq.b])
                P.op("dve", lambda e: e.tensor_scalar(out=sq[:], in0=sq[:], scalar1=0.044715, scalar2=1.0, op0=ALU.mult, op1=ALU.add), reads=[sq.b], writes=[sq.b])
                P.op("dve", lambda e: e.tensor_tensor(out=ta[:], in0=sq[:], in1=kg[:], op=ALU.mult), reads=[sq.b, kg.b], writes=[ta.b])
                P.op("act", lambda e: e.activation(out=ta[:], in_=ta[:], func=AF.Tanh, scale=0.7978845608028654), reads=[ta.b], writes=[ta.b])
                P.op("dve", lambda e: e.tensor_scalar(out=ta[:], in0=ta[:], scalar1=1.0, scalar2=0.5, op0=ALU.add, op1=ALU.mult), reads=[ta.b], writes=[ta.b])
                P.op("dve", lambda e: e.tensor_tensor(out=dst[:], in0=ta[:], in1=kg[:], op=ALU.mult), reads=[ta.b, kg.b], writes=[dst.b])

            if cfg.stop == 1:
                run_block()
                return nc
            def load_x(src, rsrc, r0, i):
                P.dma("sp", lambda e: e.dma_start(out=xt[i % 2][:], in_=src[r0:r0 + 128, :]), writes=[xt[i % 2].b])
                P.dma("sp", lambda e: e.dma_start(out=rp[i % 4][:], in_=rsrc[r0:r0 + 128, :]), writes=[rp[i % 4].b])

            def all_s3a(t):
                par = t % 2
                psK, psV, psKI = PS[2 + 3 * par], PS[3 + 3 * par], PS[4 + 3 * par]
                P.op("act", lambda e: e.activation(out=sq[:], in_=psK[:, :], func=AF.Square), reads=[psK.b], writes=[sq.b])
                P.op("dve", lambda e: e.tensor_tensor(out=kg[:].rearrange("p (h d) -> p h d", h=8), in0=psK[:, :].rearrange("p (h d) -> p h d", h=8),
                                                      in1=gk_bc[:, :].unsqueeze(1).broadcast_to([128, 8, 64]), op=ALU.mult), reads=[psK.b, gk_bc.b], writes=[kg.b])
                P.op("dve", lambda e: e.tensor_reduce(out=st8[:, 0, :], in_=sq[:].rearrange("p (h d) -> p h d", h=8), axis=AX.X, op=ALU.add), reads=[sq.b], writes=[st8.b])
                P.op("act", lambda e: e.activation(out=st8[:, 1, :], in_=st8[:, 0, :], func=AF.Sqrt, bias=eps_t[:, 0:1], scale=1.0 / 64), reads=[st8.b, eps_t.b], writes=[st8.b])
                P.op("dve", lambda e: e.reciprocal(out=st8[:, 2, :], in_=st8[:, 1, :]), reads=[st8.b], writes=[st8.b])
                v_ = vb[par]
                P.op("act", lambda e: e.activation(out=v_[:, :, 1:65], in_=psV[:, :].rearrange("p (h d) -> p h d", h=8), func=AF.Identity), reads=[psV.b], writes=[v_.b])
                P.dma("pool", lambda e: e.dma_start(out=v_scr[t * 128:(t + 1) * 128, :], in_=v_[:].rearrange("p h d -> p (h d)")), reads=[v_.b], writes=[b_vscr])
                if precast:
                    precast.pop(0)()
                P.op("act", lambda e: e.activation(out=kiraw[:, 0:64], in_=psKI[:, 0:64], func=AF.Identity), reads=[psKI.b], writes=[kiraw.b])

            def all_s1(t):
                par = t % 2
                norm_to_hT(xt[par], G1[0], S1[0], par)

            def all_s2(t):
                par = t % 2
                proj(PS[2 + 3 * par], 512, 1024, par)
                proj(PS[3 + 3 * par], 1024, 1536, par)
                proj(PS[4 + 3 * par], 1792, 1856, par)

            def all_s3b(t):
                par = t % 2
                rpt = rp[t % 4]
                rope4("dve", kg, kr, 8, rpt)
                P.op("dve", lambda e: e.tensor_tensor(out=knb[:].rearrange("p (h d) -> p h d", h=8), in0=kr[:].rearrange("p (h d) -> p h d", h=8),
                                                      in1=st8[:, 2, :].unsqueeze(2).broadcast_to([128, 8, 64]), op=ALU.mult), reads=[kr.b, st8.b], writes=[knb.b])
                transposes_to(knb, 4, KT[:, :, t * 128:(t + 1) * 128], KT.b)
                kf_ = kif[par]
                rope4("dve", kiraw, kf_, 1, rpt)
                P.op("dve", lambda e: e.tensor_copy(out=kib[:].rearrange("p (a d) -> p a d", a=2), in_=kf_[:, :].unsqueeze(1).broadcast_to([128, 2, 64])), reads=[kf_.b], writes=[kib.b])
                ps = PS[1]
                pb = ps[:, :].bitcast(BF16)
                P.op("pe", lambda e: e.transpose(out=pb[:, 512:640], in_=kib[:], identity=identb[:]), reads=[kib.b, identb.b], writes=[ps.b])
                P.op("act", lambda e: e.activation(out=KIT[:, t * 128:(t + 1) * 128], in_=pb[:, 512:640], func=AF.Identity), reads=[ps.b], writes=[KIT.b])

            load_x(x_seq, rope_seq, 0, 0)
            if NBLK > 1:
                load_x(x_seq, rope_seq, 128, 1)
            all_s1(0)
            all_s2(0)
            if NBLK > 1:
                all_s1(1)
            for t in range(NBLK):
                all_s3a(t)
                if t + 1 < NBLK:
                    all_s2(t + 1)
                if t + 2 < NBLK:
                    load_x(x_seq, rope_seq, (t + 2) * 128, t + 2)
                    all_s1(t + 2)
                all_s3b(t)

            if cfg.stop == 2:
                run_block()
                return nc
            def gelu_p1(src_ps, dst, tmp):
                P.op("act", lambda e: e.activation(out=dst[:], in_=src_ps[:, :], func=AF.Identity), reads=[src_ps.b], writes=[dst.b])
                P.op("act", lambda e: e.activation(out=tmp[:], in_=src_ps[:, :], func=AF.Square), reads=[src_ps.b], writes=[tmp.b])
                P.op("pool", lambda e: e.tensor_scalar(out=tmp[:], in0=tmp[:], scalar1=0.044715, scalar2=1.0, op0=ALU.mult, op1=ALU.add), reads=[tmp.b], writes=[tmp.b])
                P.op("pool", lambda e: e.tensor_tensor(out=tmp[:], in0=tmp[:], in1=dst[:], op=ALU.mult), reads=[tmp.b, dst.b], writes=[tmp.b])

            def gelu_p2(dst, tmp):
                P.op("act", lambda e: e.activation(out=tmp[:], in_=tmp[:], func=AF.Tanh, scale=0.7978845608028654), reads=[tmp.b], writes=[tmp.b])
                P.op("pool", lambda e: e.tensor_scalar(out=tmp[:], in0=tmp[:], scalar1=1.0, scalar2=0.5, op0=ALU.add, op1=ALU.mult), reads=[tmp.b], writes=[tmp.b])
                P.op("pool", lambda e: e.tensor_tensor(out=dst[:], in0=tmp[:], in1=dst[:], op=ALU.mult), reads=[tmp.b, dst.b], writes=[dst.b])

            bK, bV, bD, bQ, bU, bVG = PS[2], PS[3], PS[4], PS[5], PS[6], PS[7]
            load_x(x_own, rope_own, 0, 0)
            if NB1 > 1:
                load_x(x_own, rope_own, 128, 1)
            if NI == 0:
                P.dma("sp", lambda e: e.dma_start(out=G1[0][:], in_=bc_scr[1, 4, :, :]), reads=[b_bcscr], writes=[G1[0].b])
                P.dma("sp", lambda e: e.dma_start(out=S1[0][:], in_=bc_scr[1, 5, :, :]), reads=[b_bcscr], writes=[S1[0].b])
            norm_to_hT(xt[0], G1[0], S1[0], 0)
            proj(bU, 1860, 2372, 0)
            proj(bVG, 2372, 2884, 0)
            proj(bK, 512, 1024, 0)
            proj(bV, 1024, 1536, 0)
            proj(bD, 1536, 1860, 0)
            proj(bQ, 0, 512, 0)
            for i in range(NB1):
                s = 0 if i < NI else 1
                par = i % 2
                nxt = i + 1 < NB1
                pn = (i + 1) % 2
                if nxt:
                    if i + 1 == NI:
                        P.dma("sp", lambda e: e.dma_start(out=G1[0][:], in_=bc_scr[1, 4, :, :]), reads=[b_bcscr], writes=[G1[0].b])
                        P.dma("sp", lambda e: e.dma_start(out=S1[0][:], in_=bc_scr[1, 5, :, :]), reads=[b_bcscr], writes=[S1[0].b])
                    norm_to_hT(xt[pn], G1[0], S1[0], pn)
                xtile, rpt = xt[i % 2], rp[i % 4]
                r0 = i * 128
                gelu_p1(bU, ug, gx)
                gelu_p1(bVG, vn, gs)
                if nxt:
                    proj(bU, 1860, 2372, pn)
                    proj(bVG, 2372, 2884, pn)
                kn_ = kn[i % 2]
                qk_post(bK, gk_bc, rpt, kn_[:], kn_.b)
                if nxt:
                    proj(bK, 512, 1024, pn)
                P.dma("pool", lambda e, kn_=kn_, r0=r0: e.dma_start(out=k_own[r0:r0 + 128, :], in_=kn_[:]), reads=[kn_.b])
                if s == 1:
                    P.op("dve", lambda e, kn_=kn_: e.tensor_copy(out=ksn_b[:], in_=kn_[:]), reads=[kn_.b], writes=[ksn_b.b])
                vf_ = vf[i % 2]
                P.op("act", lambda e, vf_=vf_: e.activation(out=vf_[:], in_=bV[:, :], func=AF.Identity), reads=[bV.b], writes=[vf_.b])
                if nxt:
                    proj(bV, 1024, 1536, pn)
                P.dma("pool", lambda e, vf_=vf_, r0=r0: e.dma_start(out=v_own[r0:r0 + 128, :], in_=vf_[:]), reads=[vf_.b])
                if s == 1:
                    P.op("dve", lambda e, vf_=vf_: e.tensor_copy(out=vsn_b[:, :, 1:65], in_=vf_[:].rearrange("p (h d) -> p h d", h=8)), reads=[vf_.b], writes=[vsn_b.b])
                P.op("act", lambda e: e.activation(out=dt_[:], in_=bD[:, 0:324], func=AF.Identity), reads=[bD.b], writes=[dt_.b])
                if nxt:
                    proj(bD, 1536, 1860, pn)
                P.op("dve", lambda e, i=i: e.tensor_scalar(out=wis[:, i, :], in0=dt_[:, 320:324], scalar1=0.0625, scalar2=None, op0=ALU.mult), reads=[dt_.b], writes=[wis.b])
                kf_ = kif[i % 2]
                P.op("dve", lambda e: e.tensor_copy(out=kg[:, 0:64], in_=dt_[:, 256:320]), reads=[dt_.b], writes=[kg.b])
                rope4("dve", kg, kf_, 1, rpt)
                P.dma("pool", lambda e, kf_=kf_, r0=r0: e.dma_start(out=ki_own[r0:r0 + 128, :], in_=kf_[:]), reads=[kf_.b])
                if s == 1:
                    P.op("dve", lambda e, kf_=kf_: e.tensor_copy(out=kisn_b[:].rearrange("p (a d) -> p a d", a=2), in_=kf_[:, :].unsqueeze(1).broadcast_to([128, 2, 64])), reads=[kf_.b], writes=[kisn_b.b])
                P.op("dve", lambda e: e.tensor_copy(out=kg[:, 0:256], in_=dt_[:, 0:256]), reads=[dt_.b], writes=[kg.b])
                rope4("dve", kg, kr, 4, rpt)
                P.op("dve", lambda e: e.tensor_copy(out=qib[:], in_=kr[:, 0:256]), reads=[kr.b], writes=[qib.b])
                transposes_to(qib, 2, qits[:], qits.b)
                P.dma("pool", lambda e, i=i: e.dma_start(out=qit_scr[i, :, :], in_=qits[:].rearrange("p c t -> p (c t)")), reads=[qits.b], writes=[b_qit])
                gelu_p2(ug, gx)
                gelu_p2(vn, gs)
                qk_post(bQ, gq_bc, rpt, knb[:], knb.b)
                if nxt:
                    proj(bQ, 0, 512, pn)
                transposes_to(knb, 4, qts[:], qts.b)
                P.dma("pool", lambda e, i=i: e.dma_start(out=qt_scr[i, :, :], in_=qts[:].rearrange("p c t -> p (c t)")), reads=[qts.b], writes=[b_qt])
                P.op("dve", lambda e: e.bn_stats(out=bnst[:, 0:6], in_=vn[:]), reads=[vn.b], writes=[bnst.b])
                P.op("dve", lambda e: e.bn_aggr(out=bnst[:, 6:8], in_=bnst[:, 0:6]), reads=[bnst.b], writes=[bnst.b])
                P.op("act", lambda e: e.activation(out=bnst[:, 0:1], in_=bnst[:, 7:8], func=AF.Sqrt, bias=eps_t[:, 0:1], scale=1.0), reads=[bnst.b, eps_t.b], writes=[bnst.b])
                P.op("dve", lambda e: e.reciprocal(out=bnst[:, 1:2], in_=bnst[:, 0:1]), reads=[bnst.b], writes=[bnst.b])
                P.op("dve", lambda e: e.tensor_scalar(out=vn[:], in0=vn[:], scalar1=bnst[:, 6:7], scalar2=bnst[:, 1:2], op0=ALU.subtract, op1=ALU.mult), reads=[vn.b, bnst.b], writes=[vn.b])
                P.op("dve", lambda e: e.tensor_tensor(out=vn[:], in0=vn[:], in1=lng_bc[:], op=ALU.mult), reads=[vn.b, lng_bc.b], writes=[vn.b])
                P.op("dve", lambda e: e.tensor_tensor(out=vn[:], in0=vn[:], in1=lnb_bc[:], op=ALU.add), reads=[vn.b, lnb_bc.b], writes=[vn.b])
                if s == 1:
                    P.dma("pool", lambda e: e.dma_start(out=gv_own[:, :], in_=vn[:]), reads=[vn.b])
                P.op("dve", lambda e: e.tensor_copy(out=vnb[:], in_=vn[:]), reads=[vn.b], writes=[vnb.b])
                ps = PS[1]
                for g in range(4):
                    P.op("pe", lambda e, g=g, s=s, ps=ps: e.matmul(ps[:, g * 128:(g + 1) * 128], lhsT=WsT[s][:, g, :], rhs=vnb[:, g * 128:(g + 1) * 128], start=True, stop=True), reads=[WsT[s].b, vnb.b], writes=[ps.b])
                for g in range(4):
                    P.op("dve", lambda e, g=g, s=s, ps=ps: e.scalar_tensor_tensor(out=gmb[:, g * 128:(g + 1) * 128], in0=ps[:, g * 128:(g + 1) * 128], scalar=bs_t[s][:, g:g + 1], in1=ug[:, g * 128:(g + 1) * 128], op0=ALU.add, op1=ALU.mult),
                         reads=[ps.b, bs_t[s].b, ug.b], writes=[gmb.b])
                transposes_to(gmb, 4, mgT[:].rearrange("p (c t) -> p c t", c=4), mgT.b)
                P.dma("pool", lambda e, i=i: e.dma_start(out=mixg_scr[i, :, :], in_=mgT[:]), reads=[mgT.b], writes=[b_mixg])
                if i + 2 < NB1:
                    load_x(x_own, rope_own, (i + 2) * 128, i + 2)
                if precast:
                    precast.pop(0)()
            run_block()
        if cfg.stop == 3:
            mid.close()
            return nc

        with ExitStack() as st:
            Isc = sb(st, "Isc", [128, KMAX], F32)
            negm2 = [sb(st, "negm%d" % i, [128, KMAX], BF16) for i in range(2)]
            qbd = [sb(st, "qbd%d" % i, [128, 4, 256], BF16) for i in range(2)]
            sm2 = [sb(st, "sm%d" % i, [128, 8], F32) for i in range(2)]
            sa2 = [sb(st, "sa%d" % i, [128, 1], F32) for i in range(2)]
            cd2 = [sb(st, "cd%d" % i, [128, 1], F32) for i in range(2)]
            negm_b2 = [Buf("negm_b2_%d" % i) for i in range(2)]
            dl2 = [sb(st, "dl%d" % i, [128, NITER], F32) for i in range(2)]
            rr = [sb(st, "rr%d" % i, [128, 2, 512], F32) for i in range(2)]
            Vb = [sb(st, "Vb%d" % i, [128, 4, 520], BF16) for i in range(2)]
            Pb = [sb(st, "Pb%d" % i, [128, 512], BF16) for i in range(4)]
            qitb = [sb(st, "qitb%d" % i, [128, 2, 128], BF16) for i in range(2)]
            cm_p = sb(st, "cm_p", [128, 512], F32)
            cm_s = sb(st, "cm_s", [128, 128], F32)
            pw2 = sb(st, "pw2", [128, NITER], F32)
            lnd = sb(st, "lnd", [1, 1024], F32)
            rden = sb(st, "rden", [1, 1024], F32)
            bcs = sb(st, "bcs", [65, 1024], F32)
            mixa = sb(st, "mixa", [65, 8, 128], BF16)
            zl = sb(st, "zl", [128, 65], BF16)
            zr = sb(st, "zr", [128, 512], BF16)
            ckf = [sb(st, "ckf%d" % i, [128, 512], F32) for i in range(2)]
            cvf = [sb(st, "cvf%d" % i, [128, 512], F32) for i in range(2)]
            ckif = [sb(st, "ckif%d" % i, [128, 64], F32) for i in range(2)]
            ckb = sb(st, "ckb", [128, 512], BF16)
            cvb = [sb(st, "cvb%d" % i, [128, 8, 65], BF16) for i in range(2)]
            ckib = sb(st, "ckib", [128, 128], BF16)
            zv = sb(st, "zv", [128, 520], BF16)

            while precast:
                precast.pop(0)()
            P.dma("sp", lambda e: e.dma_start(out=cm_p[:], in_=cmask_p_d[:, :]), writes=[cm_p.b])
            P.dma("sp", lambda e: e.dma_start(out=cm_s[:], in_=cmask_s_d[:, :]), writes=[cm_s.b])
            P.dma("sp", lambda e: e.dma_start(out=pw2[:], in_=pow2_d[:, :]), writes=[pw2.b])
            P.op("pool", lambda e: e.memset(zl[:], 0.0), writes=[zl.b])
            P.op("pool", lambda e: e.memset(zr[:], 0.0), writes=[zr.b])
            P.op("pool", lambda e: e.memset(zv[:], 0.0), writes=[zv.b])
            for c_ in cvb:
                P.op("pool", lambda e, c_=c_: e.memset(c_[:], 1.0), writes=[c_.b])
            for q_ in qbd:
                P.op("pool", lambda e, q_=q_: e.memset(q_[:], 0.0), writes=[q_.b])
            cnts = {"l": 0, "p": 0, "sel": 0}
            LRING = [4, 5, 0, 1, 2, 3]
            LA = 5

            def idxsel(i, nkeys, topk, cm, cm_w):
                par = cnts["sel"] % 2
                cnts["sel"] += 1
                negm, qbd_, qit_, sm, dl = negm2[par], qbd[par], qitb[par], sm2[par], dl2[par]
                P.dma("sp", lambda e: e.dma_start(out=qbd_[0:64, :, 0:128], in_=qt_scr[i, 0:64, :].rearrange("p (c t) -> p c t", c=4)), reads=[b_qt], writes=[qbd_.b])
                P.dma("sp", lambda e: e.dma_start(out=qbd_[64:128, :, 128:256], in_=qt_scr[i, 64:128, :].rearrange("p (c t) -> p c t", c=4)), reads=[b_qt], writes=[qbd_.b])
                P.dma("sp", lambda e: e.dma_start(out=qit_[:].rearrange("p c t -> p (c t)"), in_=qit_scr[i, :, :]), reads=[b_qit], writes=[qit_.b])
                tiles = [(k0, min(512, nkeys - k0)) for k0 in range(0, nkeys, 512)]
                gi = 0
                for (k0, w) in tiles:
                    for grp in range(2):
                        gs = gi % 2
                        gi += 1
                        for hh in range(2):
                            ps = PS[2 * gs + hh]
                            P.op("pe", lambda e, ps=ps, hh=hh, grp=grp, k0=k0, w=w: e.matmul(ps[:, 0:w], lhsT=qit_[64 * hh:64 * hh + 64, grp, :], rhs=KIT[64 * hh:64 * hh + 64, k0:k0 + w], start=True, stop=True),
                                 reads=[qit_.b, KIT.b], writes=[ps.b])
                        r_ = rr[gs]
                        P.op("act", lambda e, gs=gs, w=w, r_=r_: e.activation(out=r_[:, :, 0:w], in_=psum[:, 2 * gs:2 * gs + 2, 0:w], func=AF.Relu),
                             reads=[PS[2 * gs].b, PS[2 * gs + 1].b], writes=[r_.b])
                        for hh in range(2):
                            h = 2 * grp + hh
                            if h == 0:
                                P.op("dve", lambda e, r_=r_, k0=k0, w=w: e.tensor_scalar(out=Isc[:, k0:k0 + w], in0=r_[:, 0, 0:w], scalar1=wis[:, i, 0:1], scalar2=None, op0=ALU.mult),
                                     reads=[r_.b, wis.b], writes=[Isc.b])
                            else:
                                P.op("dve", lambda e, r_=r_, k0=k0, w=w, hh=hh, h=h: e.scalar_tensor_tensor(out=Isc[:, k0:k0 + w], in0=r_[:, hh, 0:w], scalar=wis[:, i, h:h + 1], in1=Isc[:, k0:k0 + w], op0=ALU.mult, op1=ALU.add),
                                     reads=[r_.b, wis.b, Isc.b], writes=[Isc.b])
                P.op("dve", lambda e: e.tensor_reduce(out=sm[:, 0:1], in_=Isc[:, 0:nkeys], axis=AX.X, op=ALU.max, apply_absolute_value=True), reads=[Isc.b], writes=[sm.b])
                P.op("dve", lambda e: e.tensor_tensor(out=Isc[:, nkeys - cm_w:nkeys], in0=Isc[:, nkeys - cm_w:nkeys], in1=cm[:, 0:cm_w], op=ALU.add), reads=[Isc.b, cm.b], writes=[Isc.b])
                P.op("dve", lambda e: e.tensor_scalar(out=sm[:, 1:2], in0=sm[:, 0:1], scalar1=-1.001, scalar2=-1e-20, op0=ALU.mult, op1=ALU.add), reads=[sm.b], writes=[sm.b])
                P.op("dve", lambda e: e.tensor_scalar(out=sm[:, 2:3], in0=sm[:, 0:1], scalar1=2.003, scalar2=3e-20, op0=ALU.mult, op1=ALU.add), reads=[sm.b], writes=[sm.b])
                P.op("dve", lambda e: e.tensor_scalar(out=dl[:], in0=pw2[:], scalar1=sm[:, 2:3], scalar2=None, op0=ALU.mult), reads=[sm.b, pw2.b], writes=[dl.b])
                a_split = ((nkeys * 9 // 16) // 128) * 128
                if nkeys - a_split < 256:
                    a_split = nkeys
                stt = dict(par=par, nkeys=nkeys, topk=topk, a=a_split, m=0)
                return stt

            def bis_iter(stt):
                par, nkeys, topk, a, m = stt["par"], stt["nkeys"], stt["topk"], stt["a"], stt["m"]
                negm, sm, dl, sa, cd = negm2[par], sm2[par], dl2[par], sa2[par], cd2[par]
                P.op("dve", lambda e: e.tensor_tensor(out=cd[:, 0:1], in0=sm[:, 1:2], in1=dl[:, m:m + 1], op=ALU.add), reads=[sm.b, dl.b], writes=[cd.b])
                if a < nkeys:
                    P.op("act", lambda e: e.activation(out=negm[:, a:nkeys], in_=Isc[:, a:nkeys], func=AF.Sign, scale=-1.0, bias=cd[:, 0:1], accum_out=sa[:, 0:1]),
                         reads=[Isc.b, cd.b], writes=[negm_b2[par], sa.b])
                P.op("dve", lambda e: e.tensor_scalar(out=negm[:, 0:a], in0=Isc[:, 0:a], scalar1=cd[:, 0:1], scalar2=None, op0=ALU.is_ge, op1=ALU.add, accum_out=sm[:, 4:5]),
                     reads=[Isc.b, cd.b], writes=[negm.b, sm.b])
                if a < nkeys:
                    L = nkeys - a
                    P.op("dve", lambda e: e.scalar_tensor_tensor(out=sm[:, 6:7], in0=sa[:, 0:1], scalar=-0.5, in1=sm[:, 4:5], op0=ALU.mult, op1=ALU.add), reads=[sa.b, sm.b], writes=[sm.b])
                    P.op("dve", lambda e: e.tensor_scalar(out=sm[:, 5:6], in0=sm[:, 6:7], scalar1=float(topk) - 0.5 - 0.5 * L, scalar2=None, op0=ALU.is_ge), reads=[sm.b], writes=[sm.b])
                else:
                    P.op("dve", lambda e: e.tensor_scalar(out=sm[:, 5:6], in0=sm[:, 4:5], scalar1=float(topk) - 0.5, scalar2=None, op0=ALU.is_ge), reads=[sm.b], writes=[sm.b])
                P.op("dve", lambda e: e.scalar_tensor_tensor(out=sm[:, 1:2], in0=sm[:, 5:6], scalar=dl[:, m:m + 1], in1=sm[:, 1:2], op0=ALU.mult, op1=ALU.add), reads=[sm.b, dl.b], writes=[sm.b])
                stt["m"] += 1

            def idxsel_end(stt):
                while stt["m"] < NITER:
                    bis_iter(stt)
                par, nkeys = stt["par"], stt["nkeys"]
                negm, sm = negm2[par], sm2[par]
                P.op("dve", lambda e: e.tensor_scalar(out=negm[:, 0:nkeys], in0=Isc[:, 0:nkeys], scalar1=sm[:, 1:2], scalar2=MNEG, op0=ALU.is_lt, op1=ALU.mult), reads=[Isc.b, sm.b], writes=[negm.b, negm_b2[par]])
                return par


            def attend(i, nkeys, par, col_sel, pending=None):
                NK = nkeys // 128
                negm, qbd_ = negm2[par], qbd[par]
                negm_rb = [negm.b, negm_b2[par]]
                for hq in range(2):
                    ps = PS[6 + hq]
                    P.op("pe", lambda e, ps=ps: e.matmul(ps[0:65, :], lhsT=zl[:, 0:65], rhs=zr[:, :], start=True, stop=False, skip_group_check=True), reads=[zl.b, zr.b], writes=[ps.b])
                steps = []
                for kt4 in range((NK + 3) // 4):
                    nk_here = min(4, NK - 4 * kt4)
                    for kk in range(nk_here):
                        for hq in range(2):
                            steps.append((kt4, nk_here, kk, hq, kk == 0 and hq == 0))
                state = {}

                def emit_qk(n):
                    kt4, nk_here, kk, hq, first = steps[n]
                    vb_ = Vb[kt4 % 2]
                    if first:
                        P.dma("sp", lambda e: e.dma_start(out=vb_[:, 0:nk_here, :], in_=v_scr[kt4 * 512:kt4 * 512 + nk_here * 128, :].rearrange("(k p) c -> p k c", p=128)),
                              reads=[b_vscr], writes=[vb_.b])
                    t128 = kt4 * 4 + kk
                    psL = PS[LRING[cnts["l"] % len(LRING)]]
                    cnts["l"] += 1
                    P.op("pe", lambda e: e.matmul(psL[:, 0:512], lhsT=negm[:, t128 * 128:(t128 + 1) * 128], rhs=ident4[:, :], start=True, stop=False),
                         reads=negm_rb + [ident4.b], writes=[psL.b])
                    for pp in range(2):
                        pair = 2 * hq + pp
                        P.op("pe", lambda e, pp=pp, pair=pair: e.matmul(psL[:, pp * 256:(pp + 1) * 256], lhsT=KT[:, pair, t128 * 128:(t128 + 1) * 128], rhs=qbd_[:, pair, :], start=False, stop=(pp == 1)),
                             reads=[KT.b, qbd_.b], writes=[psL.b])
                    state[n] = psL

                def emit_pv(n):
                    kt4, nk_here, kk, hq, first = steps[n]
                    vb_ = Vb[kt4 % 2]
                    t128 = kt4 * 4 + kk
                    psL = state.pop(n)
                    pb_ = Pb[cnts["p"] % len(Pb)]
                    cnts["p"] += 1
                    P.op("act", lambda e: e.activation(out=pb_[:], in_=psL[:, :], func=AF.Exp, scale=0.125), reads=[psL.b], writes=[pb_.b])
                    pso = PS[6 + hq]
                    for h4 in range(4):
                        h = hq * 4 + h4
                        P.op("pe", lambda e, h4=h4, h=h: e.matmul(pso[0:65, h4 * 128:(h4 + 1) * 128], lhsT=vb_[:, kk, h * 65:(h + 1) * 65], rhs=pb_[:, h4 * 128:(h4 + 1) * 128], start=False, stop=(t128 == NK - 1), skip_group_check=True),
                             reads=[vb_.b, pb_.b], writes=[pso.b])

                stride = max(1, len(steps) // NITER)
                for n0 in range(min(LA, len(steps))):
                    emit_qk(n0)
                for n in range(len(steps)):
                    if n + LA < len(steps):
                        emit_qk(n + LA)
                    emit_pv(n)
                    if pending is not None and n % stride == stride - 1 and pending["m"] < NITER:
                        bis_iter(pending)
                if pending is not None:
                    idxsel_end(pending)
                rb = [PS[6].b, PS[7].b]
                P.op("act", lambda e: e.activation(out=lnd[0:1, :], in_=psum[0:1, 6:8, :].rearrange("p a b -> p (a b)"), func=AF.Ln), reads=rb, writes=[lnd.b])
                P.op("act", lambda e: e.activation(out=rden[0:1, :], in_=lnd[0:1, :], func=AF.Exp, scale=-1.0), reads=[lnd.b], writes=[rden.b])
                for half in range(2):
                    ps = PS[half]
                    P.op("pe", lambda e, ps=ps, half=half: e.matmul(ps[0:65, :], lhsT=ones_f[0:1, 0:65], rhs=rden[0:1, half * 512:(half + 1) * 512], start=True, stop=True), reads=[ones_f.b, rden.b], writes=[ps.b])
                P.op("act", lambda e: e.activation(out=bcs[0:65, :], in_=psum[0:65, 0:2, :].rearrange("p a b -> p (a b)"), func=AF.Identity), reads=[PS[0].b, PS[1].b], writes=[bcs.b])
                P.op("dve", lambda e: e.tensor_tensor(out=mixa[0:65, :, :].rearrange("p h t -> p (h t)"), in0=psum[0:65, 6:8, :].rearrange("p a b -> p (a b)"), in1=bcs[0:65, :], op=ALU.mult), reads=rb + [bcs.b], writes=[mixa.b])
                if col_sel is None:
                    P.dma("pool", lambda e: e.dma_start(out=mixa_scr[i, :, :], in_=mixa[0:65, :, :].rearrange("p h t -> p (h t)")), reads=[mixa.b], writes=[b_mixa])
                else:
                    c0 = col_sel
                    P.dma("pool", lambda e: e.dma_start(out=mixa_scr[i, :, :].rearrange("p (h t) -> p h t", h=8)[:, :, c0:c0 + 32], in_=mixa[0:65, :, c0:c0 + 32]), reads=[mixa.b], writes=[b_mixa])

            if NI > 0:
                st_cur = idxsel(0, 512, cfg.TOPK_P, cm_p, 512)
                idxsel_end(st_cur)
            for i in range(NI):
                st_next = None
                if i + 1 < NI:
                    st_next = idxsel(i + 1, 512 * (i + 2), cfg.TOPK_P, cm_p, 512)
                attend(i, 512 * (i + 1), st_cur["par"], None, pending=st_next)
                st_cur = st_next

            for bq in range(4):
                for kt in range(PAST // 128):
                    f_, v_, ki_ = ckf[kt % 2], cvf[kt % 2], ckif[kt % 2]
                    rows = slice(kt * 128, (kt + 1) * 128)
                    P.dma("sp", lambda e, f_=f_, rows=rows, bq=bq: e.dma_start(out=f_[:], in_=cache_k[bq, rows, :]), writes=[f_.b])
                    P.dma("sp", lambda e, v_=v_, rows=rows, bq=bq: e.dma_start(out=v_[:], in_=cache_v[bq, rows, :]), writes=[v_.b])
                    P.dma("sp", lambda e, ki_=ki_, rows=rows, bq=bq: e.dma_start(out=ki_[:], in_=cache_ki[bq, rows, :]), writes=[ki_.b])
                    P.op("dve", lambda e, f_=f_: e.tensor_copy(out=ckb[:], in_=f_[:]), reads=[f_.b], writes=[ckb.b])
                    ps = PS[0]
                    pb = ps[:, :].bitcast(BF16)
                    for c in range(4):
                        P.op("pe", lambda e, c=c, pb=pb: e.transpose(out=pb[:, c * 128:(c + 1) * 128], in_=ckb[:, c * 128:(c + 1) * 128], identity=identb[:]), reads=[ckb.b, identb.b], writes=[ps.b])
                    P.op("act", lambda e, pb=pb, kt=kt: e.activation(out=KT[:, :, kt * 128:(kt + 1) * 128], in_=pb[:, 0:512].rearrange("p (c t) -> p c t", c=4), func=AF.Identity), reads=[ps.b], writes=[KT.b])
                    cv_ = cvb[kt % 2]
                    P.op("dve", lambda e, cv_=cv_, v_=v_: e.tensor_copy(out=cv_[:, :, 1:65], in_=v_[:].rearrange("p (h d) -> p h d", h=8)), reads=[v_.b], writes=[cv_.b])
                    P.dma("pool", lambda e, cv_=cv_, rows=rows: e.dma_start(out=v_scr[rows, :], in_=cv_[:].rearrange("p h d -> p (h d)")), reads=[cv_.b], writes=[b_vscr])
                    P.op("dve", lambda e, ki_=ki_: e.tensor_copy(out=ckib[:].rearrange("p (a d) -> p a d", a=2), in_=ki_[:, :].unsqueeze(1).broadcast_to([128, 2, 64])), reads=[ki_.b], writes=[ckib.b])
                    ps = PS[1]
                    pb = ps[:, :].bitcast(BF16)
                    P.op("pe", lambda e, pb=pb: e.transpose(out=pb[:, 0:128], in_=ckib[:], identity=identb[:]), reads=[ckib.b, identb.b], writes=[ps.b])
                    P.op("act", lambda e, pb=pb, kt=kt: e.activation(out=KIT[:, kt * 128:(kt + 1) * 128], in_=pb[:, 0:128], func=AF.Identity), reads=[ps.b], writes=[KIT.b])
                P.op("pool", lambda e: e.memset(KT[:, :, PAST:PAST + 128], 0.0), writes=[KT.b])
                P.op("pool", lambda e: e.memset(KIT[:, PAST:PAST + 128], 0.0), writes=[KIT.b])
                ps = PS[0]
                pb = ps[:, :].bitcast(BF16)
                for c in range(4):
                    P.op("pe", lambda e, c=c, pb=pb: e.transpose(out=pb[:, c * 128:(c + 1) * 128], in_=ksn_b[:, c * 128:(c + 1) * 128], identity=identb[:]), reads=[ksn_b.b, identb.b], writes=[ps.b])
                P.op("act", lambda e, pb=pb, bq=bq: e.activation(out=KT[:, :, PAST:PAST + 32], in_=pb[:, 0:512].rearrange("p (c t) -> p c t", c=4)[:, :, 32 * bq:32 * bq + 32], func=AF.Identity), reads=[ps.b], writes=[KT.b])
                ps = PS[1]
                pb = ps[:, :].bitcast(BF16)
                P.op("pe", lambda e, pb=pb: e.transpose(out=pb[:, 0:128], in_=kisn_b[:], identity=identb[:]), reads=[kisn_b.b, identb.b], writes=[ps.b])
                P.op("act", lambda e, pb=pb, bq=bq: e.activation(out=KIT[:, PAST:PAST + 32], in_=pb[:, 32 * bq:32 * bq + 32], func=AF.Identity), reads=[ps.b], writes=[KIT.b])
                P.dma("pool", lambda e, bq=bq: e.dma_start(out=v_scr[PAST:PAST + 32, :], in_=vsn_b[32 * bq:32 * bq + 32, :, :].rearrange("p h d -> p (h d)")), reads=[vsn_b.b], writes=[b_vscr])
                P.dma("pool", lambda e: e.dma_start(out=v_scr[PAST + 32:PAST + 128, :], in_=zv[0:96, :]), reads=[zv.b], writes=[b_vscr])
                st_s = idxsel(NI, SPAD, cfg.TOPK_S, cm_s, 128)
                idxsel_end(st_s)
                attend(NI, SPAD, st_s["par"], 32 * bq)
            run_block()
        mid.close()
        if cfg.stop == 6:
            return nc

        with ExitStack() as st:
            woa = sb(st, "woa", [65, 8, 1024], BF16)
            wog = sb(st, "wog", [128, 4, 1024], BF16)
            wf1 = sb(st, "wf1", [128, 8, 4096], BF16)
            wf2 = sb(st, "wf2", [128, 32, 1024], BF16)
            bcD = [sb(st, "bcD%d" % k, [128, D], F32) for k in range(4)]
            xd = sb(st, "xd", [128, D], F32)
            x1 = sb(st, "x1", [128, D], F32)
            tmpf = sb(st, "tmpfD", [128, D], F32)
            hb = sb(st, "hbD", [128, D], BF16)
            hT = sb(st, "hTD", [128, 8, 128], BF16)
            aT = sb(st, "aT", [128, 32, 128], BF16)
            rl2 = [sb(st, "rl%d" % i, [128, 512], F32) for i in range(2)]
            ab = [sb(st, "ab%d" % i, [128, 512], BF16) for i in range(2)]
            mxa = sb(st, "mxa", [65, 8, 128], BF16)
            mxg = sb(st, "mxg", [128, 4, 128], BF16)
            st1 = sb(st, "st1D", [128, 4], F32)

            P.op("pool", lambda e: e.memset(woa[0:1, :, :], 0.0), writes=[woa.b])
            for h in range(8):
                P.dma("sp", lambda e, h=h: e.dma_start(out=woa[1:65, h, :], in_=wo_b[h * 64:(h + 1) * 64, :]), reads=[b_wob], writes=[woa.b])
            P.dma("sp", lambda e: e.dma_start(out=wog[:], in_=wo_b[512:1024, :].rearrange("(g p) n -> p g n", p=128)), reads=[b_wob], writes=[wog.b])
            for c in range(8):
                P.dma("sp", lambda e, c=c: e.dma_start(out=wf1[:, c, :], in_=wf1_b[c * 128:(c + 1) * 128, :]), reads=[b_wf1b], writes=[wf1.b])
            for f4 in range(8):
                P.dma("sp", lambda e, f4=f4: e.dma_start(out=wf2[:, f4 * 4:(f4 + 1) * 4, :], in_=wf2_b[f4 * 512:(f4 + 1) * 512, :].rearrange("(f p) n -> p f n", p=128)), reads=[b_wf2b], writes=[wf2.b])

            def loadD(i):
                r0 = i * 128
                P.dma("sp", lambda e: e.dma_start(out=xd[:], in_=x_own[r0:r0 + 128, :]), writes=[xd.b])
                P.dma("sp", lambda e: e.dma_start(out=mxa[0:65, :, :].rearrange("p h t -> p (h t)"), in_=mixa_scr[i, :, :]), reads=[b_mixa], writes=[mxa.b])
                P.dma("sp", lambda e: e.dma_start(out=mxg[:].rearrange("p c t -> p (c t)"), in_=mixg_scr[i, :, :]), reads=[b_mixg], writes=[mxg.b])

            for i in range(NB1):
                s = 0 if i < NI else 1
                if i == 0 or i == NI:
                    for k in range(4):
                        P.dma("sp", lambda e, k=k, s=s: e.dma_start(out=bcD[k][:], in_=bc_scr[s, k, :, :]), reads=[b_bcscr], writes=[bcD[k].b])
                r0 = i * 128
                if i == 0:
                    loadD(0)
                for nt in range(2):
                    ps = PS[nt]
                    for h in range(8):
                        P.op("pe", lambda e, ps=ps, h=h, nt=nt: e.matmul(ps[:, :], lhsT=mxa[0:65, h, :], rhs=woa[0:65, h, nt * 512:(nt + 1) * 512], start=(h == 0), stop=False), reads=[mxa.b, woa.b], writes=[ps.b])
                    for g in range(4):
                        P.op("pe", lambda e, ps=ps, g=g, nt=nt: e.matmul(ps[:, :], lhsT=mxg[:, g, :], rhs=wog[:, g, nt * 512:(nt + 1) * 512], start=False, stop=(g == 3)), reads=[mxg.b, wog.b], writes=[ps.b])
                    P.op("dve", lambda e, ps=ps, nt=nt: e.tensor_tensor(out=tmpf[:, nt * 512:(nt + 1) * 512], in0=ps[:, :], in1=bcD[0][:, nt * 512:(nt + 1) * 512], op=ALU.mult), reads=[ps.b, bcD[0].b], writes=[tmpf.b])
                P.op("dve", lambda e: e.tensor_tensor(out=x1[:], in0=tmpf[:], in1=xd[:], op=ALU.add), reads=[tmpf.b, xd.b], writes=[x1.b])
                if i + 1 < NB1:
                    loadD(i + 1)
                P.op("act", lambda e: e.activation(out=hb[:], in_=x1[:], func=AF.Square, accum_out=st1[:, 0:1]), reads=[x1.b], writes=[hb.b, st1.b])
                P.op("act", lambda e: e.activation(out=st1[:, 1:2], in_=st1[:, 0:1], func=AF.Sqrt, bias=eps_t[:, 0:1], scale=1.0 / D), reads=[st1.b, eps_t.b], writes=[st1.b])
                P.op("dve", lambda e: e.reciprocal(out=st1[:, 2:3], in_=st1[:, 1:2]), reads=[st1.b], writes=[st1.b])
                P.op("dve", lambda e: e.scalar_tensor_tensor(out=tmpf[:], in0=x1[:], scalar=st1[:, 2:3], in1=bcD[1][:], op0=ALU.mult, op1=ALU.mult), reads=[x1.b, st1.b, bcD[1].b], writes=[tmpf.b])
                P.op("dve", lambda e: e.tensor_tensor(out=hb[:], in0=tmpf[:], in1=bcD[2][:], op=ALU.add), reads=[tmpf.b, bcD[2].b], writes=[hb.b])
                ps = PS[2]
                pb = ps[:, :].bitcast(BF16)
                for c in range(8):
                    P.op("pe", lambda e, c=c, pb=pb: e.transpose(out=pb[:, c * 128:(c + 1) * 128], in_=hb[:, c * 128:(c + 1) * 128], identity=identb[:]), reads=[hb.b, identb.b], writes=[ps.b])
                P.op("act", lambda e, pb=pb: e.activation(out=hT[:].rearrange("p c t -> p (c t)"), in_=pb, func=AF.Identity), reads=[ps.b], writes=[hT.b])
                def ff1_mm(nt8):
                    ps = PS[3 + nt8 % 2]
                    for c in range(8):
                        P.op("pe", lambda e, ps=ps, c=c: e.matmul(ps[:, :], lhsT=hT[:, c, :], rhs=wf1[:, c, nt8 * 512:(nt8 + 1) * 512], start=(c == 0), stop=(c == 7)), reads=[hT.b, wf1.b], writes=[ps.b])

                def ff1_post(nt8):
                    ps = PS[3 + nt8 % 2]
                    rl_ = rl2[nt8 % 2]
                    P.op("act", lambda e: e.activation(out=rl_[:], in_=ps[:, :], func=AF.Relu), reads=[ps.b], writes=[rl_.b])
                    ab_ = ab[nt8 % 2]
                    P.op("dve", lambda e: e.tensor_tensor(out=ab_[:], in0=rl_[:], in1=rl_[:], op=ALU.mult), reads=[rl_.b], writes=[ab_.b])
                    pt = PS[7]
                    ptb = pt[:, :].bitcast(BF16)
                    for fc in range(4):
                        P.op("pe", lambda e, fc=fc: e.transpose(out=ptb[:, fc * 128:(fc + 1) * 128], in_=ab_[:, fc * 128:(fc + 1) * 128], identity=identb[:]), reads=[ab_.b, identb.b], writes=[pt.b])
                    P.op("act", lambda e: e.activation(out=aT[:, nt8 * 4:(nt8 + 1) * 4, :].rearrange("p c t -> p (c t)"), in_=ptb[:, 0:512], func=AF.Identity), reads=[pt.b], writes=[aT.b])

                ff1_mm(0)
                for nt8 in range(8):
                    if nt8 + 1 < 8:
                        ff1_mm(nt8 + 1)
                    ff1_post(nt8)
                for nt in range(2):
                    ps = PS[5 + nt]
                    for f in range(32):
                        P.op("pe", lambda e, ps=ps, f=f, nt=nt: e.matmul(ps[:, :], lhsT=aT[:, f, :], rhs=wf2[:, f, nt * 512:(nt + 1) * 512], start=(f == 0), stop=(f == 31)), reads=[aT.b, wf2.b], writes=[ps.b])
                    P.op("dve", lambda e, ps=ps, nt=nt: e.tensor_tensor(out=tmpf[:, nt * 512:(nt + 1) * 512], in0=ps[:, :], in1=bcD[3][:, nt * 512:(nt + 1) * 512], op=ALU.mult), reads=[ps.b, bcD[3].b], writes=[tmpf.b])
                P.op("dve", lambda e: e.tensor_tensor(out=tmpf[:], in0=tmpf[:], in1=x1[:], op=ALU.add), reads=[tmpf.b, x1.b], writes=[tmpf.b])
                P.dma("pool", lambda e, r0=r0: e.dma_start(out=y_own[r0:r0 + 128, :], in_=tmpf[:]), reads=[tmpf.b])
            run_block()

        return nc


def _rope_table(pos):
    half = 32
    inv = np.power(np.float32(10000.0), -np.arange(half, dtype=np.float32) / np.float32(half)).astype(np.float32)
    ang = pos.astype(np.float32)[:, None] * inv[None, :]
    return np.concatenate([np.cos(ang), np.sin(ang)], axis=1).astype(np.float32)


def make_in_maps(cfg, inp):
    SEQ, PAST, NI, NB1 = cfg.SEQ, cfg.PAST, cfg.NI, cfg.NB1
    f = lambda a: np.ascontiguousarray(np.asarray(a, dtype=np.float32))
    xp, xs = f(inp["x_prompt"]), f(inp["x_sample"])
    cp, cs = f(inp["c_prompt"]), f(inp["c_sample"])
    ck, cv, cki = f(inp["cache_k"])[0], f(inp["cache_v"])[0], f(inp["cache_kidx"])[0]
    shared = {
        "w_ada": f(inp["w_ada"])[0], "b_ada": f(inp["b_ada"]), "norm1_g": f(inp["norm1_g"]), "norm2_g": f(inp["norm2_g"]),
        "w_in": f(inp["w_in"])[0], "q_norm_g": f(inp["q_norm_g"]), "k_norm_g": f(inp["k_norm_g"]),
        "gmlp_ln_g": f(inp["gmlp_ln_g"]), "gmlp_ln_b": f(inp["gmlp_ln_b"]), "gmlp_ws": f(inp["gmlp_ws"])[0],
        "gmlp_bs": f(inp["gmlp_bs"])[0], "w_out": f(inp["w_out"])[0], "w_ff1": f(inp["w_ff1"])[0], "w_ff2": f(inp["w_ff2"])[0],
        "ident": np.eye(128, dtype=np.float32),
        "pow2": np.tile((2.0 ** -(np.arange(cfg.NITER) + 1.0)).astype(np.float32)[None, :], (128, 1)),
    }
    sel = np.zeros((2, 5, 128), np.float32)
    sel[0, 0, :] = 1.0
    for p in range(128):
        sel[1, 1 + p // 32, p] = 1.0
    shared["sel"] = sel
    tril = np.zeros((2, 128, 128), np.float32)
    tril[0] = np.tril(np.ones((128, 128), np.float32))
    for q in range(4):
        tril[1, q * 32:(q + 1) * 32, q * 32:(q + 1) * 32] = np.tril(np.ones((32, 32), np.float32))
    shared["tril"] = tril
    cm_s = np.full((128, 128), NEG, np.float32)
    cm_s[:, 0:32] = 0.0
    shared["cmask_s"] = cm_s
    rope_seq = _rope_table(np.arange(SEQ))
    maps = []
    for c in range(8):
        b, j = c // 4, c % 4
        blks = [4 * i + j for i in range(NI)]
        rows = np.concatenate([np.arange(bl * 128, (bl + 1) * 128) for bl in blks])
        x_own = np.concatenate([xp[b][rows], xs[4 * c:4 * c + 4].reshape(128, D)], axis=0)
        pos_own = np.concatenate([rows, PAST + (np.arange(128) % 32)])
        tl = 128 * j + np.arange(128)
        lim = (tl // 64 + 1) * 64
        cm = np.where(np.arange(512)[None, :] < lim[:, None], 0.0, NEG).astype(np.float32)
        m = dict(shared)
        m.update({
            "x_seq": xp[b], "x_own": np.ascontiguousarray(x_own), "c_all": np.ascontiguousarray(np.concatenate([cp[b:b + 1], cs[4 * c:4 * c + 4]], axis=0)),
            "rope_seq": rope_seq, "rope_own": _rope_table(pos_own),
            "cache_k": np.ascontiguousarray(ck[4 * c:4 * c + 4].reshape(4, PAST, 512)),
            "cache_v": np.ascontiguousarray(cv[4 * c:4 * c + 4].reshape(4, PAST, 512)),
            "cache_ki": np.ascontiguousarray(cki[4 * c:4 * c + 4]),
            "cmask_p": cm,
        })
        maps.append(m)
    return maps


def assemble(cfg, results):
    SEQ, NI = cfg.SEQ, cfg.NI
    yp = np.zeros((2, SEQ, D), np.float32)
    ys = np.zeros((32, 32, D), np.float32)
    kp = np.zeros((1, 2, SEQ, 8, 64), np.float32)
    vp = np.zeros((1, 2, SEQ, 8, 64), np.float32)
    kip = np.zeros((1, 2, SEQ, 64), np.float32)
    ks = np.zeros((1, 32, 32, 8, 64), np.float32)
    vs = np.zeros((1, 32, 32, 8, 64), np.float32)
    kis = np.zeros((1, 32, 32, 64), np.float32)
    gvs = np.zeros((1, 32, 32, 512), np.float32)
    for c in range(8):
        r = results[c]
        b, j = c // 4, c % 4
        for i in range(NI):
            bl = 4 * i + j
            sl = slice(bl * 128, (bl + 1) * 128)
            o = slice(i * 128, (i + 1) * 128)
            yp[b, sl] = r["y_own"][o]
            kp[0, b, sl] = r["k_own"][o].reshape(128, 8, 64)
            vp[0, b, sl] = r["v_own"][o].reshape(128, 8, 64)
            kip[0, b, sl] = r["ki_own"][o]
        o = slice(NI * 128, (NI + 1) * 128)
        ys[4 * c:4 * c + 4] = r["y_own"][o].reshape(4, 32, D)
        ks[0, 4 * c:4 * c + 4] = r["k_own"][o].reshape(4, 32, 8, 64)
        vs[0, 4 * c:4 * c + 4] = r["v_own"][o].reshape(4, 32, 8, 64)
        kis[0, 4 * c:4 * c + 4] = r["ki_own"][o].reshape(4, 32, 64)
        gvs[0, 4 * c:4 * c + 4] = r["gv_own"].reshape(4, 32, 512)
    return (yp, ys, kp, vp, kip, ks, vs, kis, gvs)


_CACHE = {}


def kernel(**inputs):
    cfg = Cfg(SEQ=int(np.asarray(inputs["x_prompt"]).shape[1]), PAST=int(np.asarray(inputs["cache_k"]).shape[2]))
    import os
    cfg.stop = int(os.environ.get("KSTOP", "99"))
    cfg.sub = int(os.environ.get("KSUB", "99"))
    key = (cfg.SEQ, cfg.PAST)
    if key not in _CACHE:
        _CACHE[key] = build(cfg)
    nc = _CACHE[key]
    maps = make_in_maps(cfg, inputs)
    res = run_bass_kernel_spmd(nc, maps, core_ids=list(range(8)))
    return assemble(cfg, res.results)
```
